# Optimizing a Trainium2 kernel written in Bass

```python
import jax, jax.numpy as jnp
from jax import lax
import numpy as np

D_MODEL = 2048
BATCH = 4
SEQ = 4096
DEPTH = 1

CHUNK = 64
Q_BLOCK = 128
RET_HEADS = D_MODEL // 256
RET_DK = 128
RET_DV = 128
RET_QK = RET_HEADS * RET_DK
RET_WIDTH = RET_HEADS * RET_DV
MLA_HEADS = D_MODEL // 256
MLA_NOPE = 128
MLA_ROPE = 64
MLA_DV = 128
MLA_Q_RANK = 768
MLA_KV_RANK = 512
MLA_WIDTH = MLA_HEADS * MLA_DV
MIX_WIDTH = RET_WIDTH + MLA_WIDTH
IN_SIZES = (RET_QK, RET_QK, RET_WIDTH, RET_WIDTH, MLA_Q_RANK, MLA_KV_RANK, MLA_ROPE)
IN_WIDTH = RET_QK * 2 + RET_WIDTH * 2 + MLA_Q_RANK + MLA_KV_RANK + MLA_ROPE
D_FF = 4 * D_MODEL
ROPE_BASE = 10000.0
EPS = 1e-5
DEEPNORM_ALPHA = (2.0 * DEPTH) ** 0.25
DEEPNORM_BETA = (8.0 * DEPTH) ** -0.25

kernel_name = 'hymba_retention_mla_deepnorm_block'


def _layer_norm(x, g, b):
    xf = x.astype(jnp.float32)
    mu = jnp.mean(xf, axis=-1, keepdims=True)
    var = jnp.mean(jnp.square(xf - mu), axis=-1, keepdims=True)
    return ((xf - mu) * lax.rsqrt(var + EPS) * g + b).astype(x.dtype)


def _rms_norm(x, g):
    xf = x.astype(jnp.float32)
    return (xf * lax.rsqrt(jnp.mean(jnp.square(xf), axis=-1, keepdims=True) + EPS) * g).astype(x.dtype)


def _rope(x, positions):
    d = x.shape[-1]
    inv = ROPE_BASE ** (-jnp.arange(0, d, 2, dtype=jnp.float32) / d)
    ang = positions.astype(jnp.float32)[..., None] * inv
    ang = ang.reshape(ang.shape[:2] + (1,) * (x.ndim - 3) + ang.shape[-1:])
    cos, sin = jnp.cos(ang), jnp.sin(ang)
    xf = x.astype(jnp.float32)
    x1, x2 = xf[..., : d // 2], xf[..., d // 2:]
    return jnp.concatenate([x1 * cos - x2 * sin, x1 * sin + x2 * cos], axis=-1).astype(x.dtype)


def _retention(q, k, v):
    B, S, H, dk = q.shape
    dv = v.shape[-1]
    nc = S // CHUNK
    dt = q.dtype
    q = q.reshape(B, nc, CHUNK, H, dk).transpose(0, 3, 1, 2, 4)
    k = k.reshape(B, nc, CHUNK, H, dk).transpose(0, 3, 1, 2, 4) * (dk ** -0.5)
    v = v.reshape(B, nc, CHUNK, H, dv).transpose(0, 3, 1, 2, 4)
    log_g = jnp.log1p(-jnp.exp2(-5.0 - jnp.arange(H, dtype=jnp.float32)))
    idx = jnp.arange(CHUNK, dtype=jnp.float32)
    intra_decay = jnp.exp(log_g[:, None, None] * jnp.abs(idx[:, None] - idx[None, :])).astype(dt)
    k_decay = jnp.exp(log_g[:, None] * (CHUNK - 1 - idx)).astype(dt)
    q_decay = jnp.exp(log_g[:, None] * (idx + 1.0)).astype(dt)
    chunk_decay = jnp.exp(log_g * CHUNK).astype(dt)[:, None, None]
    scores = jnp.einsum('bhnid,bhnjd->bhnij', q, k) * intra_decay[:, None]
    intra = jnp.einsum('bhnij,bhnje->bhnie', scores, v)
    kv = jnp.einsum('bhnjd,bhnje->nbhde', k * k_decay[:, None, :, None], v)

    def step(state, kv_c):
        return chunk_decay * state + kv_c, state

    _, s_prev = lax.scan(step, jnp.zeros((B, H, dk, dv), kv.dtype), kv)
    cross = jnp.einsum('bhnid,nbhde->bhnie', q * q_decay[:, None, :, None], s_prev)
    return (intra + cross).transpose(0, 2, 3, 1, 4).reshape(B, S, H, dv)


def _mla(c_q, c_kv, k_r, positions, q_norm_g, w_uq, kv_norm_g, w_uk, w_uv):
    B, S, _ = c_q.shape
    H = MLA_HEADS
    q = (_rms_norm(c_q, q_norm_g) @ w_uq).reshape(B, S, H, MLA_NOPE + MLA_ROPE)
    q_nope = q[..., :MLA_NOPE]
    q_rope = _rope(q[..., MLA_NOPE:], positions)
    ckv = _rms_norm(c_kv, kv_norm_g)
    k_nope = (ckv @ w_uk).reshape(B, S, H, MLA_NOPE)
    v = (ckv @ w_uv).reshape(B, S, H, MLA_DV)
    k_rope = _rope(k_r, positions)
    scale = (MLA_NOPE + MLA_ROPE) ** -0.5
    nqb = S // Q_BLOCK
    qn = q_nope.reshape(B, nqb, Q_BLOCK, H, MLA_NOPE).transpose(1, 0, 3, 2, 4)
    qr = q_rope.reshape(B, nqb, Q_BLOCK, H, MLA_ROPE).transpose(1, 0, 3, 2, 4)
    k_chunk = jnp.arange(S) // CHUNK

    def block(args):
        qn_b, qr_b, start = args
        s = (jnp.einsum('bhqd,bkhd->bhqk', qn_b, k_nope)
             + jnp.einsum('bhqd,bkd->bhqk', qr_b, k_rope)).astype(jnp.float32) * scale
        q_chunk = (start + jnp.arange(Q_BLOCK)) // CHUNK
        mask = k_chunk[None, :] <= q_chunk[:, None]
        p = jax.nn.softmax(jnp.where(mask, s, -jnp.inf), axis=-1).astype(v.dtype)
        return jnp.einsum('bhqk,bkhd->bqhd', p, v)

    out = lax.map(block, (qn, qr, jnp.arange(nqb, dtype=jnp.int32) * Q_BLOCK))
    return out.transpose(1, 0, 2, 3, 4).reshape(B, S, H * MLA_DV)


def setup_inputs(seed: int = 0) -> dict:
    key = jax.random.key(seed)
    ks = jax.random.split(key, 16)
    f32 = jnp.float32
    x = jax.random.normal(ks[0], (BATCH, SEQ, D_MODEL), f32)
    offset = jax.random.randint(ks[1], (BATCH, 1), 0, 1024, dtype=jnp.int32)
    positions = offset + jnp.arange(SEQ, dtype=jnp.int32)[None, :]
    col_scale = jnp.concatenate([
        jnp.ones((2 * RET_QK,), f32),
        jnp.full((RET_WIDTH,), DEEPNORM_BETA, f32),
        jnp.ones((IN_WIDTH - 2 * RET_QK - RET_WIDTH,), f32)])
    w_in = jax.random.normal(ks[2], (DEPTH, D_MODEL, IN_WIDTH), f32) * (D_MODEL ** -0.5) * col_scale
    q_norm_g = 1.0 + 0.02 * jax.random.normal(ks[3], (DEPTH, MLA_Q_RANK), f32)
    w_uq = jax.random.normal(ks[4], (DEPTH, MLA_Q_RANK, MLA_HEADS * (MLA_NOPE + MLA_ROPE)), f32) * (MLA_Q_RANK ** -0.5)
    kv_norm_g = 1.0 + 0.02 * jax.random.normal(ks[5], (DEPTH, MLA_KV_RANK), f32)
    w_uk = jax.random.normal(ks[6], (DEPTH, MLA_KV_RANK, MLA_HEADS * MLA_NOPE), f32) * (MLA_KV_RANK ** -0.5)
    w_uv = jax.random.normal(ks[7], (DEPTH, MLA_KV_RANK, MLA_HEADS * MLA_DV), f32) * (MLA_KV_RANK ** -0.5) * DEEPNORM_BETA
    ret_gn_g = 1.0 + 0.02 * jax.random.normal(ks[8], (DEPTH, RET_WIDTH), f32)
    w_out = jax.random.normal(ks[9], (DEPTH, MIX_WIDTH, D_MODEL), f32) * (MIX_WIDTH ** -0.5) * DEEPNORM_BETA
    ln1_g = 1.0 + 0.02 * jax.random.normal(ks[10], (DEPTH, D_MODEL), f32)
    ln1_b = 0.02 * jax.random.normal(ks[11], (DEPTH, D_MODEL), f32)
    w_up = jax.random.normal(ks[12], (DEPTH, D_MODEL, D_FF), f32) * (D_MODEL ** -0.5) * DEEPNORM_BETA
    w_down = jax.random.normal(ks[13], (DEPTH, D_FF, D_MODEL), f32) * (D_FF ** -0.5) * DEEPNORM_BETA
    ln2_g = 1.0 + 0.02 * jax.random.normal(ks[14], (DEPTH, D_MODEL), f32)
    ln2_b = 0.02 * jax.random.normal(ks[15], (DEPTH, D_MODEL), f32)
    return {'x': x, 'positions': positions, 'w_in': w_in, 'q_norm_g': q_norm_g, 'w_uq': w_uq,
            'kv_norm_g': kv_norm_g, 'w_uk': w_uk, 'w_uv': w_uv, 'ret_gn_g': ret_gn_g,
            'w_out': w_out, 'ln1_g': ln1_g, 'ln1_b': ln1_b, 'w_up': w_up, 'w_down': w_down,
            'ln2_g': ln2_g, 'ln2_b': ln2_b}


def reference(x, positions, w_in, q_norm_g, w_uq, kv_norm_g, w_uk, w_uv, ret_gn_g,
              w_out, ln1_g, ln1_b, w_up, w_down, ln2_g, ln2_b):
    B, S, _ = x.shape
    split_idx = [int(i) for i in np.cumsum(IN_SIZES)[:-1]]
    for l in range(DEPTH):
        proj = x @ w_in[l]
        rq, rk, rv, rg, cq, ckv, kr = jnp.split(proj, split_idx, axis=-1)
        rq = _rope(rq.reshape(B, S, RET_HEADS, RET_DK), positions)
        rk = _rope(rk.reshape(B, S, RET_HEADS, RET_DK), positions)
        ret = _retention(rq, rk, rv.reshape(B, S, RET_HEADS, RET_DV)).astype(jnp.float32)
        mu = jnp.mean(ret, axis=-1, keepdims=True)
        var = jnp.mean(jnp.square(ret - mu), axis=-1, keepdims=True)
        ret = ((ret - mu) * lax.rsqrt(var + EPS)).reshape(B, S, RET_WIDTH) * ret_gn_g[l]
        ret_out = (jax.nn.silu(rg.astype(jnp.float32)) * ret).astype(x.dtype)
        mla_out = _mla(cq, ckv, kr, positions, q_norm_g[l], w_uq[l], kv_norm_g[l], w_uk[l], w_uv[l])
        mix = jnp.concatenate([ret_out, mla_out], axis=-1) @ w_out[l]
        x = _layer_norm(DEEPNORM_ALPHA * x + mix, ln1_g[l], ln1_b[l])
        h = jnp.square(jax.nn.relu(x @ w_up[l])) @ w_down[l]
        x = _layer_norm(DEEPNORM_ALPHA * x + h, ln2_g[l], ln2_b[l])
    return x
```

```python
import contextlib
import numpy as np
import ml_dtypes
import concourse.bass as bass
import concourse.mybir as mybir
from concourse.bass_utils import run_bass_kernel_spmd

F32 = mybir.dt.float32
BF16 = mybir.dt.bfloat16
I32 = mybir.dt.int32
AF = mybir.ActivationFunctionType
ALU = mybir.AluOpType

D = 2048
SEQ = 4096
NOWN = 2048
H = 8
DFF = 8192
EPS = 1e-5
ALPHA = 2.0 ** 0.25
IN_W = 5440
BIG = 30000.0
MAGIC = 12582912.0
TWO_PI = 6.283185307179586
C1 = 6.28125
C2 = TWO_PI - C1
SCALE = 192.0 ** -0.5
ENGS = ("pe", "act", "dve", "pool", "sp")
PIPE_B = False
PIPE_C = False


def tb_own(j):
    return (j // 8) * 16 + j % 8


def tb_oth(j):
    return (j // 8) * 16 + 8 + j % 8


def _is_psum_key(k):
    n = k if isinstance(k, str) else k[0]
    return isinstance(n, str) and len(n) > 1 and n[0] == "p" and n[1].isupper()


class Op:
    __slots__ = ("eng", "fn", "reads", "writes", "dma", "deps", "idx", "sig", "dsem", "dval")


class Prog:
    sems = None

    def __init__(self, nc, same_engine_sync=True, dma_ring=8):
        self.nc = nc
        self.ops = []
        self.same_engine_sync = same_engine_sync
        self.dma_ring = dma_ring
        self.last_writer = {}
        self.readers = {}

    def add(self, eng, fn, reads=(), writes=(), dma=False):
        op = Op()
        ps_r = [k for k in reads if _is_psum_key(k)]
        if ps_r:
            reads = [k for k in reads if not _is_psum_key(k)]
            writes = list(writes) + [k for k in ps_r if k not in writes]
        op.eng, op.fn, op.reads, op.writes, op.dma = eng, fn, tuple(reads), tuple(writes), dma
        op.sig = None
        op.dsem = None
        op.dval = 0
        op.idx = len(self.ops)
        deps = set()
        for k in op.reads:
            w = self.last_writer.get(k)
            if w is not None:
                deps.add(w)
        for k in op.writes:
            w = self.last_writer.get(k)
            if w is not None:
                deps.add(w)
            for r in self.readers.get(k, {}).values():
                deps.add(r)
        deps.discard(op.idx)
        op.deps = sorted(deps)
        for k in op.reads:
            d = self.readers.setdefault(k, {})
            d[(eng, op.idx) if dma else eng] = op.idx
        for k in op.writes:
            self.last_writer[k] = op.idx
            self.readers[k] = {}
        self.ops.append(op)
        return op

    def emit(self):
        nc = self.nc
        ops = self.ops
        ses = self.same_engine_sync
        need = [False] * len(ops)
        for op in ops:
            for d in op.deps:
                dop = ops[d]
                if dop.dma:
                    continue
                if dop.eng == op.eng and not op.dma and (dop.eng == "pe" or not ses):
                    continue
                need[d] = True
        G = self.sems
        cnt = G["cnt"]
        dcount = G["dcount"]
        base_cnt = dict(cnt)
        base_d = dict(dcount)
        per_eng = {e: [] for e in ENGS}
        for op in ops:
            if op.dma:
                n = dcount[op.eng]
                dcount[op.eng] += 1
                op.dsem = n % self.dma_ring
                op.dval = 16 * (n // self.dma_ring + 1)
            elif need[op.idx]:
                cnt[op.eng] += 1
                op.sig = cnt[op.eng]
            per_eng[op.eng].append(op)
        with contextlib.ExitStack() as es:
            csem = G["csem"]
            dsem = G["dsem"]
            block = es.enter_context(nc.Block())

            def run_engine(e, handle):
                waited = {}
                for e2 in ENGS:
                    waited[("c", e2)] = base_cnt[e2]
                    n0 = base_d[e2]
                    for r in range(self.dma_ring):
                        uses = (n0 - 1 - r) // self.dma_ring + 1 if n0 > r else 0
                        waited[("d", e2, r)] = 16 * uses

                def wait(key, sem, val):
                    if waited.get(key, 0) >= val:
                        return
                    handle.wait_ge(sem, val)
                    waited[key] = val

                for op in per_eng[e]:
                    for d in op.deps:
                        dop = ops[d]
                        if dop.dma:
                            wait(("d", dop.eng, dop.dsem), dsem[dop.eng][dop.dsem], dop.dval)
                        else:
                            if dop.eng == e and not op.dma and (e == "pe" or not ses):
                                continue
                            wait(("c", dop.eng), csem[dop.eng], dop.sig)
                    if op.dma:
                        if op.dval > 16:
                            wait(("d", e, op.dsem), dsem[e][op.dsem], op.dval - 16)
                        ins = op.fn(handle)
                        ins.then_inc(dsem[e][op.dsem], 16)
                    else:
                        ins = op.fn(handle)
                        if op.sig is not None:
                            ins.then_inc(csem[e], 1)
                n = dcount[e]
                for r in range(min(n, self.dma_ring)):
                    uses = (n - 1 - r) // self.dma_ring + 1
                    wait(("d", e, r), dsem[e][r], 16 * uses)

            if per_eng["sp"]:
                @block.sync
                def _(h):
                    run_engine("sp", h)
            if per_eng["pool"]:
                @block.gpsimd
                def _(h):
                    run_engine("pool", h)
            if per_eng["act"]:
                @block.scalar
                def _(h):
                    run_engine("act", h)
            if per_eng["dve"]:
                @block.vector
                def _(h):
                    run_engine("dve", h)
            if per_eng["pe"]:
                @block.tensor
                def _(h):
                    run_engine("pe", h)


def dma(P, q, out, in_, reads=(), writes=()):
    P.add(q, lambda e: e.dma_start(out=out, in_=in_), reads=reads, writes=writes, dma=True)


def mm_group(P, out, pairs, reads, writes):
    def fn(e):
        n = len(pairs)
        ins = None
        for i, (l, r) in enumerate(pairs):
            ins = e.matmul(out, lhsT=l, rhs=r, start=(i == 0), stop=(i == n - 1))
        return ins
    P.add("pe", fn, reads=reads, writes=writes)


def transposes(P, items, ident, reads, writes):
    def fn(e):
        ins = None
        for o, i in items:
            ins = e.transpose(out=o, in_=i, identity=ident)
        return ins
    P.add("pe", fn, reads=reads, writes=writes)


def rope_ops(P, eng, dst, src, cos, sin, half, tmp, tmpkey, reads, writes):
    h = half

    def f1(e):
        e.tensor_tensor(out=tmp[:, 0:h], in0=src[:, 0:h], in1=cos, op=ALU.mult)
        e.tensor_tensor(out=tmp[:, h:2 * h], in0=src[:, h:2 * h], in1=sin, op=ALU.mult)
        e.tensor_tensor(out=tmp[:, 2 * h:3 * h], in0=src[:, 0:h], in1=sin, op=ALU.mult)
        return e.tensor_tensor(out=tmp[:, 3 * h:4 * h], in0=src[:, h:2 * h], in1=cos, op=ALU.mult)

    def f2(e):
        e.tensor_tensor(out=dst[:, 0:h], in0=tmp[:, 0:h], in1=tmp[:, h:2 * h], op=ALU.subtract)
        return e.tensor_tensor(out=dst[:, h:2 * h], in0=tmp[:, 2 * h:3 * h], in1=tmp[:, 3 * h:4 * h], op=ALU.add)
    P.add(eng, f1, reads=reads, writes=[tmpkey])
    P.add(eng, f2, reads=[tmpkey], writes=writes)


def rstd_ops(P, out, ssum, scale, eps_ap_or_float, tmp, reads, writes, key):
    def f1(e):
        if isinstance(eps_ap_or_float, float):
            return e.tensor_scalar(out=tmp, in0=ssum, scalar1=scale, scalar2=eps_ap_or_float, op0=ALU.mult, op1=ALU.add)
        return e.tensor_scalar(out=tmp, in0=ssum, scalar1=scale, scalar2=eps_ap_or_float, op0=ALU.mult, op1=ALU.add)
    k = ("rstd_tmp", key)
    P.add("dve", f1, reads=reads, writes=[k])
    P.add("act", lambda e: e.activation(out=tmp, in_=tmp, func=AF.Sqrt), reads=[k], writes=[k])
    P.add("dve", lambda e: e.reciprocal(out=out, in_=tmp), reads=[k], writes=writes)


class Ctx:
    pass


def build(debug=False, phases="0 A2 B A1 C D"):
    phases = phases.split()
    nc = bass.Bass("TRN2", target_bir_lowering=False)
    C = Ctx()
    C.nc = nc
    C.debug = debug

    def din(name, shape, dt=F32):
        return nc.dram_tensor(name, list(shape), dt, kind="ExternalInput").ap()

    xT = din("xT", [D, SEQ])
    x_own = din("x_own", [NOWN, D])
    pos_tm = din("pos_tm", [128, 32], I32)
    kmask = din("kmask", [128, 32])
    w_in = din("w_in", [D, IN_W])
    w_uq = din("w_uq", [768, 1536])
    w_uk = din("w_uk", [512, 1024])
    w_uv = din("w_uv", [512, 1024])
    w_out = din("w_out", [D, D])
    w_up = din("w_up", [D, DFF])
    w_down = din("w_down", [DFF, D])
    qg = din("qg", [128, 6])
    kvg = din("kvg", [128, 4])
    gng = din("gng", [1, 1024])
    ln1g = din("ln1g", [1, D])
    ln1b = din("ln1b", [1, D])
    ln2g = din("ln2g", [1, D])
    ln2b = din("ln2b", [1, D])
    c_ident = din("c_ident", [128, 128])
    c_inv64 = din("c_inv64", [128, 64])
    c_inv32 = din("c_inv32", [128, 32])
    c_decA = din("c_decA", [128, 8])
    c_decB = din("c_decB", [128, 8])
    c_epsq = din("c_epsq", [128, 8])
    c_d2t = din("c_d2t", [128, 8 * 128])
    c_dk = din("c_dk", [1, 128])
    c_dq = din("c_dq", [1, 512])
    out = nc.dram_tensor("out", [NOWN, D], F32, kind="ExternalOutput").ap()
    y1s = nc.dram_tensor("y1s", [NOWN, D], F32).ap()
    x1Ts = nc.dram_tensor("x1Ts", [D, NOWN], BF16).ap()
    dbg = {}
    if debug:
        dbg["mixR"] = nc.dram_tensor("dbg_mixR", [128, 8 * NOWN], F32, kind="ExternalOutput").ap()
        dbg["mixM"] = nc.dram_tensor("dbg_mixM", [128, 8 * NOWN], F32, kind="ExternalOutput").ap()
        dbg["cqnT"] = nc.dram_tensor("dbg_cqnT", [128, 6 * NOWN], F32, kind="ExternalOutput").ap()
        dbg["ckvnT"] = nc.dram_tensor("dbg_ckvnT", [128, 4 * SEQ], F32, kind="ExternalOutput").ap()
        dbg["krT"] = nc.dram_tensor("dbg_krT", [65, SEQ], F32, kind="ExternalOutput").ap()
        dbg["y1"] = y1s

    xTv = xT.rearrange("(c p) t -> p c t", p=128)
    w_inv = w_in.rearrange("(c p) n -> p c n", p=128)
    w_uqv = w_uq.rearrange("(c p) n -> p c n", p=128)
    w_ukv = w_uk.rearrange("(c p) n -> p c n", p=128)
    w_uvv = w_uv.rearrange("(c p) n -> p c n", p=128)
    w_outv = w_out.rearrange("(c p) n -> p c n", p=128)
    w_upv = w_up.rearrange("(c p) n -> p c n", p=128)
    w_downv = w_down.rearrange("(c p) n -> p c n", p=128)
    x1Tsv = x1Ts.rearrange("(c p) t -> p c t", p=128)

    def sbt(es, name, shape, dt):
        return es.enter_context(nc.sbuf_tensor(name, list(shape), dt))

    def pst(es, name, shape, dt):
        return es.enter_context(nc.psum_tensor(name, list(shape), dt))

    def dump(P, es, name, src_tile, ncols, src_key, dt=BF16, parts=128):
        stg = sbt(es, "dstg_" + name, [parts, 2048], F32)
        flat = src_tile
        for c0 in range(0, ncols, 2048):
            n = min(2048, ncols - c0)
            P.add("dve", (lambda c0=c0, n=n: lambda e: e.tensor_copy(out=stg[:, 0:n], in_=flat[:, c0:c0 + n]))(),
                  reads=[src_key], writes=[("dstg", name)])
            dma(P, "sp", dbg[name][:, c0:c0 + n], stg[:, 0:n], reads=[("dstg", name)])

    with contextlib.ExitStack() as g_es:
        Prog.sems = {
            "cnt": {e: 0 for e in ENGS}, "dcount": {e: 0 for e in ENGS},
            "csem": {e: g_es.enter_context(nc.semaphore("c_" + e)) for e in ENGS},
            "dsem": {e: [g_es.enter_context(nc.semaphore("d_%s_%d" % (e, i))) for i in range(8)] for e in ("sp", "pool")},
        }
        ident = sbt(g_es, "ident", [128, 128], BF16)
        with contextlib.ExitStack() as s1:
            mixM = sbt(s1, "mixM", [128, 8, NOWN], BF16)
            posf = sbt(s1, "posf", [128, 32], F32)
            kmask_sb = sbt(s1, "kmask_sb", [128, 32], F32)
            s2 = contextlib.ExitStack()
            cosM = sbt(s2, "cosM", [128, 32, 32], F32)
            sinM = sbt(s2, "sinM", [128, 32, 32], F32)

            def rope_tables(P, es, inv_dram, nf, cos_t, sin_t, tag):
                n = 32 * nf
                inv = sbt(es, "inv" + tag, [128, nf], F32)
                ang = sbt(es, "ang" + tag, [128, 32, nf], F32)
                kk = sbt(es, "kk" + tag, [128, 32, nf], F32)
                dma(P, "sp", inv[:], inv_dram[:, :], writes=["inv" + tag])
                angf = ang[:].rearrange("p a b -> p (a b)")
                kkf = kk[:].rearrange("p a b -> p (a b)")
                cosf = cos_t[:].rearrange("p a b -> p (a b)")
                sinf = sin_t[:].rearrange("p a b -> p (a b)")

                def f_ang(e):
                    ins = None
                    for tb in range(32):
                        ins = e.tensor_scalar(out=ang[:, tb, :], in0=inv[:], scalar1=posf[:, tb:tb + 1], scalar2=None,
                                              op0=ALU.mult)
                    return ins
                P.add("dve", f_ang, reads=["inv" + tag, "posf"], writes=["ang" + tag])
                ka = "ang" + tag
                kkk = "kk" + tag
                P.add("dve", lambda e: e.tensor_scalar(out=kkf, in0=angf, scalar1=1.0 / TWO_PI, scalar2=None, op0=ALU.mult),
                      reads=[ka], writes=[kkk])
                P.add("dve", lambda e: e.tensor_scalar(out=kkf, in0=kkf, scalar1=MAGIC, scalar2=None, op0=ALU.add),
                      reads=[kkk], writes=[kkk])
                P.add("dve", lambda e: e.tensor_scalar(out=kkf, in0=kkf, scalar1=-MAGIC, scalar2=None, op0=ALU.add),
                      reads=[kkk], writes=[kkk])
                P.add("dve", lambda e: e.scalar_tensor_tensor(out=angf, in0=kkf, scalar=-C1, in1=angf, op0=ALU.mult, op1=ALU.add),
                      reads=[kkk, ka], writes=[ka])
                P.add("dve", lambda e: e.scalar_tensor_tensor(out=angf, in0=kkf, scalar=-C2, in1=angf, op0=ALU.mult, op1=ALU.add),
                      reads=[kkk, ka], writes=[ka])
                P.add("dve", lambda e: e.tensor_scalar(out=angf, in0=angf, scalar1=3.1415925, scalar2=-3.1415925,
                                                      op0=ALU.min, op1=ALU.max), reads=[ka], writes=[ka])
                P.add("act", lambda e: e.activation(out=sinf, in_=angf, func=AF.Sin), reads=[ka], writes=["sin" + tag])
                P.add("act", lambda e: e.activation(out=kkf, in_=angf, func=AF.Sin, scale=0.5), reads=[ka], writes=[kkk])
                P.add("dve", lambda e: e.tensor_tensor(out=kkf, in0=kkf, in1=kkf, op=ALU.mult), reads=[kkk], writes=[kkk])
                P.add("dve", lambda e: e.tensor_scalar(out=cosf, in0=kkf, scalar1=-2.0, scalar2=1.0, op0=ALU.mult, op1=ALU.add),
                      reads=[kkk], writes=["cos" + tag])

            if "0" in phases:
                with contextlib.ExitStack() as es:
                    P = Prog(nc)
                    posi = sbt(es, "posi", [128, 32], I32)
                    dma(P, "pool", ident[:], c_ident[:, :], writes=["ident"])
                    dma(P, "sp", posi[:], pos_tm[:, :], writes=["posi"])
                    dma(P, "sp", kmask_sb[:], kmask[:, :], writes=["kmask"])
                    P.add("dve", lambda e: e.tensor_copy(out=posf[:], in_=posi[:]), reads=["posi"], writes=["posf"])
                    rope_tables(P, es, c_inv32, 32, cosM, sinM, "M")
                    P.emit()

            with s2:
                cqnT = sbt(s2, "cqnT", [128, 6, NOWN], BF16)
                ckvnT = sbt(s2, "ckvnT", [128, 4, SEQ], BF16)
                kraugT = sbt(s2, "kraugT", [65, SEQ], BF16)
                if "A2" in phases:
                    with nc.named_scope("A2"):
                        phase_A2(C, nc, sbt, pst, ident, xTv, w_inv, cosM, sinM, kmask_sb, cqnT, ckvnT, kraugT)
                    if debug:
                        with contextlib.ExitStack() as es:
                            P = Prog(nc)
                            dump(P, es, "cqnT", cqnT[:].rearrange("p a b -> p (a b)"), 6 * NOWN, "x")
                            dump(P, es, "ckvnT", ckvnT[:].rearrange("p a b -> p (a b)"), 4 * SEQ, "x")
                            stg = sbt(es, "dstg_kr", [65, SEQ], F32)
                            P.add("dve", lambda e: e.tensor_copy(out=stg[:], in_=kraugT[:]), writes=["stgkr"])
                            dma(P, "sp", dbg["krT"][:, :], stg[:], reads=["stgkr"])
                            P.emit()
                if "B" in phases:
                    with nc.named_scope("B"):
                        phase_B(C, nc, sbt, pst, ident, w_uqv, w_ukv, w_uvv, qg, kvg, c_dk, c_dq, cosM, sinM,
                                cqnT, ckvnT, kraugT, mixM)
            mixR = sbt(s1, "mixR", [128, 8, NOWN], BF16)
            if "A1" in phases:
                with nc.named_scope("A1"):
                    phase_A1(C, nc, sbt, pst, ident, xTv, w_inv, c_inv64, posf, c_decA, c_decB, c_epsq, c_d2t, gng,
                             mixR, rope_tables)
            if debug and ("A1" in phases or "B" in phases):
                with contextlib.ExitStack() as es:
                    P = Prog(nc)
                    if "A1" in phases:
                        dump(P, es, "mixR", mixR[:].rearrange("p a b -> p (a b)"), 8 * NOWN, "x")
                    if "B" in phases:
                        dump(P, es, "mixM", mixM[:].rearrange("p a b -> p (a b)"), 8 * NOWN, "x")
                    P.emit()
            if "C" in phases:
                with nc.named_scope("C"):
                    phase_C(C, nc, sbt, pst, ident, w_outv, x_own, ln1g, ln1b, mixR, mixM, y1s, x1Tsv)
        if "D" in phases:
            with nc.named_scope("D"):
                phase_D(C, nc, sbt, pst, w_upv, w_downv, ln2g, ln2b, y1s, x1Tsv, out)
    return nc


def phase_A2(C, nc, sbt, pst, ident, xTv, w_inv, cosM, sinM, kmask_sb, cqnT, ckvnT, kraugT):
    with contextlib.ExitStack() as es:
        P = Prog(nc)
        xTb = sbt(es, "a2_xT", [128, 16, 1024], BF16)
        Wg8 = sbt(es, "a2_w8", [128, 16, 512], BF16)
        Wg9 = sbt(es, "a2_w9", [128, 16, 448], BF16)
        Wg10 = sbt(es, "a2_w10", [128, 16, 384], BF16)
        junk = sbt(es, "a2_junk", [128, 512], BF16)
        NR = 2
        ss = [sbt(es, "a2_ss%d" % i, [128, 8], F32) for i in range(NR)]
        ckvn_tm = [sbt(es, "a2_ckvn%d" % i, [128, 512], BF16) for i in range(NR)]
        cq_raw = [sbt(es, "a2_cqraw%d" % i, [128, 768], F32) for i in range(NR)]
        cqn_tm = [sbt(es, "a2_cqn%d" % i, [128, 768], BF16) for i in range(NR)]
        kaug = [sbt(es, "a2_kaug%d" % i, [128, 65], BF16) for i in range(NR)]
        rtmp = [sbt(es, "a2_rtmp%d" % i, [128, 128], F32) for i in range(NR)]
        pA = [pst(es, "a2_pA%d" % i, [128, 512], F32) for i in range(3)]
        pT0 = pst(es, "a2_pT0", [128, 1024], BF16)
        pT1 = pst(es, "a2_pT1", [128, 1024], BF16)
        pT2 = pst(es, "a2_pT2", [128, 1024], BF16)
        for c in range(16):
            dma(P, "pool", Wg8[:, c, :], w_inv[:, c, 4096:4608], writes=[("w8", c)])
        for c in range(16):
            dma(P, "pool", Wg9[:, c, :], w_inv[:, c, 4608:5056], writes=[("w9", c)])
        for c in range(16):
            dma(P, "pool", Wg10[:, c, :], w_inv[:, c, 5056:5440], writes=[("w10", c)])
        it = 0
        pai = 0
        import os
        QL = int(os.environ.get("A2_Q", "4"))
        KL = int(os.environ.get("A2_K", "8"))
        for q in range(QL):
            for c in range(16):
                dma(P, "pool", xTb[:, c, :], xTv[:, c, q * 1024:(q + 1) * 1024], writes=[("xT", c)])
            own = (q % 2 == 0)
            for k in range(KL):
                tb = q * 8 + k
                j = (q // 2) * 8 + k
                r = it % NR
                it += 1
                xk = [("xT", c) for c in range(16)]
                pa = pA[pai % 3]; pak = ("pA", pai % 3); pai += 1
                mm_group(P, pa[:, 0:512], [(xTb[:, c, k * 128:(k + 1) * 128], Wg8[:, c, :]) for c in range(16)],
                         reads=xk + [("w8", c) for c in range(16)], writes=[pak])
                P.add("act", (lambda pa=pa, r=r: lambda e: e.activation(out=junk[:], in_=pa[:, 0:512], func=AF.Square,
                                                                     accum_out=ss[r][:, 0:1]))(),
                      reads=[pak], writes=[("ss0", r), "junk"])
                rstd_ops(P, ss[r][:, 4:5], ss[r][:, 0:1], 1.0 / 512, EPS, ss[r][:, 3:4], reads=[("ss0", r)], writes=[("rs0", r)], key=("a2a", r))
                P.add("dve", (lambda pa=pa, r=r: lambda e: e.tensor_scalar(out=ckvn_tm[r][:], in0=pa[:, 0:512],
                                                                         scalar1=ss[r][:, 4:5], scalar2=None, op0=ALU.mult))(),
                      reads=[pak, ("rs0", r)], writes=[("ckvn", r)])
                transposes(P, [(pT0[:, c * 128:(c + 1) * 128], ckvn_tm[r][:, c * 128:(c + 1) * 128]) for c in range(4)],
                           ident[:], reads=[("ckvn", r), "ident"], writes=["pT0a"])
                P.add("act", (lambda tb=tb: lambda e: e.activation(
                    out=ckvnT[:, 0:4, tb * 128:(tb + 1) * 128],
                    in_=pT0[:, 0:512].rearrange("p (c t) -> p c t", c=4), func=AF.Copy))(),
                    reads=["pT0a"], writes=[("ckvnT", tb)])
                ncol = 448 if own else 64
                pa = pA[pai % 3]; pak = ("pA", pai % 3); pai += 1
                mm_group(P, pa[:, 0:ncol], [(xTb[:, c, k * 128:(k + 1) * 128], Wg9[:, c, 0:ncol]) for c in range(16)],
                         reads=xk + [("w9", c) for c in range(16)], writes=[pak])
                rope_ops(P, "dve", kaug[r], pa, cosM[:, tb, :], sinM[:, tb, :], 32, rtmp[r], ("rtmp", r),
                         reads=[pak, "cosM", "sinM"], writes=[("kaug", r)])
                P.add("dve", (lambda r=r, tb=tb: lambda e: e.tensor_copy(out=kaug[r][:, 64:65], in_=kmask_sb[:, tb:tb + 1]))(),
                      reads=["kmask"], writes=[("kaug", r)])
                transposes(P, [(pT2[0:65, 0:128], kaug[r][:, 0:65])], ident[:], reads=[("kaug", r), "ident"], writes=["pT2"])
                P.add("act", (lambda tb=tb: lambda e: e.activation(out=kraugT[0:65, tb * 128:(tb + 1) * 128],
                                                                 in_=pT2[0:65, 0:128], func=AF.Copy))(),
                      reads=["pT2"], writes=[("kraugT", tb)])
                if own:
                    P.add("dve", (lambda pa=pa, r=r: lambda e: e.tensor_copy(out=cq_raw[r][:, 0:384], in_=pa[:, 64:448]))(),
                          reads=[pak], writes=[("cqraw0", r)])
                    P.add("act", (lambda pa=pa, r=r: lambda e: e.activation(out=junk[:, 0:384], in_=pa[:, 64:448], func=AF.Square,
                                                                         accum_out=ss[r][:, 1:2]))(),
                          reads=[pak], writes=[("ss1", r), "junk"])
                    pa = pA[pai % 3]; pak = ("pA", pai % 3); pai += 1
                    mm_group(P, pa[:, 0:384], [(xTb[:, c, k * 128:(k + 1) * 128], Wg10[:, c, :]) for c in range(16)],
                             reads=xk + [("w10", c) for c in range(16)], writes=[pak])
                    P.add("dve", (lambda pa=pa, r=r: lambda e: e.tensor_copy(out=cq_raw[r][:, 384:768], in_=pa[:, 0:384]))(),
                          reads=[pak], writes=[("cqraw1", r)])
                    P.add("act", (lambda pa=pa, r=r: lambda e: e.activation(out=junk[:, 0:384], in_=pa[:, 0:384], func=AF.Square,
                                                                         accum_out=ss[r][:, 2:3]))(),
                          reads=[pak], writes=[("ss2", r), "junk"])
                    P.add("dve", (lambda r=r: lambda e: e.tensor_tensor(out=ss[r][:, 5:6], in0=ss[r][:, 1:2], in1=ss[r][:, 2:3],
                                                                      op=ALU.add))(),
                          reads=[("ss1", r), ("ss2", r)], writes=[("ss12", r)])
                    rstd_ops(P, ss[r][:, 7:8], ss[r][:, 5:6], 1.0 / 768, EPS, ss[r][:, 6:7], reads=[("ss12", r)], writes=[("rs1", r)], key=("a2b", r))
                    P.add("dve", (lambda r=r: lambda e: e.tensor_scalar(out=cqn_tm[r][:], in0=cq_raw[r][:], scalar1=ss[r][:, 7:8],
                                                                      scalar2=None, op0=ALU.mult))(),
                          reads=[("cqraw0", r), ("cqraw1", r), ("rs1", r)], writes=[("cqn", r)])
                    transposes(P, [(pT1[:, c * 128:(c + 1) * 128], cqn_tm[r][:, c * 128:(c + 1) * 128]) for c in range(6)],
                               ident[:], reads=[("cqn", r), "ident"], writes=["pT1"])
                    P.add("act", (lambda j=j: lambda e: e.activation(
                        out=cqnT[:, 0:6, j * 128:(j + 1) * 128],
                        in_=pT1[:, 0:768].rearrange("p (c t) -> p c t", c=6), func=AF.Copy))(),
                        reads=["pT1"], writes=[("cqnT", j)])
        P.emit()


def phase_B(C, nc, sbt, pst, ident, w_uqv, w_ukv, w_uvv, qg, kvg, c_dk, c_dq, cosM, sinM, cqnT, ckvnT, kraugT, mixM):
    with contextlib.ExitStack() as es:
        P = Prog(nc)
        qg_sb = sbt(es, "b_qg", [128, 6], F32)
        kvg_sb = sbt(es, "b_kvg", [128, 4], F32)
        dk = sbt(es, "b_dk", [1, 128], BF16)
        dq = sbt(es, "b_dq", [1, 512], BF16)
        wq_f = sbt(es, "b_wqf", [128, 6, 192], F32)
        wk_f = sbt(es, "b_wkf", [128, 4, 128], F32)
        wv_f = sbt(es, "b_wvf", [128, 4, 128], F32)
        NH = 2
        wq_b = [sbt(es, "b_wqb%d" % i, [128, 6, 192], BF16) for i in range(NH)]
        wk_b = [sbt(es, "b_wkb%d" % i, [128, 4, 128], BF16) for i in range(NH)]
        wv_b = [sbt(es, "b_wvb%d" % i, [128, 4, 128], BF16) for i in range(NH)]
        knT = [sbt(es, "b_knT%d" % i, [128, SEQ], BF16) for i in range(NH)]
        Vaug = [sbt(es, "b_V%d" % i, [128, 32, 129], BF16) for i in range(NH)]
        qnT = [sbt(es, "b_qnT%d" % i, [128, NOWN], BF16) for i in range(NH)]
        qrT = [sbt(es, "b_qrT%d" % i, [65, NOWN], BF16) for i in range(NH)]
        qtm = [sbt(es, "b_qtm%d" % i, [128, 193], BF16) for i in range(2)]
        rtmp = [sbt(es, "b_rtmp%d" % i, [128, 128], F32) for i in range(2)]
        PT = [sbt(es, "b_PT%d" % i, [128, 512], BF16) for i in range(3)]
        otm = [sbt(es, "b_otm%d" % i, [128, 128], BF16) for i in range(2)]
        rinv = [sbt(es, "b_rinv%d" % i, [128, 1], F32) for i in range(2)]
        pP = pst(es, "b_pP", [128, 512], F32)
        pS = [pst(es, "b_pS%d" % i, [128, 512], F32) for i in range(2)]
        pO = [pst(es, "b_pO%d" % i, [128, 512], F32) for i in range(4)]
        pT = pst(es, "b_pT", [128, 1024], BF16)

        dma(P, "sp", qg_sb[:], qg[:, :], writes=["qg"])
        dma(P, "sp", kvg_sb[:], kvg[:, :], writes=["kvg"])
        dma(P, "pool", dk[:], c_dk[:, :], writes=["dk"])
        dma(P, "pool", dq[:], c_dq[:, :], writes=["dq"])
        for i in range(NH):
            P.add("pool", (lambda i=i: lambda e: e.memset(Vaug[i][:, :, 128:129], 1.0))(), writes=[("Vone", i)])
        for i in range(2):
            P.add("pool", (lambda i=i: lambda e: e.memset(qtm[i][:, 192:193], -1.0))(), writes=[("qtm1", i)])

        import os
        HL = int(os.environ.get("B_H", "8"))
        QL = int(os.environ.get("B_Q", "4"))
        cnt = {"si": 0, "pti": 0, "oi": 0, "qi": 0}

        def proj_steps(h):
            hb = h % NH
            steps = []

            def w_step():
                dma(P, "sp", wq_f[:], w_uqv[:, :, h * 192:(h + 1) * 192], writes=["wqf"])
                dma(P, "sp", wk_f[:], w_ukv[:, :, h * 128:(h + 1) * 128], writes=["wkf"])
                dma(P, "sp", wv_f[:], w_uvv[:, :, h * 128:(h + 1) * 128], writes=["wvf"])

                def f_wq(e):
                    ins = None
                    for c in range(6):
                        ins = e.tensor_scalar(out=wq_b[hb][:, c, :], in0=wq_f[:, c, :], scalar1=qg_sb[:, c:c + 1], scalar2=None,
                                              op0=ALU.mult)
                    return ins
                P.add("pool", f_wq, reads=["wqf", "qg"], writes=[("wqb", hb)])

                def f_wkv(e):
                    ins = None
                    for c in range(4):
                        e.tensor_scalar(out=wk_b[hb][:, c, :], in0=wk_f[:, c, :], scalar1=kvg_sb[:, c:c + 1], scalar2=None, op0=ALU.mult)
                        ins = e.tensor_scalar(out=wv_b[hb][:, c, :], in0=wv_f[:, c, :], scalar1=kvg_sb[:, c:c + 1], scalar2=None,
                                              op0=ALU.mult)
                    return ins
                P.add("pool", f_wkv, reads=["wkf", "wvf", "kvg"], writes=[("wkvb", hb)])
            steps.append(w_step)

            def q_step(j):
                def f():
                    r = cnt["qi"] % 2
                    cnt["qi"] += 1
                    mm_group(P, pP[:, 0:192], [(cqnT[:, c, j * 128:(j + 1) * 128], wq_b[hb][:, c, :]) for c in range(6)],
                             reads=[("cqnT", j), ("wqb", hb)], writes=["pP"])
                    P.add("act", lambda e: e.activation(out=qtm[r][:, 0:128], in_=pP[:, 0:128], func=AF.Copy),
                          reads=["pP"], writes=[("qtmA", r)])
                    tbq = tb_own(j)
                    rope_ops(P, "dve", qtm[r][:, 128:192], pP[:, 128:192], cosM[:, tbq, :], sinM[:, tbq, :], 32, rtmp[r], ("rtmp", r),
                             reads=["pP", "cosM", "sinM"], writes=[("qtmB", r)])
                    transposes(P, [(pT[:, 0:128], qtm[r][:, 0:128]), (pT[0:65, 128:256], qtm[r][:, 128:193])], ident[:],
                               reads=[("qtmA", r), ("qtmB", r), ("qtm1", r), "ident"], writes=["pT"])
                    P.add("dve", lambda e: e.tensor_copy(out=qnT[hb][:, j * 128:(j + 1) * 128], in_=pT[:, 0:128]),
                          reads=["pT"], writes=[("qnT", hb, j)])
                    P.add("act", lambda e: e.activation(out=qrT[hb][0:65, j * 128:(j + 1) * 128], in_=pT[0:65, 128:256], func=AF.Copy),
                          reads=["pT"], writes=[("qrT", hb, j)])
                return f

            def k_step(kg):
                def f():
                    mm_group(P, pP[:, 0:512], [(wk_b[hb][:, c, :], ckvnT[:, c, kg * 512:(kg + 1) * 512]) for c in range(4)],
                             reads=[("ckvnT", kg * 4 + i) for i in range(4)] + [("wkvb", hb)], writes=["pP"])
                    if kg % 2 == 0:
                        P.add("dve", lambda e: e.tensor_copy(out=knT[hb][:, kg * 512:(kg + 1) * 512], in_=pP[:, 0:512]),
                              reads=["pP"], writes=[("knT", hb, kg * 4 + i) for i in range(4)])
                    else:
                        P.add("act", lambda e: e.activation(out=knT[hb][:, kg * 512:(kg + 1) * 512], in_=pP[:, 0:512], func=AF.Copy),
                              reads=["pP"], writes=[("knT", hb, kg * 4 + i) for i in range(4)])
                return f

            def v_step(vg):
                def f():
                    def f_v(e):
                        ins = None
                        for i in range(4):
                            tb = vg * 4 + i
                            for c in range(4):
                                ins = e.matmul(pP[:, i * 128:(i + 1) * 128], lhsT=ckvnT[:, c, tb * 128:(tb + 1) * 128],
                                               rhs=wv_b[hb][:, c, :], start=(c == 0), stop=(c == 3))
                        return ins
                    P.add("pe", f_v, reads=[("ckvnT", vg * 4 + i) for i in range(4)] + [("wkvb", hb)], writes=["pP"])
                    vout = Vaug[hb][:, vg * 4:(vg + 1) * 4, 0:128]
                    vin = pP[:, 0:512].rearrange("p (a b) -> p a b", a=4)
                    if vg % 2 == 1:
                        P.add("dve", lambda e: e.tensor_copy(out=vout, in_=vin), reads=["pP"],
                              writes=[("V", hb, vg * 4 + i) for i in range(4)])
                    else:
                        P.add("act", lambda e: e.activation(out=vout, in_=vin, func=AF.Copy), reads=["pP"],
                              writes=[("V", hb, vg * 4 + i) for i in range(4)])
                return f
            for j in range(16):
                steps.append(q_step(j))
            for kg in range(8):
                steps.append(k_step(kg))
            for vg in range(8):
                steps.append(v_step(vg))
            return steps

        def attn_tiles(h):
            tiles = []
            for Q in range(QL):
                for i in range(4 * Q + 4):
                    for typ in (0, 1):
                        tb = tb_own(i) if typ == 0 else tb_oth(i)
                        if i < 4 * Q:
                            qoff, nq = 0, 4
                        else:
                            qoff, nq = i - 4 * Q, 4 * Q + 4 - i
                        tiles.append(dict(Q=Q, i=i, typ=typ, tb=tb, qoff=qoff, nq=nq, N=nq * 128, c0=(4 * Q + qoff) * 128,
                                          diag=(typ == 0 and i >= 4 * Q), lastQ=(i == 4 * Q + 3 and typ == 1)))
            return tiles

        def emit_S(h, T):
            hb = h % NH
            sb_ = cnt["si"] % 2
            cnt["si"] += 1
            pr = cnt["pti"] % 3
            cnt["pti"] += 1
            T["pr"] = pr
            tb, N, c0, diag = T["tb"], T["N"], T["c0"], T["diag"]

            def f_s(e):
                e.matmul(pS[sb_][:, 0:N], lhsT=knT[hb][:, tb * 128:(tb + 1) * 128], rhs=qnT[hb][:, c0:c0 + N],
                         start=True, stop=False)
                ins = e.matmul(pS[sb_][:, 0:N], lhsT=kraugT[0:65, tb * 128:(tb + 1) * 128],
                               rhs=qrT[hb][0:65, c0:c0 + N], start=False, stop=(not diag))
                if diag:
                    ins = e.matmul(pS[sb_][:, 0:N], lhsT=dk[0:1, :], rhs=dq[0:1, 0:N], start=False, stop=True)
                return ins
            qblocks = list(range(4 * T["Q"] + T["qoff"], 4 * T["Q"] + 4))
            P.add("pe", f_s, reads=[("knT", hb, tb), ("kraugT", tb), "dk", "dq"] +
                  [("qnT", hb, jj) for jj in qblocks] + [("qrT", hb, jj) for jj in qblocks],
                  writes=[("pS", sb_)])
            P.add("act", lambda e: e.activation(out=PT[pr][:, 0:N], in_=pS[sb_][:, 0:N], func=AF.Exp, scale=SCALE),
                  reads=[("pS", sb_)], writes=[("PT", pr)])

        def emit_PV(h, T):
            hb = h % NH
            pr, tb, nq, qoff, i, typ, Q = T["pr"], T["tb"], T["nq"], T["qoff"], T["i"], T["typ"], T["Q"]

            def f_pv(e):
                ins = None
                for a in range(nq):
                    jj = qoff + a
                    J = 4 * Q + jj
                    first = (i == 0 and typ == 0)
                    last = (i == J and typ == 1)
                    ins = e.matmul(pO[jj][:, 0:129], lhsT=PT[pr][:, a * 128:(a + 1) * 128], rhs=Vaug[hb][:, tb, :],
                                   start=first, stop=last)
                return ins
            P.add("pe", f_pv, reads=[("PT", pr), ("V", hb, tb), ("Vone", hb)], writes=[("pO", qoff + a) for a in range(nq)])

        def emit_fin(h, Q):
            for jj in range(4):
                J = 4 * Q + jj
                r = cnt["oi"] % 2
                cnt["oi"] += 1
                P.add("dve", (lambda jj=jj, r=r: lambda e: e.reciprocal(out=rinv[r][:], in_=pO[jj][:, 128:129]))(),
                      reads=[("pO", jj)], writes=[("rinv", r)])
                P.add("dve", (lambda jj=jj, r=r: lambda e: e.tensor_scalar(out=otm[r][:], in0=pO[jj][:, 0:128], scalar1=rinv[r][:],
                                                                         scalar2=None, op0=ALU.mult))(),
                      reads=[("pO", jj), ("rinv", r)], writes=[("otm", r)])
                transposes(P, [(pT[:, 256:384], otm[r][:])], ident[:], reads=[("otm", r), "ident"], writes=["pT"])
                P.add("act", (lambda J=J: lambda e: e.activation(out=mixM[:, h, J * 128:(J + 1) * 128], in_=pT[:, 256:384],
                                                               func=AF.Copy))(),
                      reads=["pT"], writes=[("mixinT", 8 + h, J)])

        if PIPE_B:
            for st_ in proj_steps(0):
                st_()
        for h in range(HL):
            if not PIPE_B:
                for st_ in proj_steps(h):
                    st_()
                for T in attn_tiles(h):
                    emit_S(h, T)
                    emit_PV(h, T)
                    if T["lastQ"]:
                        emit_fin(h, T["Q"])
                continue
            nxt = proj_steps(h + 1) if h + 1 < HL else []
            prev = None
            for t, T in enumerate(attn_tiles(h)):
                emit_S(h, T)
                if prev is not None:
                    emit_PV(h, prev)
                    if prev["lastQ"]:
                        emit_fin(h, prev["Q"])
                prev = T
                if nxt and t % 2 == 1:
                    nxt.pop(0)()
            emit_PV(h, prev)
            emit_fin(h, prev["Q"])
            while nxt:
                nxt.pop(0)()
        P.emit()


def phase_A1(C, nc, sbt, pst, ident, xTv, w_inv, c_inv64, posf, c_decA, c_decB, c_epsq, c_d2t, gng, mixR, rope_tables):
    with contextlib.ExitStack() as es:
        P = Prog(nc)
        cosR = sbt(es, "a1_cosR", [128, 32, 64], F32)
        sinR = sbt(es, "a1_sinR", [128, 32, 64], F32)
        with contextlib.ExitStack() as es0:
            P0 = Prog(nc)
            rope_tables(P0, es0, c_inv64, 64, cosR, sinR, "R")
            P0.emit()
        xTb = sbt(es, "a1_xT", [128, 16, 2048], BF16)
        Wg = [sbt(es, "a1_w%d" % i, [128, 16, 512], BF16) for i in range(2)]
        decA = sbt(es, "a1_decA", [128, 8], F32)
        decB = sbt(es, "a1_decB", [128, 8], F32)
        epsq = sbt(es, "a1_epsq", [128, 8], F32)
        d2t = sbt(es, "a1_d2t", [128, 8, 128], F32)
        gng_sb = sbt(es, "a1_gng", [128, 1024], F32)
        U32 = sbt(es, "a1_U", [128, 8, 128], F32)
        NR = 3
        T32 = [sbt(es, "a1_T%d" % i, [128, 128], F32) for i in range(NR)]
        Tb = [sbt(es, "a1_Tb%d" % i, [128, 128], BF16) for i in range(NR)]
        k_tm = [sbt(es, "a1_k%d" % i, [128, 128], BF16) for i in range(NR)]
        kd_tm = [sbt(es, "a1_kd%d" % i, [128, 128], BF16) for i in range(NR)]
        v_tm = [sbt(es, "a1_v%d" % i, [128, 128], BF16) for i in range(NR)]
        ko_tm = [sbt(es, "a1_ko%d" % i, [128, 128], BF16) for i in range(NR)]
        vo_tm = [sbt(es, "a1_vo%d" % i, [128, 128], BF16) for i in range(NR)]
        q_tm = [sbt(es, "a1_q%d" % i, [128, 128], BF16) for i in range(NR)]
        g_tm = [sbt(es, "a1_g%d" % i, [128, 128], F32) for i in range(NR)]
        qT = [sbt(es, "a1_qT%d" % i, [128, 128], BF16) for i in range(NR)]
        kT = [sbt(es, "a1_kT%d" % i, [128, 128], BF16) for i in range(NR)]
        Pm = [sbt(es, "a1_P%d" % i, [128, 128], BF16) for i in range(NR)]
        NT_ = 4
        rtmp = [sbt(es, "a1_rtmp%d" % i, [128, 384], F32) for i in range(NT_)]
        st = [sbt(es, "a1_st%d" % i, [128, 8], F32) for i in range(NR)]
        yn = [sbt(es, "a1_yn%d" % i, [128, 128], F32) for i in range(NR)]
        ret_tm = [sbt(es, "a1_ret%d" % i, [128, 128], BF16) for i in range(NR)]
        junk = sbt(es, "a1_junk", [128, 128], BF16)
        pA = [pst(es, "a1_pA%d" % i, [128, 512], F32) for i in range(3)]
        pS = pst(es, "a1_pS", [128, 512], F32)
        pO = pst(es, "a1_pO", [128, 512], F32)
        pKV = pst(es, "a1_pKV", [128, 512], F32)
        pT = pst(es, "a1_pT", [128, 1024], BF16)
        pT2 = pst(es, "a1_pT2", [128, 1024], BF16)

        dma(P, "sp", decA[:], c_decA[:, :], writes=["decA"])
        dma(P, "sp", decB[:], c_decB[:, :], writes=["decB"])
        dma(P, "sp", epsq[:], c_epsq[:, :], writes=["epsq"])
        dma(P, "sp", d2t[:].rearrange("p a b -> p (a b)"), c_d2t[:, :], writes=["d2t"])
        dma(P, "sp", gng_sb[:], gng[0:1, :].broadcast_to([128, 1024]), writes=["gng"])
        P.add("pool", lambda e: e.memset(U32[:].rearrange("p a b -> p (a b)"), 0.0), writes=[("U", h) for h in range(8)])
        g256 = [float(np.exp(256.0 * np.log1p(-2.0 ** (-5.0 - h)))) for h in range(8)]
        xk = [("xT", c) for c in range(16)]
        cnt = {"pai": 0, "rti": 0}

        def load_W(step):
            hh = step % 8
            for c in range(16):
                dma(P, "pool", Wg[step % 2][:, c, :], w_inv[:, c, hh * 512:(hh + 1) * 512], writes=[("W", step % 2, c)])

        iters = [(ps_, h, k) for ps_ in range(2) for h in range(8) for k in range(8)]
        ST = {}

        def S0(n):
            ps_, h, k = iters[n]
            step = ps_ * 8 + h
            wb = step % 2
            if k == 0:
                if h == 0:
                    for c in range(16):
                        dma(P, "pool", xTb[:, c, :], xTv[:, c, ps_ * 2048:(ps_ + 1) * 2048], writes=[("xT", c)])
                if step == 0:
                    load_W(0)
                if step + 1 < 16:
                    load_W(step + 1)
            wk = [("W", wb, c) for c in range(16)]
            ko, kw = 8 + k, k
            a0 = cnt["pai"] % 3; cnt["pai"] += 1
            a1 = cnt["pai"] % 3; cnt["pai"] += 1
            ST[n] = (a0, a1)
            mm_group(P, pA[a0][:, 0:256], [(xTb[:, c, ko * 128:(ko + 1) * 128], Wg[wb][:, c, 0:256]) for c in range(16)],
                     reads=xk + wk, writes=[("pA", a0)])
            mm_group(P, pA[a1][:, 0:512], [(xTb[:, c, kw * 128:(kw + 1) * 128], Wg[wb][:, c, :]) for c in range(16)],
                     reads=xk + wk, writes=[("pA", a1)])

        def S1(n):
            ps_, h, k = iters[n]
            j = ps_ * 8 + k
            r = n % NR
            a0, a1 = ST[n]
            tbo, tbw = tb_oth(j), tb_own(j)
            pa = pA[a0]; pak = ("pA", a0)
            ri = cnt["rti"] % NT_; cnt["rti"] += 1
            rt = rtmp[ri]; rtk = ("rtmp", ri)
            rope_ops(P, "dve", rt[:, 0:128], pa[:, 0:128], cosR[:, tbo, :], sinR[:, tbo, :], 64, rt[:, 128:384], (rtk, "t"),
                     reads=[pak, "cosR", "sinR"], writes=[rtk])
            P.add("pool", lambda e: e.tensor_scalar(out=ko_tm[r][:], in0=rt[:, 0:128], scalar1=decB[:, h:h + 1], scalar2=None,
                                                   op0=ALU.mult), reads=[rtk, "decB"], writes=[("ko", r)])
            P.add("act", lambda e: e.activation(out=vo_tm[r][:], in_=pa[:, 128:256], func=AF.Copy), reads=[pak], writes=[("vo", r)])
            pb = pA[a1]; pbk = ("pA", a1)
            ri = cnt["rti"] % NT_; cnt["rti"] += 1
            rt2 = rtmp[ri]; rtk2 = ("rtmp", ri)
            rope_ops(P, "dve", k_tm[r][:], pb[:, 0:128], cosR[:, tbw, :], sinR[:, tbw, :], 64, rt2[:], rtk2,
                     reads=[pbk, "cosR", "sinR"], writes=[("k", r)])
            ri = cnt["rti"] % NT_; cnt["rti"] += 1
            rt3 = rtmp[ri]; rtk3 = ("rtmp", ri)
            rope_ops(P, "dve", q_tm[r][:], pb[:, 256:384], cosR[:, tbw, :], sinR[:, tbw, :], 64, rt3[:], rtk3,
                     reads=[pbk, "cosR", "sinR"], writes=[("q", r)])
            P.add("act", lambda e: e.activation(out=v_tm[r][:], in_=pb[:, 128:256], func=AF.Copy), reads=[pbk], writes=[("v", r)])
            P.add("act", lambda e: e.activation(out=g_tm[r][:], in_=pb[:, 384:512], func=AF.Silu), reads=[pbk], writes=[("g", r)])
            P.add("pool", lambda e: e.tensor_scalar(out=kd_tm[r][:], in0=k_tm[r][:], scalar1=decA[:, h:h + 1], scalar2=None,
                                                   op0=ALU.mult), reads=[("k", r), "decA"], writes=[("kd", r)])

        def S2(n):
            ps_, h, k = iters[n]
            r = n % NR
            mm_group(P, pKV[:, 0:128], [(ko_tm[r][:], vo_tm[r][:])], reads=[("ko", r), ("vo", r)], writes=["pKV"])
            P.add("dve", lambda e: e.tensor_tensor(out=T32[r][:], in0=U32[:, h, :], in1=pKV[:, 0:128], op=ALU.add),
                  reads=["pKV", ("U", h)], writes=[("T32", r)])
            P.add("pool", lambda e: e.tensor_copy(out=Tb[r][:], in_=T32[r][:]), reads=[("T32", r)], writes=[("Tb", r)])
            mm_group(P, pKV[:, 128:256], [(kd_tm[r][:], v_tm[r][:])], reads=[("kd", r), ("v", r)], writes=["pKV"])
            P.add("dve", lambda e: e.scalar_tensor_tensor(out=U32[:, h, :], in0=T32[r][:], scalar=g256[h], in1=pKV[:, 128:256],
                                                         op0=ALU.mult, op1=ALU.add),
                  reads=["pKV", ("T32", r)], writes=[("U", h)])
            transposes(P, [(pT[:, 0:128], q_tm[r][:]), (pT[:, 128:256], k_tm[r][:])], ident[:],
                       reads=[("q", r), ("k", r), "ident"], writes=["pTqk"])
            P.add("act", lambda e: e.activation(out=qT[r][:], in_=pT[:, 0:128], func=AF.Copy), reads=["pTqk"], writes=[("qT", r)])
            P.add("dve", lambda e: e.tensor_copy(out=kT[r][:], in_=pT[:, 128:256]), reads=["pTqk"], writes=[("kT", r)])
            mm_group(P, pS[:, 0:128], [(kT[r][:], qT[r][:])], reads=[("kT", r), ("qT", r)], writes=["pS"])
            P.add("dve", lambda e: e.tensor_tensor(out=Pm[r][:], in0=pS[:, 0:128], in1=d2t[:, h, :], op=ALU.mult),
                  reads=["pS", "d2t"], writes=[("Pm", r)])

        def S3(n):
            ps_, h, k = iters[n]
            j = ps_ * 8 + k
            r = n % NR
            mm_group(P, pO[:, 0:128], [(Pm[r][:], v_tm[r][:]), (qT[r][:], Tb[r][:])],
                     reads=[("Pm", r), ("v", r), ("qT", r), ("Tb", r)], writes=["pO"])
            s = st[r]
            P.add("act", lambda e: e.activation(out=junk[:], in_=pO[:, 0:128], func=AF.Copy, accum_out=s[:, 0:1]),
                  reads=["pO"], writes=[("st0", r), "junk"])
            P.add("act", lambda e: e.activation(out=junk[:], in_=pO[:, 0:128], func=AF.Square, accum_out=s[:, 1:2]),
                  reads=["pO"], writes=[("st1", r), "junk"])
            P.add("dve", lambda e: e.tensor_scalar(out=s[:, 2:3], in0=s[:, 0:1], scalar1=1.0 / 128, scalar2=None, op0=ALU.mult),
                  reads=[("st0", r)], writes=[("stm", r)])
            P.add("dve", lambda e: e.tensor_tensor(out=s[:, 3:4], in0=s[:, 2:3], in1=s[:, 2:3], op=ALU.mult),
                  reads=[("stm", r)], writes=[("stm2", r)])
            P.add("dve", lambda e: e.scalar_tensor_tensor(out=s[:, 4:5], in0=s[:, 1:2], scalar=1.0 / 128, in1=s[:, 3:4],
                                                         op0=ALU.mult, op1=ALU.subtract),
                  reads=[("st1", r), ("stm2", r)], writes=[("stv", r)])
            P.add("dve", lambda e: e.tensor_tensor(out=s[:, 7:8], in0=s[:, 4:5], in1=epsq[:, h:h + 1], op=ALU.add),
                  reads=[("stv", r), "epsq"], writes=[("st2", r)])
            P.add("act", lambda e: e.activation(out=s[:, 5:6], in_=s[:, 7:8], func=AF.Sqrt), reads=[("st2", r)], writes=[("st3", r)])
            P.add("dve", lambda e: e.reciprocal(out=s[:, 6:7], in_=s[:, 5:6]), reads=[("st3", r)], writes=[("st4", r)])
            P.add("dve", lambda e: e.tensor_scalar(out=yn[r][:], in0=pO[:, 0:128], scalar1=s[:, 2:3], scalar2=s[:, 6:7],
                                                  op0=ALU.subtract, op1=ALU.mult),
                  reads=["pO", ("stm", r), ("st4", r)], writes=[("yn", r)])
            P.add("pool", lambda e: e.tensor_tensor(out=yn[r][:], in0=yn[r][:], in1=gng_sb[:, h * 128:(h + 1) * 128], op=ALU.mult),
                  reads=[("yn", r), "gng"], writes=[("yn", r)])
            P.add("pool", lambda e: e.tensor_tensor(out=ret_tm[r][:], in0=yn[r][:], in1=g_tm[r][:], op=ALU.mult),
                  reads=[("yn", r), ("g", r)], writes=[("ret", r)])
            transposes(P, [(pT2[:, 0:128], ret_tm[r][:])], ident[:], reads=[("ret", r), "ident"], writes=["pTr"])
            P.add("act", lambda e: e.activation(out=mixR[:, h, j * 128:(j + 1) * 128], in_=pT2[:, 0:128], func=AF.Copy),
                  reads=["pTr"], writes=[("mixinT", h, j)])

        stages = [S0, S1, S2, S3]
        n_it = len(iters)
        for t in range(n_it + len(stages) - 1):
            for si_ in reversed(range(len(stages))):
                n = t - si_
                if 0 <= n < n_it:
                    stages[si_](n)
        P.emit()


def layer_norm_ops(P, y, ykey, st, stkey, junk, g_sb, b_sb, gkeys, outs):
    P.add("act", lambda e: e.activation(out=junk[:], in_=y, func=AF.Copy, accum_out=st[:, 0:1]),
          reads=[ykey], writes=[(stkey, 0), "junk"])
    P.add("dve", lambda e: e.tensor_scalar(out=st[:, 1:2], in0=st[:, 0:1], scalar1=-1.0 / D, scalar2=None, op0=ALU.mult),
          reads=[(stkey, 0)], writes=[(stkey, 1)])
    P.add("dve", lambda e: e.tensor_scalar(out=y, in0=y, scalar1=st[:, 1:2], scalar2=None, op0=ALU.add),
          reads=[ykey, (stkey, 1)], writes=[ykey])
    P.add("act", lambda e: e.activation(out=junk[:], in_=y, func=AF.Square, accum_out=st[:, 2:3]),
          reads=[ykey], writes=[(stkey, 2), "junk"])
    rstd_ops(P, st[:, 4:5], st[:, 2:3], 1.0 / D, EPS, st[:, 3:4], reads=[(stkey, 2)], writes=[(stkey, 4)], key=stkey)
    P.add("dve", lambda e: e.scalar_tensor_tensor(out=y, in0=y, scalar=st[:, 4:5], in1=g_sb[:], op0=ALU.mult, op1=ALU.mult),
          reads=[ykey, (stkey, 4)] + gkeys, writes=[ykey])
    P.add("pool", lambda e: e.tensor_tensor(out=y, in0=y, in1=b_sb[:], op=ALU.add), reads=[ykey] + gkeys, writes=[ykey])


def phase_C(C, nc, sbt, pst, ident, w_outv, x_own, ln1g, ln1b, mixR, mixM, y1s, x1Tsv):
    with contextlib.ExitStack() as es:
        P = Prog(nc)
        Wo = sbt(es, "c_wo", [128, 16, D], BF16)
        g_sb = sbt(es, "c_g", [128, D], F32)
        b_sb = sbt(es, "c_b", [128, D], F32)
        NR = 2
        xo = [sbt(es, "c_xo%d" % i, [128, D], F32) for i in range(NR)]
        y = [sbt(es, "c_y%d" % i, [128, D], F32) for i in range(NR)]
        x1b = [sbt(es, "c_x1b%d" % i, [128, D], BF16) for i in range(NR)]
        x1T = [sbt(es, "c_x1T%d" % i, [128, 16, 128], BF16) for i in range(NR)]
        st = [sbt(es, "c_st%d" % i, [128, 8], F32) for i in range(NR)]
        junk = sbt(es, "c_junk", [128, D], BF16)
        pC = [pst(es, "c_pC%d" % i, [128, 512], F32) for i in range(4)]
        pT = [pst(es, "c_pT%d" % i, [128, 1024], BF16) for i in range(2)]
        for c in range(16):
            dma(P, "pool", Wo[:, c, :], w_outv[:, c, :], writes=[("Wo", c)])
        dma(P, "sp", g_sb[:], ln1g[0:1, :].broadcast_to([128, D]), writes=["lng"])
        dma(P, "sp", b_sb[:], ln1b[0:1, :].broadcast_to([128, D]), writes=["lnb"])
        wk = [("Wo", c) for c in range(16)]

        def S0(j):
            r = j % NR
            dma(P, "sp", xo[r][:], x_own[j * 128:(j + 1) * 128, :], writes=[("xo", r)])
            for cg in range(4):
                mm_group(P, pC[cg][:, 0:512],
                         [((mixR if c < 8 else mixM)[:, c % 8, j * 128:(j + 1) * 128], Wo[:, c, cg * 512:(cg + 1) * 512])
                          for c in range(16)],
                         reads=[("mixinT", c, j) for c in range(16)] + wk, writes=[("pC", cg)])
                P.add("dve", (lambda cg=cg: lambda e: e.scalar_tensor_tensor(
                    out=y[r][:, cg * 512:(cg + 1) * 512], in0=xo[r][:, cg * 512:(cg + 1) * 512], scalar=ALPHA,
                    in1=pC[cg][:, 0:512], op0=ALU.mult, op1=ALU.add))(),
                    reads=[("xo", r), ("pC", cg)], writes=[("y", r)])

        def S1(j):
            r = j % NR
            layer_norm_ops(P, y[r][:], ("y", r), st[r], ("c_st", r), junk, g_sb, b_sb, ["lng", "lnb"], None)
            P.add("act", lambda e: e.activation(out=x1b[r][:], in_=y[r][:], func=AF.Copy), reads=[("y", r)], writes=[("x1b", r)])
            P.add("pool", lambda e: e.tensor_scalar(out=y[r][:], in0=y[r][:], scalar1=ALPHA, scalar2=None, op0=ALU.mult),
                  reads=[("y", r), ("x1b", r)], writes=[("y", r)])
            dma(P, "sp", y1s[j * 128:(j + 1) * 128, :], y[r][:], reads=[("y", r)])

        def S2(j):
            r = j % NR
            for half in range(2):
                transposes(P, [(pT[half][:, c * 128:(c + 1) * 128], x1b[r][:, (half * 8 + c) * 128:(half * 8 + c + 1) * 128])
                               for c in range(8)], ident[:], reads=[("x1b", r), "ident"], writes=[("pT", half)])
                xout = x1T[r][:, half * 8:(half + 1) * 8, :]
                xin = pT[half][:, 0:1024].rearrange("p (c t) -> p c t", c=8)
                if half == 0:
                    P.add("act", (lambda xout=xout, xin=xin: lambda e: e.activation(out=xout, in_=xin, func=AF.Copy))(),
                          reads=[("pT", half)], writes=[("x1T", r, half)])
                else:
                    P.add("dve", (lambda xout=xout, xin=xin: lambda e: e.tensor_copy(out=xout, in_=xin))(),
                          reads=[("pT", half)], writes=[("x1T", r, half)])
            dma(P, "sp", x1Tsv[:, :, j * 128:(j + 1) * 128], x1T[r][:], reads=[("x1T", r, 0), ("x1T", r, 1)])

        stages = [S0, S1, S2]
        if not PIPE_C:
            for j in range(16):
                for st_ in stages:
                    st_(j)
        else:
            for t in range(16 + len(stages) - 1):
                for si_ in reversed(range(len(stages))):
                    n = t - si_
                    if 0 <= n < 16:
                        stages[si_](n)
        P.emit()


def phase_D(C, nc, sbt, pst, w_upv, w_downv, ln2g, ln2b, y1s, x1Tsv, out):
    with contextlib.ExitStack() as es:
        P = Prog(nc)
        x1Tg = sbt(es, "d_x1T", [128, 16, 1024], BF16)
        acc = sbt(es, "d_acc", [128, 8, D], F32)
        Wup = [sbt(es, "d_wup%d" % i, [128, 16, 512], BF16) for i in range(2)]
        Wdn = [sbt(es, "d_wdn%d" % i, [128, 4, D], BF16) for i in range(2)]
        hT = [sbt(es, "d_hT%d" % i, [128, 4, 1024], BF16) for i in range(2)]
        rt = [sbt(es, "d_rt%d" % i, [128, 512], F32) for i in range(2)]
        g_sb = sbt(es, "d_g", [128, D], F32)
        b_sb = sbt(es, "d_b", [128, D], F32)
        st = [sbt(es, "d_st%d" % i, [128, 8], F32) for i in range(2)]
        junk = sbt(es, "d_junk", [128, D], BF16)
        pU = [pst(es, "d_pU%d" % i, [128, 512], F32) for i in range(3)]
        pD = [pst(es, "d_pD%d" % i, [128, 512], F32) for i in range(4)]
        dma(P, "sp", g_sb[:], ln2g[0:1, :].broadcast_to([128, D]), writes=["lng"])
        dma(P, "sp", b_sb[:], ln2b[0:1, :].broadcast_to([128, D]), writes=["lnb"])
        wi = 0
        ui = 0
        di = 0

        def load_FW(step):
            f = step % 16
            b = step % 2
            for c in range(16):
                dma(P, "pool", Wup[b][:, c, :], w_upv[:, c, f * 512:(f + 1) * 512], writes=[("Wup", b, c)])
            for c in range(4):
                dma(P, "pool", Wdn[b][:, c, :], w_downv[:, f * 4 + c, :], writes=[("Wdn", b, c)])
        for grp in range(2):
            for c in range(16):
                dma(P, "sp", x1Tg[:, c, :], x1Tsv[:, c, grp * 1024:(grp + 1) * 1024], writes=[("x1T", c)])
            for tb in range(8):
                row0 = (grp * 8 + tb) * 128
                dma(P, "sp", acc[:, tb, :], y1s[row0:row0 + 128, :], writes=[("acc", tb)])
            xk = [("x1T", c) for c in range(16)]
            for fc in range(16):
                wb = wi % 2
                if wi == 0:
                    load_FW(0)
                if wi + 1 < 32:
                    load_FW(wi + 1)
                wi += 1
                wuk = [("Wup", wb, c) for c in range(16)]
                for fs in range(4):
                    for tg in range(2):
                        pu = pU[ui % 3]; puk = ("pU", ui % 3)
                        rr = ui % 2
                        ui += 1
                        mm_group(P, pu[:, 0:512],
                                 [(Wup[wb][:, c, fs * 128:(fs + 1) * 128], x1Tg[:, c, tg * 512:(tg + 1) * 512]) for c in range(16)],
                                 reads=xk + wuk, writes=[puk])
                        P.add("act", (lambda pu=pu, rr=rr: lambda e: e.activation(out=rt[rr][:], in_=pu[:, 0:512], func=AF.Relu))(),
                              reads=[puk], writes=[("rt", rr)])
                        P.add("act", (lambda rr=rr, wb=wb, fs=fs, tg=tg: lambda e: e.activation(
                            out=hT[wb][:, fs, tg * 512:(tg + 1) * 512], in_=rt[rr][:], func=AF.Square))(),
                            reads=[("rt", rr)], writes=[("hT", wb, fs, tg)])
                for tb in range(8):
                    tg = tb // 4
                    for cg in range(4):
                        pd = pD[di % 4]; pdk = ("pD", di % 4)
                        di += 1
                        mm_group(P, pd[:, 0:512],
                                 [(hT[wb][:, fs, tb * 128:(tb + 1) * 128], Wdn[wb][:, fs, cg * 512:(cg + 1) * 512]) for fs in range(4)],
                                 reads=[("hT", wb, fs, tg) for fs in range(4)] + [("Wdn", wb, fs) for fs in range(4)], writes=[pdk])
                        P.add("dve", (lambda pd=pd, tb=tb, cg=cg: lambda e: e.tensor_tensor(
                            out=acc[:, tb, cg * 512:(cg + 1) * 512], in0=acc[:, tb, cg * 512:(cg + 1) * 512], in1=pd[:, 0:512],
                            op=ALU.add))(),
                            reads=[pdk, ("acc", tb)], writes=[("acc", tb)])
            for tb in range(8):
                row0 = (grp * 8 + tb) * 128
                layer_norm_ops(P, acc[:, tb, :], ("acc", tb), st[tb % 2], ("d_st", tb % 2), junk, g_sb, b_sb, ["lng", "lnb"], None)
                dma(P, "sp", out[row0:row0 + 128, :], acc[:, tb, :], reads=[("acc", tb)])
        P.emit()


def _consts(p):
    c = {}
    c["c_ident"] = np.eye(128, dtype=np.float32)
    inv64 = (np.float32(10000.0) ** (-np.arange(0, 128, 2, dtype=np.float32) / np.float32(128))).astype(np.float32)
    inv32 = (np.float32(10000.0) ** (-np.arange(0, 64, 2, dtype=np.float32) / np.float32(64))).astype(np.float32)
    c["c_inv64"] = np.ascontiguousarray(np.broadcast_to(inv64[None, :], (128, 64)))
    c["c_inv32"] = np.ascontiguousarray(np.broadcast_to(inv32[None, :], (128, 32)))
    t = np.arange(128, dtype=np.float64)
    decA = np.zeros((128, 8)); decB = np.zeros((128, 8)); epsq = np.zeros((128, 8)); d2t = np.zeros((128, 8, 128))
    sc = 128.0 ** -0.5
    ii = t[None, :]
    jj = t[:, None]
    ci = np.floor(ii / 64); cj = np.floor(jj / 64)
    for h in range(8):
        lg = np.log1p(-2.0 ** (-5.0 - h))
        decA[:, h] = np.exp(lg * (255 - t)) * sc
        decB[:, h] = np.exp(lg * (127 - t)) * sc
        qdec = np.exp(lg * (t + 1))
        epsq[:, h] = EPS / qdec ** 2
        dd = np.where(cj == ci, np.exp(lg * np.abs(ii - jj)), np.where(cj < ci, np.exp(lg * (ii - jj)), 0.0))
        d2t[:, h, :] = dd * sc / qdec[None, :]
    c["c_decA"] = decA.astype(np.float32)
    c["c_decB"] = decB.astype(np.float32)
    c["c_epsq"] = epsq.astype(np.float32)
    c["c_d2t"] = d2t.reshape(128, 1024).astype(np.float32)
    dk = np.zeros((1, 128), np.float32); dk[0, 64:] = BIG
    dq = np.zeros((1, 512), np.float32); dq[0, :64] = -1.0
    c["c_dk"] = dk
    c["c_dq"] = dq
    return c


def _col_perm():
    cols = []
    for h in range(8):
        cols += list(range(1024 + h * 128, 1024 + (h + 1) * 128))
        cols += list(range(2048 + h * 128, 2048 + (h + 1) * 128))
        cols += list(range(0 + h * 128, (h + 1) * 128))
        cols += list(range(3072 + h * 128, 3072 + (h + 1) * 128))
    cols += list(range(4096 + 768, 4096 + 768 + 512))
    cols += list(range(4096 + 768 + 512, 5440))
    cols += list(range(4096, 4096 + 768))
    return np.array(cols)


def prep_inputs(inputs):
    x = np.asarray(inputs["x"], np.float32)
    pos = np.asarray(inputs["positions"], np.int32)
    w_in = np.ascontiguousarray(np.asarray(inputs["w_in"], np.float32)[0][:, _col_perm()])
    shared = {
        "w_in": w_in,
        "w_uq": np.ascontiguousarray(inputs["w_uq"][0], np.float32),
        "w_uk": np.ascontiguousarray(inputs["w_uk"][0], np.float32),
        "w_uv": np.ascontiguousarray(inputs["w_uv"][0], np.float32),
        "w_out": np.ascontiguousarray(inputs["w_out"][0], np.float32),
        "w_up": np.ascontiguousarray(inputs["w_up"][0], np.float32),
        "w_down": np.ascontiguousarray(inputs["w_down"][0], np.float32),
        "qg": np.ascontiguousarray(np.asarray(inputs["q_norm_g"][0], np.float32).reshape(6, 128).T),
        "kvg": np.ascontiguousarray(np.asarray(inputs["kv_norm_g"][0], np.float32).reshape(4, 128).T),
        "gng": np.asarray(inputs["ret_gn_g"][0], np.float32).reshape(1, 1024),
        "ln1g": np.asarray(inputs["ln1_g"][0], np.float32).reshape(1, D),
        "ln1b": np.asarray(inputs["ln1_b"][0], np.float32).reshape(1, D),
        "ln2g": np.asarray(inputs["ln2_g"][0], np.float32).reshape(1, D),
        "ln2b": np.asarray(inputs["ln2_b"][0], np.float32).reshape(1, D),
    }
    in_maps = []
    metas = []
    for core in range(8):
        b, p = core // 2, core % 2
        own_g = [2 * j + p for j in range(16)]
        oth_g = [(2 * j - 1) if p == 0 else (2 * j) for j in range(16)]
        blocks = [None] * 32
        for j in range(16):
            blocks[tb_own(j)] = own_g[j]
            blocks[tb_oth(j)] = oth_g[j]
        xb = x[b]
        xTc = np.zeros((D, SEQ), np.float32)
        pos_tm = np.zeros((128, 32), np.int32)
        km = np.zeros((128, 32), np.float32)
        for tb, g in enumerate(blocks):
            if g < 0:
                km[:, tb] = BIG
                continue
            xTc[:, tb * 128:(tb + 1) * 128] = xb[g * 128:(g + 1) * 128, :].T
            pos_tm[:, tb] = pos[b, g * 128:(g + 1) * 128]
        x_own = np.concatenate([xb[g * 128:(g + 1) * 128, :] for g in own_g], axis=0)
        m = dict(shared)
        m.update(_consts(p))
        m["xT"] = xTc
        m["x_own"] = np.ascontiguousarray(x_own)
        m["pos_tm"] = pos_tm
        m["kmask"] = km
        in_maps.append(m)
        metas.append((b, own_g))
    return in_maps, metas


def kernel(**inputs):
    in_maps, metas = prep_inputs(inputs)
    nc = build()
    res = run_bass_kernel_spmd(nc, in_maps, core_ids=list(range(8)))
    outp = np.zeros((4, SEQ, D), np.float32)
    for core in range(8):
        b, own_g = metas[core]
        o = np.asarray(res.results[core]["out"], np.float32)
        for j, g in enumerate(own_g):
            outp[b, g * 128:(g + 1) * 128, :] = o[j * 128:(j + 1) * 128, :]
    return outp
```

```python
import contextlib
import numpy as np
import ml_dtypes
import concourse.bass as bass
import concourse.mybir as mybir
from concourse.bass_utils import run_bass_kernel_spmd

F32 = mybir.dt.float32
BF16 = mybir.dt.bfloat16
I32 = mybir.dt.int32
AF = mybir.ActivationFunctionType
ALU = mybir.AluOpType

D = 2048
SEQ = 4096
NOWN = 2048
H = 8
DFF = 8192
EPS = 1e-5
ALPHA = 2.0 ** 0.25
IN_W = 5440
BIG = 30000.0
MAGIC = 12582912.0
TWO_PI = 6.283185307179586
C1 = 6.28125
C2 = TWO_PI - C1
SCALE = 192.0 ** -0.5
ENGS = ("pe", "act", "dve", "pool", "sp")
PIPE_B = 1
PIPE_C = False


def tb_own(j):
    return (j // 8) * 16 + j % 8


def tb_oth(j):
    return (j // 8) * 16 + 8 + j % 8


def _is_psum_key(k):
    n = k if isinstance(k, str) else k[0]
    return isinstance(n, str) and len(n) > 1 and n[0] == "p" and n[1].isupper()


class Op:
    __slots__ = ("eng", "fn", "reads", "writes", "dma", "deps", "idx", "sig", "dsem", "dval")


class Prog:
    sems = None

    def __init__(self, nc, same_engine_sync=True, dma_ring=8):
        self.nc = nc
        self.ops = []
        self.same_engine_sync = same_engine_sync
        self.dma_ring = dma_ring
        self.last_writer = {}
        self.readers = {}

    def add(self, eng, fn, reads=(), writes=(), dma=False):
        op = Op()
        ps_r = [k for k in reads if _is_psum_key(k)]
        if ps_r:
            reads = [k for k in reads if not _is_psum_key(k)]
            writes = list(writes) + [k for k in ps_r if k not in writes]
        op.eng, op.fn, op.reads, op.writes, op.dma = eng, fn, tuple(reads), tuple(writes), dma
        op.sig = None
        op.dsem = None
        op.dval = 0
        op.idx = len(self.ops)
        deps = set()
        for k in op.reads:
            w = self.last_writer.get(k)
            if w is not None:
                deps.add(w)
        for k in op.writes:
            w = self.last_writer.get(k)
            if w is not None:
                deps.add(w)
            for r in self.readers.get(k, {}).values():
                deps.add(r)
        deps.discard(op.idx)
        op.deps = sorted(deps)
        for k in op.reads:
            d = self.readers.setdefault(k, {})
            d[(eng, op.idx) if dma else eng] = op.idx
        for k in op.writes:
            self.last_writer[k] = op.idx
            self.readers[k] = {}
        self.ops.append(op)
        return op

    def emit(self):
        nc = self.nc
        ops = self.ops
        ses = self.same_engine_sync
        need = [False] * len(ops)
        for op in ops:
            for d in op.deps:
                dop = ops[d]
                if dop.dma:
                    continue
                if dop.eng == op.eng and not op.dma and (dop.eng == "pe" or not ses):
                    continue
                need[d] = True
        G = self.sems
        cnt = G["cnt"]
        dcount = G["dcount"]
        base_cnt = dict(cnt)
        base_d = dict(dcount)
        per_eng = {e: [] for e in ENGS}
        for op in ops:
            if op.dma:
                n = dcount[op.eng]
                dcount[op.eng] += 1
                op.dsem = n % self.dma_ring
                op.dval = 16 * (n // self.dma_ring + 1)
            elif need[op.idx]:
                cnt[op.eng] += 1
                op.sig = cnt[op.eng]
            per_eng[op.eng].append(op)
        with contextlib.ExitStack() as es:
            csem = G["csem"]
            dsem = G["dsem"]
            block = es.enter_context(nc.Block())

            def run_engine(e, handle):
                waited = {}
                for e2 in ENGS:
                    waited[("c", e2)] = base_cnt[e2]
                    n0 = base_d[e2]
                    for r in range(self.dma_ring):
                        uses = (n0 - 1 - r) // self.dma_ring + 1 if n0 > r else 0
                        waited[("d", e2, r)] = 16 * uses

                def wait(key, sem, val):
                    if waited.get(key, 0) >= val:
                        return
                    handle.wait_ge(sem, val)
                    waited[key] = val

                for op in per_eng[e]:
                    for d in op.deps:
                        dop = ops[d]
                        if dop.dma:
                            wait(("d", dop.eng, dop.dsem), dsem[dop.eng][dop.dsem], dop.dval)
                        else:
                            if dop.eng == e and not op.dma and (e == "pe" or not ses):
                                continue
                            wait(("c", dop.eng), csem[dop.eng], dop.sig)
                    if op.dma:
                        if op.dval > 16:
                            wait(("d", e, op.dsem), dsem[e][op.dsem], op.dval - 16)
                        ins = op.fn(handle)
                        ins.then_inc(dsem[e][op.dsem], 16)
                    else:
                        ins = op.fn(handle)
                        if op.sig is not None:
                            ins.then_inc(csem[e], 1)
                n = dcount[e]
                for r in range(min(n, self.dma_ring)):
                    uses = (n - 1 - r) // self.dma_ring + 1
                    wait(("d", e, r), dsem[e][r], 16 * uses)

            if per_eng["sp"]:
                @block.sync
                def _(h):
                    run_engine("sp", h)
            if per_eng["pool"]:
                @block.gpsimd
                def _(h):
                    run_engine("pool", h)
            if per_eng["act"]:
                @block.scalar
                def _(h):
                    run_engine("act", h)
            if per_eng["dve"]:
                @block.vector
                def _(h):
                    run_engine("dve", h)
            if per_eng["pe"]:
                @block.tensor
                def _(h):
                    run_engine("pe", h)


def dma(P, q, out, in_, reads=(), writes=()):
    P.add(q, lambda e: e.dma_start(out=out, in_=in_), reads=reads, writes=writes, dma=True)


def mm_group(P, out, pairs, reads, writes):
    def fn(e):
        n = len(pairs)
        ins = None
        for i, (l, r) in enumerate(pairs):
            ins = e.matmul(out, lhsT=l, rhs=r, start=(i == 0), stop=(i == n - 1))
        return ins
    P.add("pe", fn, reads=reads, writes=writes)


def transposes(P, items, ident, reads, writes):
    def fn(e):
        ins = None
        for o, i in items:
            ins = e.transpose(out=o, in_=i, identity=ident)
        return ins
    P.add("pe", fn, reads=reads, writes=writes)


def rope_ops(P, eng, dst, src, cos, sin, half, tmp, tmpkey, reads, writes):
    h = half

    def f1(e):
        e.tensor_tensor(out=tmp[:, 0:h], in0=src[:, 0:h], in1=cos, op=ALU.mult)
        e.tensor_tensor(out=tmp[:, h:2 * h], in0=src[:, h:2 * h], in1=sin, op=ALU.mult)
        e.tensor_tensor(out=tmp[:, 2 * h:3 * h], in0=src[:, 0:h], in1=sin, op=ALU.mult)
        return e.tensor_tensor(out=tmp[:, 3 * h:4 * h], in0=src[:, h:2 * h], in1=cos, op=ALU.mult)

    def f2(e):
        e.tensor_tensor(out=dst[:, 0:h], in0=tmp[:, 0:h], in1=tmp[:, h:2 * h], op=ALU.subtract)
        return e.tensor_tensor(out=dst[:, h:2 * h], in0=tmp[:, 2 * h:3 * h], in1=tmp[:, 3 * h:4 * h], op=ALU.add)
    P.add(eng, f1, reads=reads, writes=[tmpkey])
    P.add(eng, f2, reads=[tmpkey], writes=writes)


def rstd_ops(P, out, ssum, scale, eps_ap_or_float, tmp, reads, writes, key):
    def f1(e):
        if isinstance(eps_ap_or_float, float):
            return e.tensor_scalar(out=tmp, in0=ssum, scalar1=scale, scalar2=eps_ap_or_float, op0=ALU.mult, op1=ALU.add)
        return e.tensor_scalar(out=tmp, in0=ssum, scalar1=scale, scalar2=eps_ap_or_float, op0=ALU.mult, op1=ALU.add)
    k = ("rstd_tmp", key)
    P.add("dve", f1, reads=reads, writes=[k])
    P.add("act", lambda e: e.activation(out=tmp, in_=tmp, func=AF.Sqrt), reads=[k], writes=[k])
    P.add("dve", lambda e: e.reciprocal(out=out, in_=tmp), reads=[k], writes=writes)


class Ctx:
    pass


def build(debug=False, phases="0 A2 B A1 C D"):
    phases = phases.split()
    nc = bass.Bass("TRN2", target_bir_lowering=False)
    C = Ctx()
    C.nc = nc
    C.debug = debug

    def din(name, shape, dt=F32):
        return nc.dram_tensor(name, list(shape), dt, kind="ExternalInput").ap()

    xT = din("xT", [D, SEQ])
    x_own = din("x_own", [NOWN, D])
    pos_tm = din("pos_tm", [128, 32], I32)
    kmask = din("kmask", [128, 32])
    w_in = din("w_in", [D, IN_W])
    w_uq = din("w_uq", [768, 1536])
    w_uk = din("w_uk", [512, 1024])
    w_uv = din("w_uv", [512, 1024])
    w_out = din("w_out", [D, D])
    w_up = din("w_up", [D, DFF])
    w_down = din("w_down", [DFF, D])
    qg = din("qg", [128, 6])
    kvg = din("kvg", [128, 4])
    gng = din("gng", [1, 1024])
    ln1g = din("ln1g", [1, D])
    ln1b = din("ln1b", [1, D])
    ln2g = din("ln2g", [1, D])
    ln2b = din("ln2b", [1, D])
    c_ident = din("c_ident", [128, 128])
    c_inv64 = din("c_inv64", [128, 64])
    c_inv32 = din("c_inv32", [128, 32])
    c_decA = din("c_decA", [128, 8])
    c_decB = din("c_decB", [128, 8])
    c_epsq = din("c_epsq", [128, 8])
    c_d2t = din("c_d2t", [128, 8 * 128])
    c_dk = din("c_dk", [1, 128])
    c_dq = din("c_dq", [1, 512])
    out = nc.dram_tensor("out", [NOWN, D], F32, kind="ExternalOutput").ap()
    y1s = nc.dram_tensor("y1s", [NOWN, D], F32).ap()
    x1Ts = nc.dram_tensor("x1Ts", [D, NOWN], BF16).ap()
    dbg = {}
    if debug:
        dbg["mixR"] = nc.dram_tensor("dbg_mixR", [128, 8 * NOWN], F32, kind="ExternalOutput").ap()
        dbg["mixM"] = nc.dram_tensor("dbg_mixM", [128, 8 * NOWN], F32, kind="ExternalOutput").ap()
        dbg["cqnT"] = nc.dram_tensor("dbg_cqnT", [128, 6 * NOWN], F32, kind="ExternalOutput").ap()
        dbg["ckvnT"] = nc.dram_tensor("dbg_ckvnT", [128, 4 * SEQ], F32, kind="ExternalOutput").ap()
        dbg["krT"] = nc.dram_tensor("dbg_krT", [65, SEQ], F32, kind="ExternalOutput").ap()
        dbg["y1"] = y1s

    xTv = xT.rearrange("(c p) t -> p c t", p=128)
    w_inv = w_in.rearrange("(c p) n -> p c n", p=128)
    w_uqv = w_uq.rearrange("(c p) n -> p c n", p=128)
    w_ukv = w_uk.rearrange("(c p) n -> p c n", p=128)
    w_uvv = w_uv.rearrange("(c p) n -> p c n", p=128)
    w_outv = w_out.rearrange("(c p) n -> p c n", p=128)
    w_upv = w_up.rearrange("(c p) n -> p c n", p=128)
    w_downv = w_down.rearrange("(c p) n -> p c n", p=128)
    x1Tsv = x1Ts.rearrange("(c p) t -> p c t", p=128)

    def sbt(es, name, shape, dt):
        return es.enter_context(nc.sbuf_tensor(name, list(shape), dt))

    def pst(es, name, shape, dt):
        return es.enter_context(nc.psum_tensor(name, list(shape), dt))

    def dump(P, es, name, src_tile, ncols, src_key, dt=BF16, parts=128):
        stg = sbt(es, "dstg_" + name, [parts, 2048], F32)
        flat = src_tile
        for c0 in range(0, ncols, 2048):
            n = min(2048, ncols - c0)
            P.add("dve", (lambda c0=c0, n=n: lambda e: e.tensor_copy(out=stg[:, 0:n], in_=flat[:, c0:c0 + n]))(),
                  reads=[src_key], writes=[("dstg", name)])
            dma(P, "sp", dbg[name][:, c0:c0 + n], stg[:, 0:n], reads=[("dstg", name)])

    with contextlib.ExitStack() as g_es:
        Prog.sems = {
            "cnt": {e: 0 for e in ENGS}, "dcount": {e: 0 for e in ENGS},
            "csem": {e: g_es.enter_context(nc.semaphore("c_" + e)) for e in ENGS},
            "dsem": {e: [g_es.enter_context(nc.semaphore("d_%s_%d" % (e, i))) for i in range(8)] for e in ("sp", "pool")},
        }
        ident = sbt(g_es, "ident", [128, 128], BF16)
        with contextlib.ExitStack() as s1:
            mixM = sbt(s1, "mixM", [128, 8, NOWN], BF16)
            posf = sbt(s1, "posf", [128, 32], F32)
            kmask_sb = sbt(s1, "kmask_sb", [128, 32], F32)
            s2 = contextlib.ExitStack()
            cosM = sbt(s2, "cosM", [128, 32, 32], F32)
            sinM = sbt(s2, "sinM", [128, 32, 32], F32)

            def rope_tables(P, es, inv_dram, nf, cos_t, sin_t, tag):
                n = 32 * nf
                inv = sbt(es, "inv" + tag, [128, nf], F32)
                ang = sbt(es, "ang" + tag, [128, 32, nf], F32)
                kk = sbt(es, "kk" + tag, [128, 32, nf], F32)
                dma(P, "sp", inv[:], inv_dram[:, :], writes=["inv" + tag])
                angf = ang[:].rearrange("p a b -> p (a b)")
                kkf = kk[:].rearrange("p a b -> p (a b)")
                cosf = cos_t[:].rearrange("p a b -> p (a b)")
                sinf = sin_t[:].rearrange("p a b -> p (a b)")

                def f_ang(e):
                    ins = None
                    for tb in range(32):
                        ins = e.tensor_scalar(out=ang[:, tb, :], in0=inv[:], scalar1=posf[:, tb:tb + 1], scalar2=None,
                                              op0=ALU.mult)
                    return ins
                P.add("dve", f_ang, reads=["inv" + tag, "posf"], writes=["ang" + tag])
                ka = "ang" + tag
                kkk = "kk" + tag
                P.add("dve", lambda e: e.tensor_scalar(out=kkf, in0=angf, scalar1=1.0 / TWO_PI, scalar2=None, op0=ALU.mult),
                      reads=[ka], writes=[kkk])
                P.add("dve", lambda e: e.tensor_scalar(out=kkf, in0=kkf, scalar1=MAGIC, scalar2=None, op0=ALU.add),
                      reads=[kkk], writes=[kkk])
                P.add("dve", lambda e: e.tensor_scalar(out=kkf, in0=kkf, scalar1=-MAGIC, scalar2=None, op0=ALU.add),
                      reads=[kkk], writes=[kkk])
                P.add("dve", lambda e: e.scalar_tensor_tensor(out=angf, in0=kkf, scalar=-C1, in1=angf, op0=ALU.mult, op1=ALU.add),
                      reads=[kkk, ka], writes=[ka])
                P.add("dve", lambda e: e.scalar_tensor_tensor(out=angf, in0=kkf, scalar=-C2, in1=angf, op0=ALU.mult, op1=ALU.add),
                      reads=[kkk, ka], writes=[ka])
                P.add("dve", lambda e: e.tensor_scalar(out=angf, in0=angf, scalar1=3.1415925, scalar2=-3.1415925,
                                                      op0=ALU.min, op1=ALU.max), reads=[ka], writes=[ka])
                P.add("act", lambda e: e.activation(out=sinf, in_=angf, func=AF.Sin), reads=[ka], writes=["sin" + tag])
                P.add("act", lambda e: e.activation(out=kkf, in_=angf, func=AF.Sin, scale=0.5), reads=[ka], writes=[kkk])
                P.add("dve", lambda e: e.tensor_tensor(out=kkf, in0=kkf, in1=kkf, op=ALU.mult), reads=[kkk], writes=[kkk])
                P.add("dve", lambda e: e.tensor_scalar(out=cosf, in0=kkf, scalar1=-2.0, scalar2=1.0, op0=ALU.mult, op1=ALU.add),
                      reads=[kkk], writes=["cos" + tag])

            if "0" in phases:
                with contextlib.ExitStack() as es:
                    P = Prog(nc)
                    posi = sbt(es, "posi", [128, 32], I32)
                    dma(P, "pool", ident[:], c_ident[:, :], writes=["ident"])
                    dma(P, "sp", posi[:], pos_tm[:, :], writes=["posi"])
                    dma(P, "sp", kmask_sb[:], kmask[:, :], writes=["kmask"])
                    P.add("dve", lambda e: e.tensor_copy(out=posf[:], in_=posi[:]), reads=["posi"], writes=["posf"])
                    rope_tables(P, es, c_inv32, 32, cosM, sinM, "M")
                    P.emit()

            with s2:
                cqnT = sbt(s2, "cqnT", [128, 6, NOWN], BF16)
                ckvnT = sbt(s2, "ckvnT", [128, 4, SEQ], BF16)
                kraugT = sbt(s2, "kraugT", [65, SEQ], BF16)
                if "A2" in phases:
                    with nc.named_scope("A2"):
                        phase_A2(C, nc, sbt, pst, ident, xTv, w_inv, cosM, sinM, kmask_sb, cqnT, ckvnT, kraugT)
                    if debug:
                        with contextlib.ExitStack() as es:
                            P = Prog(nc)
                            dump(P, es, "cqnT", cqnT[:].rearrange("p a b -> p (a b)"), 6 * NOWN, "x")
                            dump(P, es, "ckvnT", ckvnT[:].rearrange("p a b -> p (a b)"), 4 * SEQ, "x")
                            stg = sbt(es, "dstg_kr", [65, SEQ], F32)
                            P.add("dve", lambda e: e.tensor_copy(out=stg[:], in_=kraugT[:]), writes=["stgkr"])
                            dma(P, "sp", dbg["krT"][:, :], stg[:], reads=["stgkr"])
                            P.emit()
                if "B" in phases:
                    with nc.named_scope("B"):
                        phase_B(C, nc, sbt, pst, ident, w_uqv, w_ukv, w_uvv, qg, kvg, c_dk, c_dq, cosM, sinM,
                                cqnT, ckvnT, kraugT, mixM)
            mixR = sbt(s1, "mixR", [128, 8, NOWN], BF16)
            if "A1" in phases:
                with nc.named_scope("A1"):
                    phase_A1(C, nc, sbt, pst, ident, xTv, w_inv, c_inv64, posf, c_decA, c_decB, c_epsq, c_d2t, gng,
                             mixR, rope_tables)
            if debug and ("A1" in phases or "B" in phases):
                with contextlib.ExitStack() as es:
                    P = Prog(nc)
                    if "A1" in phases:
                        dump(P, es, "mixR", mixR[:].rearrange("p a b -> p (a b)"), 8 * NOWN, "x")
                    if "B" in phases:
                        dump(P, es, "mixM", mixM[:].rearrange("p a b -> p (a b)"), 8 * NOWN, "x")
                    P.emit()
            if "C" in phases:
                with nc.named_scope("C"):
                    phase_C(C, nc, sbt, pst, ident, w_outv, x_own, ln1g, ln1b, mixR, mixM, y1s, x1Tsv)
        if "D" in phases:
            with nc.named_scope("D"):
                phase_D(C, nc, sbt, pst, w_upv, w_downv, ln2g, ln2b, y1s, x1Tsv, out)
    return nc


def phase_A2(C, nc, sbt, pst, ident, xTv, w_inv, cosM, sinM, kmask_sb, cqnT, ckvnT, kraugT):
    with contextlib.ExitStack() as es:
        P = Prog(nc)
        xTb = sbt(es, "a2_xT", [128, 16, 1024], BF16)
        Wg8 = sbt(es, "a2_w8", [128, 16, 512], BF16)
        Wg9 = sbt(es, "a2_w9", [128, 16, 448], BF16)
        Wg10 = sbt(es, "a2_w10", [128, 16, 384], BF16)
        junk = sbt(es, "a2_junk", [128, 512], BF16)
        NR = 2
        ss = [sbt(es, "a2_ss%d" % i, [128, 8], F32) for i in range(NR)]
        ckvn_tm = [sbt(es, "a2_ckvn%d" % i, [128, 512], BF16) for i in range(NR)]
        cq_raw = [sbt(es, "a2_cqraw%d" % i, [128, 768], F32) for i in range(NR)]
        cqn_tm = [sbt(es, "a2_cqn%d" % i, [128, 768], BF16) for i in range(NR)]
        kaug = [sbt(es, "a2_kaug%d" % i, [128, 65], BF16) for i in range(NR)]
        rtmp = [sbt(es, "a2_rtmp%d" % i, [128, 128], F32) for i in range(NR)]
        pA = [pst(es, "a2_pA%d" % i, [128, 512], F32) for i in range(3)]
        pT0 = pst(es, "a2_pT0", [128, 1024], BF16)
        pT1 = pst(es, "a2_pT1", [128, 1024], BF16)
        pT2 = pst(es, "a2_pT2", [128, 1024], BF16)
        for c in range(16):
            dma(P, "pool", Wg8[:, c, :], w_inv[:, c, 4096:4608], writes=[("w8", c)])
        for c in range(16):
            dma(P, "pool", Wg9[:, c, :], w_inv[:, c, 4608:5056], writes=[("w9", c)])
        for c in range(16):
            dma(P, "pool", Wg10[:, c, :], w_inv[:, c, 5056:5440], writes=[("w10", c)])
        it = 0
        pai = 0
        import os
        QL = int(os.environ.get("A2_Q", "4"))
        KL = int(os.environ.get("A2_K", "8"))
        for q in range(QL):
            for c in range(16):
                dma(P, "pool", xTb[:, c, :], xTv[:, c, q * 1024:(q + 1) * 1024], writes=[("xT", c)])
            own = (q % 2 == 0)
            for k in range(KL):
                tb = q * 8 + k
                j = (q // 2) * 8 + k
                r = it % NR
                it += 1
                xk = [("xT", c) for c in range(16)]
                pa = pA[pai % 3]; pak = ("pA", pai % 3); pai += 1
                mm_group(P, pa[:, 0:512], [(xTb[:, c, k * 128:(k + 1) * 128], Wg8[:, c, :]) for c in range(16)],
                         reads=xk + [("w8", c) for c in range(16)], writes=[pak])
                P.add("act", (lambda pa=pa, r=r: lambda e: e.activation(out=junk[:], in_=pa[:, 0:512], func=AF.Square,
                                                                     accum_out=ss[r][:, 0:1]))(),
                      reads=[pak], writes=[("ss0", r), "junk"])
                rstd_ops(P, ss[r][:, 4:5], ss[r][:, 0:1], 1.0 / 512, EPS, ss[r][:, 3:4], reads=[("ss0", r)], writes=[("rs0", r)], key=("a2a", r))
                P.add("dve", (lambda pa=pa, r=r: lambda e: e.tensor_scalar(out=ckvn_tm[r][:], in0=pa[:, 0:512],
                                                                         scalar1=ss[r][:, 4:5], scalar2=None, op0=ALU.mult))(),
                      reads=[pak, ("rs0", r)], writes=[("ckvn", r)])
                transposes(P, [(pT0[:, c * 128:(c + 1) * 128], ckvn_tm[r][:, c * 128:(c + 1) * 128]) for c in range(4)],
                           ident[:], reads=[("ckvn", r), "ident"], writes=["pT0a"])
                P.add("act", (lambda tb=tb: lambda e: e.activation(
                    out=ckvnT[:, 0:4, tb * 128:(tb + 1) * 128],
                    in_=pT0[:, 0:512].rearrange("p (c t) -> p c t", c=4), func=AF.Copy))(),
                    reads=["pT0a"], writes=[("ckvnT", tb)])
                ncol = 448 if own else 64
                pa = pA[pai % 3]; pak = ("pA", pai % 3); pai += 1
                mm_group(P, pa[:, 0:ncol], [(xTb[:, c, k * 128:(k + 1) * 128], Wg9[:, c, 0:ncol]) for c in range(16)],
                         reads=xk + [("w9", c) for c in range(16)], writes=[pak])
                rope_ops(P, "dve", kaug[r], pa, cosM[:, tb, :], sinM[:, tb, :], 32, rtmp[r], ("rtmp", r),
                         reads=[pak, "cosM", "sinM"], writes=[("kaug", r)])
                P.add("dve", (lambda r=r, tb=tb: lambda e: e.tensor_copy(out=kaug[r][:, 64:65], in_=kmask_sb[:, tb:tb + 1]))(),
                      reads=["kmask"], writes=[("kaug", r)])
                transposes(P, [(pT2[0:65, 0:128], kaug[r][:, 0:65])], ident[:], reads=[("kaug", r), "ident"], writes=["pT2"])
                P.add("act", (lambda tb=tb: lambda e: e.activation(out=kraugT[0:65, tb * 128:(tb + 1) * 128],
                                                                 in_=pT2[0:65, 0:128], func=AF.Copy))(),
                      reads=["pT2"], writes=[("kraugT", tb)])
                if own:
                    P.add("dve", (lambda pa=pa, r=r: lambda e: e.tensor_copy(out=cq_raw[r][:, 0:384], in_=pa[:, 64:448]))(),
                          reads=[pak], writes=[("cqraw0", r)])
                    P.add("act", (lambda pa=pa, r=r: lambda e: e.activation(out=junk[:, 0:384], in_=pa[:, 64:448], func=AF.Square,
                                                                         accum_out=ss[r][:, 1:2]))(),
                          reads=[pak], writes=[("ss1", r), "junk"])
                    pa = pA[pai % 3]; pak = ("pA", pai % 3); pai += 1
                    mm_group(P, pa[:, 0:384], [(xTb[:, c, k * 128:(k + 1) * 128], Wg10[:, c, :]) for c in range(16)],
                             reads=xk + [("w10", c) for c in range(16)], writes=[pak])
                    P.add("dve", (lambda pa=pa, r=r: lambda e: e.tensor_copy(out=cq_raw[r][:, 384:768], in_=pa[:, 0:384]))(),
                          reads=[pak], writes=[("cqraw1", r)])
                    P.add("act", (lambda pa=pa, r=r: lambda e: e.activation(out=junk[:, 0:384], in_=pa[:, 0:384], func=AF.Square,
                                                                         accum_out=ss[r][:, 2:3]))(),
                          reads=[pak], writes=[("ss2", r), "junk"])
                    P.add("dve", (lambda r=r: lambda e: e.tensor_tensor(out=ss[r][:, 5:6], in0=ss[r][:, 1:2], in1=ss[r][:, 2:3],
                                                                      op=ALU.add))(),
                          reads=[("ss1", r), ("ss2", r)], writes=[("ss12", r)])
                    rstd_ops(P, ss[r][:, 7:8], ss[r][:, 5:6], 1.0 / 768, EPS, ss[r][:, 6:7], reads=[("ss12", r)], writes=[("rs1", r)], key=("a2b", r))
                    P.add("dve", (lambda r=r: lambda e: e.tensor_scalar(out=cqn_tm[r][:], in0=cq_raw[r][:], scalar1=ss[r][:, 7:8],
                                                                      scalar2=None, op0=ALU.mult))(),
                          reads=[("cqraw0", r), ("cqraw1", r), ("rs1", r)], writes=[("cqn", r)])
                    transposes(P, [(pT1[:, c * 128:(c + 1) * 128], cqn_tm[r][:, c * 128:(c + 1) * 128]) for c in range(6)],
                               ident[:], reads=[("cqn", r), "ident"], writes=["pT1"])
                    P.add("act", (lambda j=j: lambda e: e.activation(
                        out=cqnT[:, 0:6, j * 128:(j + 1) * 128],
                        in_=pT1[:, 0:768].rearrange("p (c t) -> p c t", c=6), func=AF.Copy))(),
                        reads=["pT1"], writes=[("cqnT", j)])
        P.emit()


def phase_B(C, nc, sbt, pst, ident, w_uqv, w_ukv, w_uvv, qg, kvg, c_dk, c_dq, cosM, sinM, cqnT, ckvnT, kraugT, mixM):
    with contextlib.ExitStack() as es:
        P = Prog(nc)
        qg_sb = sbt(es, "b_qg", [128, 6], F32)
        kvg_sb = sbt(es, "b_kvg", [128, 4], F32)
        dk = sbt(es, "b_dk", [1, 128], BF16)
        dq = sbt(es, "b_dq", [1, 512], BF16)
        wq_f = sbt(es, "b_wqf", [128, 6, 192], F32)
        wk_f = sbt(es, "b_wkf", [128, 4, 128], F32)
        wv_f = sbt(es, "b_wvf", [128, 4, 128], F32)
        NH = 2
        wq_b = [sbt(es, "b_wqb%d" % i, [128, 6, 192], BF16) for i in range(NH)]
        wk_b = [sbt(es, "b_wkb%d" % i, [128, 4, 128], BF16) for i in range(NH)]
        wv_b = [sbt(es, "b_wvb%d" % i, [128, 4, 128], BF16) for i in range(NH)]
        knT = [sbt(es, "b_knT%d" % i, [128, SEQ], BF16) for i in range(NH)]
        Vaug = [sbt(es, "b_V%d" % i, [128, 32, 129], BF16) for i in range(NH)]
        qnT = [sbt(es, "b_qnT%d" % i, [128, NOWN], BF16) for i in range(NH)]
        qrT = [sbt(es, "b_qrT%d" % i, [65, NOWN], BF16) for i in range(NH)]
        qtm = [sbt(es, "b_qtm%d" % i, [128, 193], BF16) for i in range(2)]
        rtmp = [sbt(es, "b_rtmp%d" % i, [128, 128], F32) for i in range(2)]
        PT = [sbt(es, "b_PT%d" % i, [128, 512], BF16) for i in range(3)]
        otm = [sbt(es, "b_otm%d" % i, [128, 128], BF16) for i in range(2)]
        rinv = [sbt(es, "b_rinv%d" % i, [128, 1], F32) for i in range(2)]
        pP = pst(es, "b_pP", [128, 512], F32)
        pS = [pst(es, "b_pS%d" % i, [128, 512], F32) for i in range(2)]
        pO = [pst(es, "b_pO%d" % i, [128, 512], F32) for i in range(4)]
        pT = pst(es, "b_pT", [128, 1024], BF16)

        dma(P, "sp", qg_sb[:], qg[:, :], writes=["qg"])
        dma(P, "sp", kvg_sb[:], kvg[:, :], writes=["kvg"])
        dma(P, "pool", dk[:], c_dk[:, :], writes=["dk"])
        dma(P, "pool", dq[:], c_dq[:, :], writes=["dq"])
        for i in range(NH):
            P.add("pool", (lambda i=i: lambda e: e.memset(Vaug[i][:, :, 128:129], 1.0))(), writes=[("Vone", i)])
        for i in range(2):
            P.add("pool", (lambda i=i: lambda e: e.memset(qtm[i][:, 192:193], -1.0))(), writes=[("qtm1", i)])

        import os
        HL = int(os.environ.get("B_H", "8"))
        QL = int(os.environ.get("B_Q", "4"))
        cnt = {"si": 0, "pti": 0, "oi": 0, "qi": 0}

        def proj_steps(h):
            hb = h % NH
            steps = []

            def w_step():
                dma(P, "sp", wq_f[:], w_uqv[:, :, h * 192:(h + 1) * 192], writes=["wqf"])
                dma(P, "sp", wk_f[:], w_ukv[:, :, h * 128:(h + 1) * 128], writes=["wkf"])
                dma(P, "sp", wv_f[:], w_uvv[:, :, h * 128:(h + 1) * 128], writes=["wvf"])

                def f_wq(e):
                    ins = None
                    for c in range(6):
                        ins = e.tensor_scalar(out=wq_b[hb][:, c, :], in0=wq_f[:, c, :], scalar1=qg_sb[:, c:c + 1], scalar2=None,
                                              op0=ALU.mult)
                    return ins
                P.add("pool", f_wq, reads=["wqf", "qg"], writes=[("wqb", hb)])

                def f_wkv(e):
                    ins = None
                    for c in range(4):
                        e.tensor_scalar(out=wk_b[hb][:, c, :], in0=wk_f[:, c, :], scalar1=kvg_sb[:, c:c + 1], scalar2=None, op0=ALU.mult)
                        ins = e.tensor_scalar(out=wv_b[hb][:, c, :], in0=wv_f[:, c, :], scalar1=kvg_sb[:, c:c + 1], scalar2=None,
                                              op0=ALU.mult)
                    return ins
                P.add("pool", f_wkv, reads=["wkf", "wvf", "kvg"], writes=[("wkvb", hb)])
            steps.append(w_step)

            def q_step(j):
                def f():
                    r = cnt["qi"] % 2
                    cnt["qi"] += 1
                    mm_group(P, pP[:, 0:192], [(cqnT[:, c, j * 128:(j + 1) * 128], wq_b[hb][:, c, :]) for c in range(6)],
                             reads=[("cqnT", j), ("wqb", hb)], writes=["pP"])
                    P.add("act", lambda e: e.activation(out=qtm[r][:, 0:128], in_=pP[:, 0:128], func=AF.Copy),
                          reads=["pP"], writes=[("qtmA", r)])
                    tbq = tb_own(j)
                    rope_ops(P, "dve", qtm[r][:, 128:192], pP[:, 128:192], cosM[:, tbq, :], sinM[:, tbq, :], 32, rtmp[r], ("rtmp", r),
                             reads=["pP", "cosM", "sinM"], writes=[("qtmB", r)])
                    transposes(P, [(pT[:, 0:128], qtm[r][:, 0:128]), (pT[0:65, 128:256], qtm[r][:, 128:193])], ident[:],
                               reads=[("qtmA", r), ("qtmB", r), ("qtm1", r), "ident"], writes=["pT"])
                    P.add("dve", lambda e: e.tensor_copy(out=qnT[hb][:, j * 128:(j + 1) * 128], in_=pT[:, 0:128]),
                          reads=["pT"], writes=[("qnT", hb, j)])
                    P.add("act", lambda e: e.activation(out=qrT[hb][0:65, j * 128:(j + 1) * 128], in_=pT[0:65, 128:256], func=AF.Copy),
                          reads=["pT"], writes=[("qrT", hb, j)])
                return f

            def k_step(kg):
                def f():
                    mm_group(P, pP[:, 0:512], [(wk_b[hb][:, c, :], ckvnT[:, c, kg * 512:(kg + 1) * 512]) for c in range(4)],
                             reads=[("ckvnT", kg * 4 + i) for i in range(4)] + [("wkvb", hb)], writes=["pP"])
                    if kg % 2 == 0:
                        P.add("dve", lambda e: e.tensor_copy(out=knT[hb][:, kg * 512:(kg + 1) * 512], in_=pP[:, 0:512]),
                              reads=["pP"], writes=[("knT", hb, kg * 4 + i) for i in range(4)])
                    else:
                        P.add("act", lambda e: e.activation(out=knT[hb][:, kg * 512:(kg + 1) * 512], in_=pP[:, 0:512], func=AF.Copy),
                              reads=["pP"], writes=[("knT", hb, kg * 4 + i) for i in range(4)])
                return f

            def v_step(vg):
                def f():
                    def f_v(e):
                        ins = None
                        for i in range(4):
                            tb = vg * 4 + i
                            for c in range(4):
                                ins = e.matmul(pP[:, i * 128:(i + 1) * 128], lhsT=ckvnT[:, c, tb * 128:(tb + 1) * 128],
                                               rhs=wv_b[hb][:, c, :], start=(c == 0), stop=(c == 3))
                        return ins
                    P.add("pe", f_v, reads=[("ckvnT", vg * 4 + i) for i in range(4)] + [("wkvb", hb)], writes=["pP"])
                    vout = Vaug[hb][:, vg * 4:(vg + 1) * 4, 0:128]
                    vin = pP[:, 0:512].rearrange("p (a b) -> p a b", a=4)
                    if vg % 2 == 1:
                        P.add("dve", lambda e: e.tensor_copy(out=vout, in_=vin), reads=["pP"],
                              writes=[("V", hb, vg * 4 + i) for i in range(4)])
                    else:
                        P.add("act", lambda e: e.activation(out=vout, in_=vin, func=AF.Copy), reads=["pP"],
                              writes=[("V", hb, vg * 4 + i) for i in range(4)])
                return f
            for j in range(16):
                steps.append(q_step(j))
            for kg in range(8):
                steps.append(k_step(kg))
            for vg in range(8):
                steps.append(v_step(vg))
            return steps

        def attn_tiles(h):
            tiles = []
            for Q in range(QL):
                for i in range(4 * Q + 4):
                    for typ in (0, 1):
                        tb = tb_own(i) if typ == 0 else tb_oth(i)
                        if i < 4 * Q:
                            qoff, nq = 0, 4
                        else:
                            qoff, nq = i - 4 * Q, 4 * Q + 4 - i
                        tiles.append(dict(Q=Q, i=i, typ=typ, tb=tb, qoff=qoff, nq=nq, N=nq * 128, c0=(4 * Q + qoff) * 128,
                                          diag=(typ == 0 and i >= 4 * Q), lastQ=(i == 4 * Q + 3 and typ == 1)))
            return tiles

        def emit_S(h, T):
            hb = h % NH
            sb_ = cnt["si"] % 2
            cnt["si"] += 1
            pr = cnt["pti"] % 3
            cnt["pti"] += 1
            T["pr"] = pr
            tb, N, c0, diag = T["tb"], T["N"], T["c0"], T["diag"]

            def f_s(e):
                e.matmul(pS[sb_][:, 0:N], lhsT=knT[hb][:, tb * 128:(tb + 1) * 128], rhs=qnT[hb][:, c0:c0 + N],
                         start=True, stop=False)
                ins = e.matmul(pS[sb_][:, 0:N], lhsT=kraugT[0:65, tb * 128:(tb + 1) * 128],
                               rhs=qrT[hb][0:65, c0:c0 + N], start=False, stop=(not diag))
                if diag:
                    ins = e.matmul(pS[sb_][:, 0:N], lhsT=dk[0:1, :], rhs=dq[0:1, 0:N], start=False, stop=True)
                return ins
            qblocks = list(range(4 * T["Q"] + T["qoff"], 4 * T["Q"] + 4))
            P.add("pe", f_s, reads=[("knT", hb, tb), ("kraugT", tb), "dk", "dq"] +
                  [("qnT", hb, jj) for jj in qblocks] + [("qrT", hb, jj) for jj in qblocks],
                  writes=[("pS", sb_)])
            P.add("act", lambda e: e.activation(out=PT[pr][:, 0:N], in_=pS[sb_][:, 0:N], func=AF.Exp, scale=SCALE),
                  reads=[("pS", sb_)], writes=[("PT", pr)])

        def emit_PV(h, T):
            hb = h % NH
            pr, tb, nq, qoff, i, typ, Q = T["pr"], T["tb"], T["nq"], T["qoff"], T["i"], T["typ"], T["Q"]

            def f_pv(e):
                ins = None
                for a in range(nq):
                    jj = qoff + a
                    J = 4 * Q + jj
                    first = (i == 0 and typ == 0)
                    last = (i == J and typ == 1)
                    ins = e.matmul(pO[jj][:, 0:129], lhsT=PT[pr][:, a * 128:(a + 1) * 128], rhs=Vaug[hb][:, tb, :],
                                   start=first, stop=last)
                return ins
            P.add("pe", f_pv, reads=[("PT", pr), ("V", hb, tb), ("Vone", hb)], writes=[("pO", qoff + a) for a in range(nq)])

        def emit_fin(h, Q):
            for jj in range(4):
                J = 4 * Q + jj
                r = cnt["oi"] % 2
                cnt["oi"] += 1
                P.add("dve", (lambda jj=jj, r=r: lambda e: e.reciprocal(out=rinv[r][:], in_=pO[jj][:, 128:129]))(),
                      reads=[("pO", jj)], writes=[("rinv", r)])
                P.add("dve", (lambda jj=jj, r=r: lambda e: e.tensor_scalar(out=otm[r][:], in0=pO[jj][:, 0:128], scalar1=rinv[r][:],
                                                                         scalar2=None, op0=ALU.mult))(),
                      reads=[("pO", jj), ("rinv", r)], writes=[("otm", r)])
                transposes(P, [(pT[:, 256:384], otm[r][:])], ident[:], reads=[("otm", r), "ident"], writes=["pT"])
                P.add("act", (lambda J=J: lambda e: e.activation(out=mixM[:, h, J * 128:(J + 1) * 128], in_=pT[:, 256:384],
                                                               func=AF.Copy))(),
                      reads=["pT"], writes=[("mixinT", 8 + h, J)])

        if PIPE_B == 2:
            for st_ in proj_steps(0):
                st_()
        for h in range(HL):
            if PIPE_B == 1:
                for st_ in proj_steps(h):
                    st_()
                prev = None
                for T in attn_tiles(h):
                    emit_S(h, T)
                    if prev is not None:
                        emit_PV(h, prev)
                        if prev["lastQ"]:
                            emit_fin(h, prev["Q"])
                    prev = T
                emit_PV(h, prev)
                emit_fin(h, prev["Q"])
                continue
            if not PIPE_B:
                for st_ in proj_steps(h):
                    st_()
                for T in attn_tiles(h):
                    emit_S(h, T)
                    emit_PV(h, T)
                    if T["lastQ"]:
                        emit_fin(h, T["Q"])
                continue
            nxt = proj_steps(h + 1) if h + 1 < HL else []
            prev = None
            for t, T in enumerate(attn_tiles(h)):
                emit_S(h, T)
                if prev is not None:
                    emit_PV(h, prev)
                    if prev["lastQ"]:
                        emit_fin(h, prev["Q"])
                prev = T
                if nxt and t % 2 == 1:
                    nxt.pop(0)()
            emit_PV(h, prev)
            emit_fin(h, prev["Q"])
            while nxt:
                nxt.pop(0)()
        P.emit()


def phase_A1(C, nc, sbt, pst, ident, xTv, w_inv, c_inv64, posf, c_decA, c_decB, c_epsq, c_d2t, gng, mixR, rope_tables):
    with contextlib.ExitStack() as es:
        P = Prog(nc)
        cosR = sbt(es, "a1_cosR", [128, 32, 64], F32)
        sinR = sbt(es, "a1_sinR", [128, 32, 64], F32)
        with contextlib.ExitStack() as es0:
            P0 = Prog(nc)
            rope_tables(P0, es0, c_inv64, 64, cosR, sinR, "R")
            P0.emit()
        xTb = sbt(es, "a1_xT", [128, 16, 2048], BF16)
        Wg = [sbt(es, "a1_w%d" % i, [128, 16, 512], BF16) for i in range(2)]
        decA = sbt(es, "a1_decA", [128, 8], F32)
        decB = sbt(es, "a1_decB", [128, 8], F32)
        epsq = sbt(es, "a1_epsq", [128, 8], F32)
        d2t = sbt(es, "a1_d2t", [128, 8, 128], F32)
        gng_sb = sbt(es, "a1_gng", [128, 1024], F32)
        U32 = sbt(es, "a1_U", [128, 8, 128], F32)
        NR = 3
        T32 = [sbt(es, "a1_T%d" % i, [128, 128], F32) for i in range(NR)]
        Tb = [sbt(es, "a1_Tb%d" % i, [128, 128], BF16) for i in range(NR)]
        k_tm = [sbt(es, "a1_k%d" % i, [128, 128], BF16) for i in range(NR)]
        kd_tm = [sbt(es, "a1_kd%d" % i, [128, 128], BF16) for i in range(NR)]
        v_tm = [sbt(es, "a1_v%d" % i, [128, 128], BF16) for i in range(NR)]
        ko_tm = [sbt(es, "a1_ko%d" % i, [128, 128], BF16) for i in range(NR)]
        vo_tm = [sbt(es, "a1_vo%d" % i, [128, 128], BF16) for i in range(NR)]
        q_tm = [sbt(es, "a1_q%d" % i, [128, 128], BF16) for i in range(NR)]
        g_tm = [sbt(es, "a1_g%d" % i, [128, 128], F32) for i in range(NR)]
        qT = [sbt(es, "a1_qT%d" % i, [128, 128], BF16) for i in range(NR)]
        kT = [sbt(es, "a1_kT%d" % i, [128, 128], BF16) for i in range(NR)]
        Pm = [sbt(es, "a1_P%d" % i, [128, 128], BF16) for i in range(NR)]
        NT_ = 4
        rtmp = [sbt(es, "a1_rtmp%d" % i, [128, 384], F32) for i in range(NT_)]
        st = [sbt(es, "a1_st%d" % i, [128, 8], F32) for i in range(NR)]
        yn = [sbt(es, "a1_yn%d" % i, [128, 128], F32) for i in range(NR)]
        ret_tm = [sbt(es, "a1_ret%d" % i, [128, 128], BF16) for i in range(NR)]
        junk = sbt(es, "a1_junk", [128, 128], BF16)
        pA = [pst(es, "a1_pA%d" % i, [128, 512], F32) for i in range(3)]
        pS = pst(es, "a1_pS", [128, 512], F32)
        pO = pst(es, "a1_pO", [128, 512], F32)
        pKV = pst(es, "a1_pKV", [128, 512], F32)
        pT = pst(es, "a1_pT", [128, 1024], BF16)
        pT2 = pst(es, "a1_pT2", [128, 1024], BF16)

        dma(P, "sp", decA[:], c_decA[:, :], writes=["decA"])
        dma(P, "sp", decB[:], c_decB[:, :], writes=["decB"])
        dma(P, "sp", epsq[:], c_epsq[:, :], writes=["epsq"])
        dma(P, "sp", d2t[:].rearrange("p a b -> p (a b)"), c_d2t[:, :], writes=["d2t"])
        dma(P, "sp", gng_sb[:], gng[0:1, :].broadcast_to([128, 1024]), writes=["gng"])
        P.add("pool", lambda e: e.memset(U32[:].rearrange("p a b -> p (a b)"), 0.0), writes=[("U", h) for h in range(8)])
        g256 = [float(np.exp(256.0 * np.log1p(-2.0 ** (-5.0 - h)))) for h in range(8)]
        xk = [("xT", c) for c in range(16)]
        cnt = {"pai": 0, "rti": 0}

        def load_W(step):
            hh = step % 8
            for c in range(16):
                dma(P, "pool", Wg[step % 2][:, c, :], w_inv[:, c, hh * 512:(hh + 1) * 512], writes=[("W", step % 2, c)])

        iters = [(ps_, h, k) for ps_ in range(2) for h in range(8) for k in range(8)]
        ST = {}

        def S0(n):
            ps_, h, k = iters[n]
            step = ps_ * 8 + h
            wb = step % 2
            if k == 0:
                if h == 0:
                    for c in range(16):
                        dma(P, "pool", xTb[:, c, :], xTv[:, c, ps_ * 2048:(ps_ + 1) * 2048], writes=[("xT", c)])
                if step == 0:
                    load_W(0)
                if step + 1 < 16:
                    load_W(step + 1)
            wk = [("W", wb, c) for c in range(16)]
            ko, kw = 8 + k, k
            a0 = cnt["pai"] % 3; cnt["pai"] += 1
            a1 = cnt["pai"] % 3; cnt["pai"] += 1
            ST[n] = (a0, a1)
            mm_group(P, pA[a0][:, 0:256], [(xTb[:, c, ko * 128:(ko + 1) * 128], Wg[wb][:, c, 0:256]) for c in range(16)],
                     reads=xk + wk, writes=[("pA", a0)])
            mm_group(P, pA[a1][:, 0:512], [(xTb[:, c, kw * 128:(kw + 1) * 128], Wg[wb][:, c, :]) for c in range(16)],
                     reads=xk + wk, writes=[("pA", a1)])

        def S1(n):
            ps_, h, k = iters[n]
            j = ps_ * 8 + k
            r = n % NR
            a0, a1 = ST[n]
            tbo, tbw = tb_oth(j), tb_own(j)
            pa = pA[a0]; pak = ("pA", a0)
            ri = cnt["rti"] % NT_; cnt["rti"] += 1
            rt = rtmp[ri]; rtk = ("rtmp", ri)
            rope_ops(P, "dve", rt[:, 0:128], pa[:, 0:128], cosR[:, tbo, :], sinR[:, tbo, :], 64, rt[:, 128:384], (rtk, "t"),
                     reads=[pak, "cosR", "sinR"], writes=[rtk])
            P.add("pool", lambda e: e.tensor_scalar(out=ko_tm[r][:], in0=rt[:, 0:128], scalar1=decB[:, h:h + 1], scalar2=None,
                                                   op0=ALU.mult), reads=[rtk, "decB"], writes=[("ko", r)])
            P.add("act", lambda e: e.activation(out=vo_tm[r][:], in_=pa[:, 128:256], func=AF.Copy), reads=[pak], writes=[("vo", r)])
            pb = pA[a1]; pbk = ("pA", a1)
            ri = cnt["rti"] % NT_; cnt["rti"] += 1
            rt2 = rtmp[ri]; rtk2 = ("rtmp", ri)
            rope_ops(P, "dve", k_tm[r][:], pb[:, 0:128], cosR[:, tbw, :], sinR[:, tbw, :], 64, rt2[:], rtk2,
                     reads=[pbk, "cosR", "sinR"], writes=[("k", r)])
            ri = cnt["rti"] % NT_; cnt["rti"] += 1
            rt3 = rtmp[ri]; rtk3 = ("rtmp", ri)
            rope_ops(P, "dve", q_tm[r][:], pb[:, 256:384], cosR[:, tbw, :], sinR[:, tbw, :], 64, rt3[:], rtk3,
                     reads=[pbk, "cosR", "sinR"], writes=[("q", r)])
            P.add("act", lambda e: e.activation(out=v_tm[r][:], in_=pb[:, 128:256], func=AF.Copy), reads=[pbk], writes=[("v", r)])
            P.add("act", lambda e: e.activation(out=g_tm[r][:], in_=pb[:, 384:512], func=AF.Silu), reads=[pbk], writes=[("g", r)])
            P.add("pool", lambda e: e.tensor_scalar(out=kd_tm[r][:], in0=k_tm[r][:], scalar1=decA[:, h:h + 1], scalar2=None,
                                                   op0=ALU.mult), reads=[("k", r), "decA"], writes=[("kd", r)])

        def S2(n):
            ps_, h, k = iters[n]
            r = n % NR
            mm_group(P, pKV[:, 0:128], [(ko_tm[r][:], vo_tm[r][:])], reads=[("ko", r), ("vo", r)], writes=["pKV"])
            P.add("dve", lambda e: e.tensor_tensor(out=T32[r][:], in0=U32[:, h, :], in1=pKV[:, 0:128], op=ALU.add),
                  reads=["pKV", ("U", h)], writes=[("T32", r)])
            P.add("pool", lambda e: e.tensor_copy(out=Tb[r][:], in_=T32[r][:]), reads=[("T32", r)], writes=[("Tb", r)])
            mm_group(P, pKV[:, 128:256], [(kd_tm[r][:], v_tm[r][:])], reads=[("kd", r), ("v", r)], writes=["pKV"])
            P.add("dve", lambda e: e.scalar_tensor_tensor(out=U32[:, h, :], in0=T32[r][:], scalar=g256[h], in1=pKV[:, 128:256],
                                                         op0=ALU.mult, op1=ALU.add),
                  reads=["pKV", ("T32", r)], writes=[("U", h)])
            transposes(P, [(pT[:, 0:128], q_tm[r][:]), (pT[:, 128:256], k_tm[r][:])], ident[:],
                       reads=[("q", r), ("k", r), "ident"], writes=["pTqk"])
            P.add("act", lambda e: e.activation(out=qT[r][:], in_=pT[:, 0:128], func=AF.Copy), reads=["pTqk"], writes=[("qT", r)])
            P.add("dve", lambda e: e.tensor_copy(out=kT[r][:], in_=pT[:, 128:256]), reads=["pTqk"], writes=[("kT", r)])
            mm_group(P, pS[:, 0:128], [(kT[r][:], qT[r][:])], reads=[("kT", r), ("qT", r)], writes=["pS"])
            P.add("dve", lambda e: e.tensor_tensor(out=Pm[r][:], in0=pS[:, 0:128], in1=d2t[:, h, :], op=ALU.mult),
                  reads=["pS", "d2t"], writes=[("Pm", r)])

        def S3(n):
            ps_, h, k = iters[n]
            j = ps_ * 8 + k
            r = n % NR
            mm_group(P, pO[:, 0:128], [(Pm[r][:], v_tm[r][:]), (qT[r][:], Tb[r][:])],
                     reads=[("Pm", r), ("v", r), ("qT", r), ("Tb", r)], writes=["pO"])
            s = st[r]
            P.add("act", lambda e: e.activation(out=junk[:], in_=pO[:, 0:128], func=AF.Copy, accum_out=s[:, 0:1]),
                  reads=["pO"], writes=[("st0", r), "junk"])
            P.add("act", lambda e: e.activation(out=junk[:], in_=pO[:, 0:128], func=AF.Square, accum_out=s[:, 1:2]),
                  reads=["pO"], writes=[("st1", r), "junk"])
            P.add("dve", lambda e: e.tensor_scalar(out=s[:, 2:3], in0=s[:, 0:1], scalar1=1.0 / 128, scalar2=None, op0=ALU.mult),
                  reads=[("st0", r)], writes=[("stm", r)])
            P.add("dve", lambda e: e.tensor_tensor(out=s[:, 3:4], in0=s[:, 2:3], in1=s[:, 2:3], op=ALU.mult),
                  reads=[("stm", r)], writes=[("stm2", r)])
            P.add("dve", lambda e: e.scalar_tensor_tensor(out=s[:, 4:5], in0=s[:, 1:2], scalar=1.0 / 128, in1=s[:, 3:4],
                                                         op0=ALU.mult, op1=ALU.subtract),
                  reads=[("st1", r), ("stm2", r)], writes=[("stv", r)])
            P.add("dve", lambda e: e.tensor_tensor(out=s[:, 7:8], in0=s[:, 4:5], in1=epsq[:, h:h + 1], op=ALU.add),
                  reads=[("stv", r), "epsq"], writes=[("st2", r)])
            P.add("act", lambda e: e.activation(out=s[:, 5:6], in_=s[:, 7:8], func=AF.Sqrt), reads=[("st2", r)], writes=[("st3", r)])
            P.add("dve", lambda e: e.reciprocal(out=s[:, 6:7], in_=s[:, 5:6]), reads=[("st3", r)], writes=[("st4", r)])
            P.add("dve", lambda e: e.tensor_scalar(out=yn[r][:], in0=pO[:, 0:128], scalar1=s[:, 2:3], scalar2=s[:, 6:7],
                                                  op0=ALU.subtract, op1=ALU.mult),
                  reads=["pO", ("stm", r), ("st4", r)], writes=[("yn", r)])
            P.add("pool", lambda e: e.tensor_tensor(out=yn[r][:], in0=yn[r][:], in1=gng_sb[:, h * 128:(h + 1) * 128], op=ALU.mult),
                  reads=[("yn", r), "gng"], writes=[("yn", r)])
            P.add("pool", lambda e: e.tensor_tensor(out=ret_tm[r][:], in0=yn[r][:], in1=g_tm[r][:], op=ALU.mult),
                  reads=[("yn", r), ("g", r)], writes=[("ret", r)])
            transposes(P, [(pT2[:, 0:128], ret_tm[r][:])], ident[:], reads=[("ret", r), "ident"], writes=["pTr"])
            P.add("act", lambda e: e.activation(out=mixR[:, h, j * 128:(j + 1) * 128], in_=pT2[:, 0:128], func=AF.Copy),
                  reads=["pTr"], writes=[("mixinT", h, j)])

        stages = [S0, S1, S2, S3]
        n_it = len(iters)
        for t in range(n_it + len(stages) - 1):
            for si_ in reversed(range(len(stages))):
                n = t - si_
                if 0 <= n < n_it:
                    stages[si_](n)
        P.emit()


def layer_norm_ops(P, y, ykey, st, stkey, junk, g_sb, b_sb, gkeys, outs):
    P.add("act", lambda e: e.activation(out=junk[:], in_=y, func=AF.Copy, accum_out=st[:, 0:1]),
          reads=[ykey], writes=[(stkey, 0), "junk"])
    P.add("dve", lambda e: e.tensor_scalar(out=st[:, 1:2], in0=st[:, 0:1], scalar1=-1.0 / D, scalar2=None, op0=ALU.mult),
          reads=[(stkey, 0)], writes=[(stkey, 1)])
    P.add("dve", lambda e: e.tensor_scalar(out=y, in0=y, scalar1=st[:, 1:2], scalar2=None, op0=ALU.add),
          reads=[ykey, (stkey, 1)], writes=[ykey])
    P.add("act", lambda e: e.activation(out=junk[:], in_=y, func=AF.Square, accum_out=st[:, 2:3]),
          reads=[ykey], writes=[(stkey, 2), "junk"])
    rstd_ops(P, st[:, 4:5], st[:, 2:3], 1.0 / D, EPS, st[:, 3:4], reads=[(stkey, 2)], writes=[(stkey, 4)], key=stkey)
    P.add("dve", lambda e: e.scalar_tensor_tensor(out=y, in0=y, scalar=st[:, 4:5], in1=g_sb[:], op0=ALU.mult, op1=ALU.mult),
          reads=[ykey, (stkey, 4)] + gkeys, writes=[ykey])
    P.add("pool", lambda e: e.tensor_tensor(out=y, in0=y, in1=b_sb[:], op=ALU.add), reads=[ykey] + gkeys, writes=[ykey])


def phase_C(C, nc, sbt, pst, ident, w_outv, x_own, ln1g, ln1b, mixR, mixM, y1s, x1Tsv):
    with contextlib.ExitStack() as es:
        P = Prog(nc)
        Wo = sbt(es, "c_wo", [128, 16, D], BF16)
        g_sb = sbt(es, "c_g", [128, D], F32)
        b_sb = sbt(es, "c_b", [128, D], F32)
        NR = 2
        xo = [sbt(es, "c_xo%d" % i, [128, D], F32) for i in range(NR)]
        y = [sbt(es, "c_y%d" % i, [128, D], F32) for i in range(NR)]
        x1b = [sbt(es, "c_x1b%d" % i, [128, D], BF16) for i in range(NR)]
        x1T = [sbt(es, "c_x1T%d" % i, [128, 16, 128], BF16) for i in range(NR)]
        st = [sbt(es, "c_st%d" % i, [128, 8], F32) for i in range(NR)]
        junk = sbt(es, "c_junk", [128, D], BF16)
        pC = [pst(es, "c_pC%d" % i, [128, 512], F32) for i in range(4)]
        pT = [pst(es, "c_pT%d" % i, [128, 1024], BF16) for i in range(2)]
        for c in range(16):
            dma(P, "pool", Wo[:, c, :], w_outv[:, c, :], writes=[("Wo", c)])
        dma(P, "sp", g_sb[:], ln1g[0:1, :].broadcast_to([128, D]), writes=["lng"])
        dma(P, "sp", b_sb[:], ln1b[0:1, :].broadcast_to([128, D]), writes=["lnb"])
        wk = [("Wo", c) for c in range(16)]

        def S0(j):
            r = j % NR
            dma(P, "sp", xo[r][:], x_own[j * 128:(j + 1) * 128, :], writes=[("xo", r)])
            for cg in range(4):
                mm_group(P, pC[cg][:, 0:512],
                         [((mixR if c < 8 else mixM)[:, c % 8, j * 128:(j + 1) * 128], Wo[:, c, cg * 512:(cg + 1) * 512])
                          for c in range(16)],
                         reads=[("mixinT", c, j) for c in range(16)] + wk, writes=[("pC", cg)])
                P.add("dve", (lambda cg=cg: lambda e: e.scalar_tensor_tensor(
                    out=y[r][:, cg * 512:(cg + 1) * 512], in0=xo[r][:, cg * 512:(cg + 1) * 512], scalar=ALPHA,
                    in1=pC[cg][:, 0:512], op0=ALU.mult, op1=ALU.add))(),
                    reads=[("xo", r), ("pC", cg)], writes=[("y", r)])

        def S1(j):
            r = j % NR
            layer_norm_ops(P, y[r][:], ("y", r), st[r], ("c_st", r), junk, g_sb, b_sb, ["lng", "lnb"], None)
            P.add("act", lambda e: e.activation(out=x1b[r][:], in_=y[r][:], func=AF.Copy), reads=[("y", r)], writes=[("x1b", r)])
            P.add("pool", lambda e: e.tensor_scalar(out=y[r][:], in0=y[r][:], scalar1=ALPHA, scalar2=None, op0=ALU.mult),
                  reads=[("y", r), ("x1b", r)], writes=[("y", r)])
            dma(P, "sp", y1s[j * 128:(j + 1) * 128, :], y[r][:], reads=[("y", r)])

        def S2(j):
            r = j % NR
            for half in range(2):
                transposes(P, [(pT[half][:, c * 128:(c + 1) * 128], x1b[r][:, (half * 8 + c) * 128:(half * 8 + c + 1) * 128])
                               for c in range(8)], ident[:], reads=[("x1b", r), "ident"], writes=[("pT", half)])
                xout = x1T[r][:, half * 8:(half + 1) * 8, :]
                xin = pT[half][:, 0:1024].rearrange("p (c t) -> p c t", c=8)
                if half == 0:
                    P.add("act", (lambda xout=xout, xin=xin: lambda e: e.activation(out=xout, in_=xin, func=AF.Copy))(),
                          reads=[("pT", half)], writes=[("x1T", r, half)])
                else:
                    P.add("dve", (lambda xout=xout, xin=xin: lambda e: e.tensor_copy(out=xout, in_=xin))(),
                          reads=[("pT", half)], writes=[("x1T", r, half)])
            dma(P, "sp", x1Tsv[:, :, j * 128:(j + 1) * 128], x1T[r][:], reads=[("x1T", r, 0), ("x1T", r, 1)])

        stages = [S0, S1, S2]
        if not PIPE_C:
            for j in range(16):
                for st_ in stages:
                    st_(j)
        else:
            for t in range(16 + len(stages) - 1):
                for si_ in reversed(range(len(stages))):
                    n = t - si_
                    if 0 <= n < 16:
                        stages[si_](n)
        P.emit()


def phase_D(C, nc, sbt, pst, w_upv, w_downv, ln2g, ln2b, y1s, x1Tsv, out):
    with contextlib.ExitStack() as es:
        P = Prog(nc)
        x1Tg = sbt(es, "d_x1T", [128, 16, 1024], BF16)
        acc = sbt(es, "d_acc", [128, 8, D], F32)
        Wup = [sbt(es, "d_wup%d" % i, [128, 16, 512], BF16) for i in range(2)]
        Wdn = [sbt(es, "d_wdn%d" % i, [128, 4, D], BF16) for i in range(2)]
        hT = [sbt(es, "d_hT%d" % i, [128, 4, 1024], BF16) for i in range(2)]
        rt = [sbt(es, "d_rt%d" % i, [128, 512], F32) for i in range(2)]
        g_sb = sbt(es, "d_g", [128, D], F32)
        b_sb = sbt(es, "d_b", [128, D], F32)
        st = [sbt(es, "d_st%d" % i, [128, 8], F32) for i in range(2)]
        junk = sbt(es, "d_junk", [128, D], BF16)
        pU = [pst(es, "d_pU%d" % i, [128, 512], F32) for i in range(3)]
        pD = [pst(es, "d_pD%d" % i, [128, 512], F32) for i in range(4)]
        dma(P, "sp", g_sb[:], ln2g[0:1, :].broadcast_to([128, D]), writes=["lng"])
        dma(P, "sp", b_sb[:], ln2b[0:1, :].broadcast_to([128, D]), writes=["lnb"])
        wi = 0
        ui = 0
        di = 0

        def load_FW(step):
            f = step % 16
            b = step % 2
            for c in range(16):
                dma(P, "pool", Wup[b][:, c, :], w_upv[:, c, f * 512:(f + 1) * 512], writes=[("Wup", b, c)])
            for c in range(4):
                dma(P, "pool", Wdn[b][:, c, :], w_downv[:, f * 4 + c, :], writes=[("Wdn", b, c)])
        for grp in range(2):
            for c in range(16):
                dma(P, "sp", x1Tg[:, c, :], x1Tsv[:, c, grp * 1024:(grp + 1) * 1024], writes=[("x1T", c)])
            for tb in range(8):
                row0 = (grp * 8 + tb) * 128
                dma(P, "sp", acc[:, tb, :], y1s[row0:row0 + 128, :], writes=[("acc", tb)])
            xk = [("x1T", c) for c in range(16)]
            for fc in range(16):
                wb = wi % 2
                if wi == 0:
                    load_FW(0)
                if wi + 1 < 32:
                    load_FW(wi + 1)
                wi += 1
                wuk = [("Wup", wb, c) for c in range(16)]
                for fs in range(4):
                    for tg in range(2):
                        pu = pU[ui % 3]; puk = ("pU", ui % 3)
                        rr = ui % 2
                        ui += 1
                        mm_group(P, pu[:, 0:512],
                                 [(Wup[wb][:, c, fs * 128:(fs + 1) * 128], x1Tg[:, c, tg * 512:(tg + 1) * 512]) for c in range(16)],
                                 reads=xk + wuk, writes=[puk])
                        P.add("act", (lambda pu=pu, rr=rr: lambda e: e.activation(out=rt[rr][:], in_=pu[:, 0:512], func=AF.Relu))(),
                              reads=[puk], writes=[("rt", rr)])
                        P.add("act", (lambda rr=rr, wb=wb, fs=fs, tg=tg: lambda e: e.activation(
                            out=hT[wb][:, fs, tg * 512:(tg + 1) * 512], in_=rt[rr][:], func=AF.Square))(),
                            reads=[("rt", rr)], writes=[("hT", wb, fs, tg)])
                for tb in range(8):
                    tg = tb // 4
                    for cg in range(4):
                        pd = pD[di % 4]; pdk = ("pD", di % 4)
                        di += 1
                        mm_group(P, pd[:, 0:512],
                                 [(hT[wb][:, fs, tb * 128:(tb + 1) * 128], Wdn[wb][:, fs, cg * 512:(cg + 1) * 512]) for fs in range(4)],
                                 reads=[("hT", wb, fs, tg) for fs in range(4)] + [("Wdn", wb, fs) for fs in range(4)], writes=[pdk])
                        P.add("dve", (lambda pd=pd, tb=tb, cg=cg: lambda e: e.tensor_tensor(
                            out=acc[:, tb, cg * 512:(cg + 1) * 512], in0=acc[:, tb, cg * 512:(cg + 1) * 512], in1=pd[:, 0:512],
                            op=ALU.add))(),
                            reads=[pdk, ("acc", tb)], writes=[("acc", tb)])
            for tb in range(8):
                row0 = (grp * 8 + tb) * 128
                layer_norm_ops(P, acc[:, tb, :], ("acc", tb), st[tb % 2], ("d_st", tb % 2), junk, g_sb, b_sb, ["lng", "lnb"], None)
                dma(P, "sp", out[row0:row0 + 128, :], acc[:, tb, :], reads=[("acc", tb)])
        P.emit()


def _consts(p):
    c = {}
    c["c_ident"] = np.eye(128, dtype=np.float32)
    inv64 = (np.float32(10000.0) ** (-np.arange(0, 128, 2, dtype=np.float32) / np.float32(128))).astype(np.float32)
    inv32 = (np.float32(10000.0) ** (-np.arange(0, 64, 2, dtype=np.float32) / np.float32(64))).astype(np.float32)
    c["c_inv64"] = np.ascontiguousarray(np.broadcast_to(inv64[None, :], (128, 64)))
    c["c_inv32"] = np.ascontiguousarray(np.broadcast_to(inv32[None, :], (128, 32)))
    t = np.arange(128, dtype=np.float64)
    decA = np.zeros((128, 8)); decB = np.zeros((128, 8)); epsq = np.zeros((128, 8)); d2t = np.zeros((128, 8, 128))
    sc = 128.0 ** -0.5
    ii = t[None, :]
    jj = t[:, None]
    ci = np.floor(ii / 64); cj = np.floor(jj / 64)
    for h in range(8):
        lg = np.log1p(-2.0 ** (-5.0 - h))
        decA[:, h] = np.exp(lg * (255 - t)) * sc
        decB[:, h] = np.exp(lg * (127 - t)) * sc
        qdec = np.exp(lg * (t + 1))
        epsq[:, h] = EPS / qdec ** 2
        dd = np.where(cj == ci, np.exp(lg * np.abs(ii - jj)), np.where(cj < ci, np.exp(lg * (ii - jj)), 0.0))
        d2t[:, h, :] = dd * sc / qdec[None, :]
    c["c_decA"] = decA.astype(np.float32)
    c["c_decB"] = decB.astype(np.float32)
    c["c_epsq"] = epsq.astype(np.float32)
    c["c_d2t"] = d2t.reshape(128, 1024).astype(np.float32)
    dk = np.zeros((1, 128), np.float32); dk[0, 64:] = BIG
    dq = np.zeros((1, 512), np.float32); dq[0, :64] = -1.0
    c["c_dk"] = dk
    c["c_dq"] = dq
    return c


def _col_perm():
    cols = []
    for h in range(8):
        cols += list(range(1024 + h * 128, 1024 + (h + 1) * 128))
        cols += list(range(2048 + h * 128, 2048 + (h + 1) * 128))
        cols += list(range(0 + h * 128, (h + 1) * 128))
        cols += list(range(3072 + h * 128, 3072 + (h + 1) * 128))
    cols += list(range(4096 + 768, 4096 + 768 + 512))
    cols += list(range(4096 + 768 + 512, 5440))
    cols += list(range(4096, 4096 + 768))
    return np.array(cols)


def prep_inputs(inputs):
    x = np.asarray(inputs["x"], np.float32)
    pos = np.asarray(inputs["positions"], np.int32)
    w_in = np.ascontiguousarray(np.asarray(inputs["w_in"], np.float32)[0][:, _col_perm()])
    shared = {
        "w_in": w_in,
        "w_uq": np.ascontiguousarray(inputs["w_uq"][0], np.float32),
        "w_uk": np.ascontiguousarray(inputs["w_uk"][0], np.float32),
        "w_uv": np.ascontiguousarray(inputs["w_uv"][0], np.float32),
        "w_out": np.ascontiguousarray(inputs["w_out"][0], np.float32),
        "w_up": np.ascontiguousarray(inputs["w_up"][0], np.float32),
        "w_down": np.ascontiguousarray(inputs["w_down"][0], np.float32),
        "qg": np.ascontiguousarray(np.asarray(inputs["q_norm_g"][0], np.float32).reshape(6, 128).T),
        "kvg": np.ascontiguousarray(np.asarray(inputs["kv_norm_g"][0], np.float32).reshape(4, 128).T),
        "gng": np.asarray(inputs["ret_gn_g"][0], np.float32).reshape(1, 1024),
        "ln1g": np.asarray(inputs["ln1_g"][0], np.float32).reshape(1, D),
        "ln1b": np.asarray(inputs["ln1_b"][0], np.float32).reshape(1, D),
        "ln2g": np.asarray(inputs["ln2_g"][0], np.float32).reshape(1, D),
        "ln2b": np.asarray(inputs["ln2_b"][0], np.float32).reshape(1, D),
    }
    in_maps = []
    metas = []
    for core in range(8):
        b, p = core // 2, core % 2
        own_g = [2 * j + p for j in range(16)]
        oth_g = [(2 * j - 1) if p == 0 else (2 * j) for j in range(16)]
        blocks = [None] * 32
        for j in range(16):
            blocks[tb_own(j)] = own_g[j]
            blocks[tb_oth(j)] = oth_g[j]
        xb = x[b]
        xTc = np.zeros((D, SEQ), np.float32)
        pos_tm = np.zeros((128, 32), np.int32)
        km = np.zeros((128, 32), np.float32)
        for tb, g in enumerate(blocks):
            if g < 0:
                km[:, tb] = BIG
                continue
            xTc[:, tb * 128:(tb + 1) * 128] = xb[g * 128:(g + 1) * 128, :].T
            pos_tm[:, tb] = pos[b, g * 128:(g + 1) * 128]
        x_own = np.concatenate([xb[g * 128:(g + 1) * 128, :] for g in own_g], axis=0)
        m = dict(shared)
        m.update(_consts(p))
        m["xT"] = xTc
        m["x_own"] = np.ascontiguousarray(x_own)
        m["pos_tm"] = pos_tm
        m["kmask"] = km
        in_maps.append(m)
        metas.append((b, own_g))
    return in_maps, metas


def kernel(**inputs):
    in_maps, metas = prep_inputs(inputs)
    nc = build()
    res = run_bass_kernel_spmd(nc, in_maps, core_ids=list(range(8)))
    outp = np.zeros((4, SEQ, D), np.float32)
    for core in range(8):
        b, own_g = metas[core]
        o = np.asarray(res.results[core]["out"], np.float32)
        for j, g in enumerate(own_g):
            outp[b, g * 128:(g + 1) * 128, :] = o[j * 128:(j + 1) * 128, :]
    return outp
```

```python
import contextlib
import numpy as np
import ml_dtypes
import concourse.bass as bass
import concourse.mybir as mybir
from concourse.bass_utils import run_bass_kernel_spmd

F32 = mybir.dt.float32
BF16 = mybir.dt.bfloat16
I32 = mybir.dt.int32
AF = mybir.ActivationFunctionType
ALU = mybir.AluOpType

D = 2048
SEQ = 4096
NOWN = 2048
H = 8
DFF = 8192
EPS = 1e-5
ALPHA = 2.0 ** 0.25
IN_W = 5440
BIG = 30000.0
MAGIC = 12582912.0
TWO_PI = 6.283185307179586
C1 = 6.28125
C2 = TWO_PI - C1
SCALE = 192.0 ** -0.5
ENGS = ("pe", "act", "dve", "pool", "sp")
PIPE_B = 2
PIPE_C = False


def tb_own(j):
    return (j // 8) * 16 + j % 8


def tb_oth(j):
    return (j // 8) * 16 + 8 + j % 8


def _is_psum_key(k):
    n = k if isinstance(k, str) else k[0]
    return isinstance(n, str) and len(n) > 1 and n[0] == "p" and n[1].isupper()


class Op:
    __slots__ = ("eng", "fn", "reads", "writes", "dma", "deps", "idx", "sig", "dsem", "dval")


class Prog:
    sems = None

    def __init__(self, nc, same_engine_sync=True, dma_ring=8):
        self.nc = nc
        self.ops = []
        self.same_engine_sync = same_engine_sync
        self.dma_ring = dma_ring
        self.last_writer = {}
        self.readers = {}

    def add(self, eng, fn, reads=(), writes=(), dma=False):
        op = Op()
        ps_r = [k for k in reads if _is_psum_key(k)]
        if ps_r:
            reads = [k for k in reads if not _is_psum_key(k)]
            writes = list(writes) + [k for k in ps_r if k not in writes]
        op.eng, op.fn, op.reads, op.writes, op.dma = eng, fn, tuple(reads), tuple(writes), dma
        op.sig = None
        op.dsem = None
        op.dval = 0
        op.idx = len(self.ops)
        deps = set()
        for k in op.reads:
            w = self.last_writer.get(k)
            if w is not None:
                deps.add(w)
        for k in op.writes:
            w = self.last_writer.get(k)
            if w is not None:
                deps.add(w)
            for r in self.readers.get(k, {}).values():
                deps.add(r)
        deps.discard(op.idx)
        op.deps = sorted(deps)
        for k in op.reads:
            d = self.readers.setdefault(k, {})
            d[(eng, op.idx) if dma else eng] = op.idx
        for k in op.writes:
            self.last_writer[k] = op.idx
            self.readers[k] = {}
        self.ops.append(op)
        return op

    def emit(self):
        nc = self.nc
        ops = self.ops
        ses = self.same_engine_sync
        need = [False] * len(ops)
        for op in ops:
            for d in op.deps:
                dop = ops[d]
                if dop.dma:
                    continue
                if dop.eng == op.eng and not op.dma and (dop.eng == "pe" or not ses):
                    continue
                need[d] = True
        G = self.sems
        cnt = G["cnt"]
        dcount = G["dcount"]
        base_cnt = dict(cnt)
        base_d = dict(dcount)
        per_eng = {e: [] for e in ENGS}
        for op in ops:
            if op.dma:
                n = dcount[op.eng]
                dcount[op.eng] += 1
                op.dsem = n % self.dma_ring
                op.dval = 16 * (n // self.dma_ring + 1)
            elif need[op.idx]:
                cnt[op.eng] += 1
                op.sig = cnt[op.eng]
            per_eng[op.eng].append(op)
        with contextlib.ExitStack() as es:
            csem = G["csem"]
            dsem = G["dsem"]
            block = es.enter_context(nc.Block())

            def run_engine(e, handle):
                waited = {}
                for e2 in ENGS:
                    waited[("c", e2)] = base_cnt[e2]
                    n0 = base_d[e2]
                    for r in range(self.dma_ring):
                        uses = (n0 - 1 - r) // self.dma_ring + 1 if n0 > r else 0
                        waited[("d", e2, r)] = 16 * uses

                def wait(key, sem, val):
                    if waited.get(key, 0) >= val:
                        return
                    handle.wait_ge(sem, val)
                    waited[key] = val

                for op in per_eng[e]:
                    for d in op.deps:
                        dop = ops[d]
                        if dop.dma:
                            wait(("d", dop.eng, dop.dsem), dsem[dop.eng][dop.dsem], dop.dval)
                        else:
                            if dop.eng == e and not op.dma and (e == "pe" or not ses):
                                continue
                            wait(("c", dop.eng), csem[dop.eng], dop.sig)
                    if op.dma:
                        if op.dval > 16:
                            wait(("d", e, op.dsem), dsem[e][op.dsem], op.dval - 16)
                        ins = op.fn(handle)
                        ins.then_inc(dsem[e][op.dsem], 16)
                    else:
                        ins = op.fn(handle)
                        if op.sig is not None:
                            ins.then_inc(csem[e], 1)
                n = dcount[e]
                for r in range(min(n, self.dma_ring)):
                    uses = (n - 1 - r) // self.dma_ring + 1
                    wait(("d", e, r), dsem[e][r], 16 * uses)

            if per_eng["sp"]:
                @block.sync
                def _(h):
                    run_engine("sp", h)
            if per_eng["pool"]:
                @block.gpsimd
                def _(h):
                    run_engine("pool", h)
            if per_eng["act"]:
                @block.scalar
                def _(h):
                    run_engine("act", h)
            if per_eng["dve"]:
                @block.vector
                def _(h):
                    run_engine("dve", h)
            if per_eng["pe"]:
                @block.tensor
                def _(h):
                    run_engine("pe", h)


def dma(P, q, out, in_, reads=(), writes=()):
    P.add(q, lambda e: e.dma_start(out=out, in_=in_), reads=reads, writes=writes, dma=True)


def mm_group(P, out, pairs, reads, writes):
    def fn(e):
        n = len(pairs)
        ins = None
        for i, (l, r) in enumerate(pairs):
            ins = e.matmul(out, lhsT=l, rhs=r, start=(i == 0), stop=(i == n - 1))
        return ins
    P.add("pe", fn, reads=reads, writes=writes)


def transposes(P, items, ident, reads, writes):
    def fn(e):
        ins = None
        for o, i in items:
            ins = e.transpose(out=o, in_=i, identity=ident)
        return ins
    P.add("pe", fn, reads=reads, writes=writes)


def rope_ops(P, eng, dst, src, cos, sin, half, tmp, tmpkey, reads, writes):
    h = half

    def f1(e):
        e.tensor_tensor(out=tmp[:, 0:h], in0=src[:, 0:h], in1=cos, op=ALU.mult)
        e.tensor_tensor(out=tmp[:, h:2 * h], in0=src[:, h:2 * h], in1=sin, op=ALU.mult)
        e.tensor_tensor(out=tmp[:, 2 * h:3 * h], in0=src[:, 0:h], in1=sin, op=ALU.mult)
        return e.tensor_tensor(out=tmp[:, 3 * h:4 * h], in0=src[:, h:2 * h], in1=cos, op=ALU.mult)

    def f2(e):
        e.tensor_tensor(out=dst[:, 0:h], in0=tmp[:, 0:h], in1=tmp[:, h:2 * h], op=ALU.subtract)
        return e.tensor_tensor(out=dst[:, h:2 * h], in0=tmp[:, 2 * h:3 * h], in1=tmp[:, 3 * h:4 * h], op=ALU.add)
    P.add(eng, f1, reads=reads, writes=[tmpkey])
    P.add(eng, f2, reads=[tmpkey], writes=writes)


def rstd_ops(P, out, ssum, scale, eps_ap_or_float, tmp, reads, writes, key):
    def f1(e):
        if isinstance(eps_ap_or_float, float):
            return e.tensor_scalar(out=tmp, in0=ssum, scalar1=scale, scalar2=eps_ap_or_float, op0=ALU.mult, op1=ALU.add)
        return e.tensor_scalar(out=tmp, in0=ssum, scalar1=scale, scalar2=eps_ap_or_float, op0=ALU.mult, op1=ALU.add)
    k = ("rstd_tmp", key)
    P.add("dve", f1, reads=reads, writes=[k])
    P.add("act", lambda e: e.activation(out=tmp, in_=tmp, func=AF.Sqrt), reads=[k], writes=[k])
    P.add("dve", lambda e: e.reciprocal(out=out, in_=tmp), reads=[k], writes=writes)


class Ctx:
    pass


def build(debug=False, phases="0 A2 B A1 C D"):
    phases = phases.split()
    nc = bass.Bass("TRN2", target_bir_lowering=False)
    C = Ctx()
    C.nc = nc
    C.debug = debug

    def din(name, shape, dt=F32):
        return nc.dram_tensor(name, list(shape), dt, kind="ExternalInput").ap()

    xT = din("xT", [D, SEQ])
    x_own = din("x_own", [NOWN, D])
    pos_tm = din("pos_tm", [128, 32], I32)
    kmask = din("kmask", [128, 32])
    w_in = din("w_in", [D, IN_W])
    w_uq = din("w_uq", [768, 1536])
    w_uk = din("w_uk", [512, 1024])
    w_uv = din("w_uv", [512, 1024])
    w_out = din("w_out", [D, D])
    w_up = din("w_up", [D, DFF])
    w_down = din("w_down", [DFF, D])
    qg = din("qg", [128, 6])
    kvg = din("kvg", [128, 4])
    gng = din("gng", [1, 1024])
    ln1g = din("ln1g", [1, D])
    ln1b = din("ln1b", [1, D])
    ln2g = din("ln2g", [1, D])
    ln2b = din("ln2b", [1, D])
    c_ident = din("c_ident", [128, 128])
    c_inv64 = din("c_inv64", [128, 64])
    c_inv32 = din("c_inv32", [128, 32])
    c_decA = din("c_decA", [128, 8])
    c_decB = din("c_decB", [128, 8])
    c_epsq = din("c_epsq", [128, 8])
    c_d2t = din("c_d2t", [128, 8 * 128])
    c_dk = din("c_dk", [1, 128])
    c_dq = din("c_dq", [1, 512])
    out = nc.dram_tensor("out", [NOWN, D], F32, kind="ExternalOutput").ap()
    y1s = nc.dram_tensor("y1s", [NOWN, D], F32).ap()
    x1Ts = nc.dram_tensor("x1Ts", [D, NOWN], BF16).ap()
    dbg = {}
    if debug:
        dbg["mixR"] = nc.dram_tensor("dbg_mixR", [128, 8 * NOWN], F32, kind="ExternalOutput").ap()
        dbg["mixM"] = nc.dram_tensor("dbg_mixM", [128, 8 * NOWN], F32, kind="ExternalOutput").ap()
        dbg["cqnT"] = nc.dram_tensor("dbg_cqnT", [128, 6 * NOWN], F32, kind="ExternalOutput").ap()
        dbg["ckvnT"] = nc.dram_tensor("dbg_ckvnT", [128, 4 * SEQ], F32, kind="ExternalOutput").ap()
        dbg["krT"] = nc.dram_tensor("dbg_krT", [65, SEQ], F32, kind="ExternalOutput").ap()
        dbg["y1"] = y1s

    xTv = xT.rearrange("(c p) t -> p c t", p=128)
    w_inv = w_in.rearrange("(c p) n -> p c n", p=128)
    w_uqv = w_uq.rearrange("(c p) n -> p c n", p=128)
    w_ukv = w_uk.rearrange("(c p) n -> p c n", p=128)
    w_uvv = w_uv.rearrange("(c p) n -> p c n", p=128)
    w_outv = w_out.rearrange("(c p) n -> p c n", p=128)
    w_upv = w_up.rearrange("(c p) n -> p c n", p=128)
    w_downv = w_down.rearrange("(c p) n -> p c n", p=128)
    x1Tsv = x1Ts.rearrange("(c p) t -> p c t", p=128)

    def sbt(es, name, shape, dt):
        return es.enter_context(nc.sbuf_tensor(name, list(shape), dt))

    def pst(es, name, shape, dt):
        return es.enter_context(nc.psum_tensor(name, list(shape), dt))

    def dump(P, es, name, src_tile, ncols, src_key, dt=BF16, parts=128):
        stg = sbt(es, "dstg_" + name, [parts, 2048], F32)
        flat = src_tile
        for c0 in range(0, ncols, 2048):
            n = min(2048, ncols - c0)
            P.add("dve", (lambda c0=c0, n=n: lambda e: e.tensor_copy(out=stg[:, 0:n], in_=flat[:, c0:c0 + n]))(),
                  reads=[src_key], writes=[("dstg", name)])
            dma(P, "sp", dbg[name][:, c0:c0 + n], stg[:, 0:n], reads=[("dstg", name)])

    with contextlib.ExitStack() as g_es:
        Prog.sems = {
            "cnt": {e: 0 for e in ENGS}, "dcount": {e: 0 for e in ENGS},
            "csem": {e: g_es.enter_context(nc.semaphore("c_" + e)) for e in ENGS},
            "dsem": {e: [g_es.enter_context(nc.semaphore("d_%s_%d" % (e, i))) for i in range(8)] for e in ("sp", "pool")},
        }
        ident = sbt(g_es, "ident", [128, 128], BF16)
        with contextlib.ExitStack() as s1:
            mixM = sbt(s1, "mixM", [128, 8, NOWN], BF16)
            posf = sbt(s1, "posf", [128, 32], F32)
            kmask_sb = sbt(s1, "kmask_sb", [128, 32], F32)
            s2 = contextlib.ExitStack()
            cosM = sbt(s2, "cosM", [128, 32, 32], F32)
            sinM = sbt(s2, "sinM", [128, 32, 32], F32)

            def rope_tables(P, es, inv_dram, nf, cos_t, sin_t, tag):
                n = 32 * nf
                inv = sbt(es, "inv" + tag, [128, nf], F32)
                ang = sbt(es, "ang" + tag, [128, 32, nf], F32)
                kk = sbt(es, "kk" + tag, [128, 32, nf], F32)
                dma(P, "sp", inv[:], inv_dram[:, :], writes=["inv" + tag])
                angf = ang[:].rearrange("p a b -> p (a b)")
                kkf = kk[:].rearrange("p a b -> p (a b)")
                cosf = cos_t[:].rearrange("p a b -> p (a b)")
                sinf = sin_t[:].rearrange("p a b -> p (a b)")

                def f_ang(e):
                    ins = None
                    for tb in range(32):
                        ins = e.tensor_scalar(out=ang[:, tb, :], in0=inv[:], scalar1=posf[:, tb:tb + 1], scalar2=None,
                                              op0=ALU.mult)
                    return ins
                P.add("dve", f_ang, reads=["inv" + tag, "posf"], writes=["ang" + tag])
                ka = "ang" + tag
                kkk = "kk" + tag
                P.add("dve", lambda e: e.tensor_scalar(out=kkf, in0=angf, scalar1=1.0 / TWO_PI, scalar2=None, op0=ALU.mult),
                      reads=[ka], writes=[kkk])
                P.add("dve", lambda e: e.tensor_scalar(out=kkf, in0=kkf, scalar1=MAGIC, scalar2=None, op0=ALU.add),
                      reads=[kkk], writes=[kkk])
                P.add("dve", lambda e: e.tensor_scalar(out=kkf, in0=kkf, scalar1=-MAGIC, scalar2=None, op0=ALU.add),
                      reads=[kkk], writes=[kkk])
                P.add("dve", lambda e: e.scalar_tensor_tensor(out=angf, in0=kkf, scalar=-C1, in1=angf, op0=ALU.mult, op1=ALU.add),
                      reads=[kkk, ka], writes=[ka])
                P.add("dve", lambda e: e.scalar_tensor_tensor(out=angf, in0=kkf, scalar=-C2, in1=angf, op0=ALU.mult, op1=ALU.add),
                      reads=[kkk, ka], writes=[ka])
                P.add("dve", lambda e: e.tensor_scalar(out=angf, in0=angf, scalar1=3.1415925, scalar2=-3.1415925,
                                                      op0=ALU.min, op1=ALU.max), reads=[ka], writes=[ka])
                P.add("act", lambda e: e.activation(out=sinf, in_=angf, func=AF.Sin), reads=[ka], writes=["sin" + tag])
                P.add("act", lambda e: e.activation(out=kkf, in_=angf, func=AF.Sin, scale=0.5), reads=[ka], writes=[kkk])
                P.add("dve", lambda e: e.tensor_tensor(out=kkf, in0=kkf, in1=kkf, op=ALU.mult), reads=[kkk], writes=[kkk])
                P.add("dve", lambda e: e.tensor_scalar(out=cosf, in0=kkf, scalar1=-2.0, scalar2=1.0, op0=ALU.mult, op1=ALU.add),
                      reads=[kkk], writes=["cos" + tag])

            if "0" in phases:
                with contextlib.ExitStack() as es:
                    P = Prog(nc)
                    posi = sbt(es, "posi", [128, 32], I32)
                    dma(P, "pool", ident[:], c_ident[:, :], writes=["ident"])
                    dma(P, "sp", posi[:], pos_tm[:, :], writes=["posi"])
                    dma(P, "sp", kmask_sb[:], kmask[:, :], writes=["kmask"])
                    P.add("dve", lambda e: e.tensor_copy(out=posf[:], in_=posi[:]), reads=["posi"], writes=["posf"])
                    rope_tables(P, es, c_inv32, 32, cosM, sinM, "M")
                    P.emit()

            with s2:
                cqnT = sbt(s2, "cqnT", [128, 6, NOWN], BF16)
                ckvnT = sbt(s2, "ckvnT", [128, 4, SEQ], BF16)
                kraugT = sbt(s2, "kraugT", [65, SEQ], BF16)
                if "A2" in phases:
                    with nc.named_scope("A2"):
                        phase_A2(C, nc, sbt, pst, ident, xTv, w_inv, cosM, sinM, kmask_sb, cqnT, ckvnT, kraugT)
                    if debug:
                        with contextlib.ExitStack() as es:
                            P = Prog(nc)
                            dump(P, es, "cqnT", cqnT[:].rearrange("p a b -> p (a b)"), 6 * NOWN, "x")
                            dump(P, es, "ckvnT", ckvnT[:].rearrange("p a b -> p (a b)"), 4 * SEQ, "x")
                            stg = sbt(es, "dstg_kr", [65, SEQ], F32)
                            P.add("dve", lambda e: e.tensor_copy(out=stg[:], in_=kraugT[:]), writes=["stgkr"])
                            dma(P, "sp", dbg["krT"][:, :], stg[:], reads=["stgkr"])
                            P.emit()
                if "B" in phases:
                    with nc.named_scope("B"):
                        phase_B(C, nc, sbt, pst, ident, w_uqv, w_ukv, w_uvv, qg, kvg, c_dk, c_dq, cosM, sinM,
                                cqnT, ckvnT, kraugT, mixM)
            mixR = sbt(s1, "mixR", [128, 8, NOWN], BF16)
            if "A1" in phases:
                with nc.named_scope("A1"):
                    phase_A1(C, nc, sbt, pst, ident, xTv, w_inv, c_inv64, posf, c_decA, c_decB, c_epsq, c_d2t, gng,
                             mixR, rope_tables)
            if debug and ("A1" in phases or "B" in phases):
                with contextlib.ExitStack() as es:
                    P = Prog(nc)
                    if "A1" in phases:
                        dump(P, es, "mixR", mixR[:].rearrange("p a b -> p (a b)"), 8 * NOWN, "x")
                    if "B" in phases:
                        dump(P, es, "mixM", mixM[:].rearrange("p a b -> p (a b)"), 8 * NOWN, "x")
                    P.emit()
            if "C" in phases:
                with nc.named_scope("C"):
                    phase_C(C, nc, sbt, pst, ident, w_outv, x_own, ln1g, ln1b, mixR, mixM, y1s, x1Tsv)
        if "D" in phases:
            with nc.named_scope("D"):
                phase_D(C, nc, sbt, pst, w_upv, w_downv, ln2g, ln2b, y1s, x1Tsv, out)
    return nc


def phase_A2(C, nc, sbt, pst, ident, xTv, w_inv, cosM, sinM, kmask_sb, cqnT, ckvnT, kraugT):
    with contextlib.ExitStack() as es:
        P = Prog(nc)
        xTb = sbt(es, "a2_xT", [128, 16, 1024], BF16)
        Wg8 = sbt(es, "a2_w8", [128, 16, 512], BF16)
        Wg9 = sbt(es, "a2_w9", [128, 16, 448], BF16)
        Wg10 = sbt(es, "a2_w10", [128, 16, 384], BF16)
        junk = sbt(es, "a2_junk", [128, 512], BF16)
        NR = 2
        ss = [sbt(es, "a2_ss%d" % i, [128, 8], F32) for i in range(NR)]
        ckvn_tm = [sbt(es, "a2_ckvn%d" % i, [128, 512], BF16) for i in range(NR)]
        cq_raw = [sbt(es, "a2_cqraw%d" % i, [128, 768], F32) for i in range(NR)]
        cqn_tm = [sbt(es, "a2_cqn%d" % i, [128, 768], BF16) for i in range(NR)]
        kaug = [sbt(es, "a2_kaug%d" % i, [128, 65], BF16) for i in range(NR)]
        rtmp = [sbt(es, "a2_rtmp%d" % i, [128, 128], F32) for i in range(NR)]
        pA = [pst(es, "a2_pA%d" % i, [128, 512], F32) for i in range(3)]
        pT0 = pst(es, "a2_pT0", [128, 1024], BF16)
        pT1 = pst(es, "a2_pT1", [128, 1024], BF16)
        pT2 = pst(es, "a2_pT2", [128, 1024], BF16)
        for c in range(16):
            dma(P, "pool", Wg8[:, c, :], w_inv[:, c, 4096:4608], writes=[("w8", c)])
        for c in range(16):
            dma(P, "pool", Wg9[:, c, :], w_inv[:, c, 4608:5056], writes=[("w9", c)])
        for c in range(16):
            dma(P, "pool", Wg10[:, c, :], w_inv[:, c, 5056:5440], writes=[("w10", c)])
        it = 0
        pai = 0
        import os
        QL = int(os.environ.get("A2_Q", "4"))
        KL = int(os.environ.get("A2_K", "8"))
        for q in range(QL):
            for c in range(16):
                dma(P, "pool", xTb[:, c, :], xTv[:, c, q * 1024:(q + 1) * 1024], writes=[("xT", c)])
            own = (q % 2 == 0)
            for k in range(KL):
                tb = q * 8 + k
                j = (q // 2) * 8 + k
                r = it % NR
                it += 1
                xk = [("xT", c) for c in range(16)]
                pa = pA[pai % 3]; pak = ("pA", pai % 3); pai += 1
                mm_group(P, pa[:, 0:512], [(xTb[:, c, k * 128:(k + 1) * 128], Wg8[:, c, :]) for c in range(16)],
                         reads=xk + [("w8", c) for c in range(16)], writes=[pak])
                P.add("act", (lambda pa=pa, r=r: lambda e: e.activation(out=junk[:], in_=pa[:, 0:512], func=AF.Square,
                                                                     accum_out=ss[r][:, 0:1]))(),
                      reads=[pak], writes=[("ss0", r), "junk"])
                rstd_ops(P, ss[r][:, 4:5], ss[r][:, 0:1], 1.0 / 512, EPS, ss[r][:, 3:4], reads=[("ss0", r)], writes=[("rs0", r)], key=("a2a", r))
                P.add("dve", (lambda pa=pa, r=r: lambda e: e.tensor_scalar(out=ckvn_tm[r][:], in0=pa[:, 0:512],
                                                                         scalar1=ss[r][:, 4:5], scalar2=None, op0=ALU.mult))(),
                      reads=[pak, ("rs0", r)], writes=[("ckvn", r)])
                transposes(P, [(pT0[:, c * 128:(c + 1) * 128], ckvn_tm[r][:, c * 128:(c + 1) * 128]) for c in range(4)],
                           ident[:], reads=[("ckvn", r), "ident"], writes=["pT0a"])
                P.add("act", (lambda tb=tb: lambda e: e.activation(
                    out=ckvnT[:, 0:4, tb * 128:(tb + 1) * 128],
                    in_=pT0[:, 0:512].rearrange("p (c t) -> p c t", c=4), func=AF.Copy))(),
                    reads=["pT0a"], writes=[("ckvnT", tb)])
                ncol = 448 if own else 64
                pa = pA[pai % 3]; pak = ("pA", pai % 3); pai += 1
                mm_group(P, pa[:, 0:ncol], [(xTb[:, c, k * 128:(k + 1) * 128], Wg9[:, c, 0:ncol]) for c in range(16)],
                         reads=xk + [("w9", c) for c in range(16)], writes=[pak])
                rope_ops(P, "dve", kaug[r], pa, cosM[:, tb, :], sinM[:, tb, :], 32, rtmp[r], ("rtmp", r),
                         reads=[pak, "cosM", "sinM"], writes=[("kaug", r)])
                P.add("dve", (lambda r=r, tb=tb: lambda e: e.tensor_copy(out=kaug[r][:, 64:65], in_=kmask_sb[:, tb:tb + 1]))(),
                      reads=["kmask"], writes=[("kaug", r)])
                transposes(P, [(pT2[0:65, 0:128], kaug[r][:, 0:65])], ident[:], reads=[("kaug", r), "ident"], writes=["pT2"])
                P.add("act", (lambda tb=tb: lambda e: e.activation(out=kraugT[0:65, tb * 128:(tb + 1) * 128],
                                                                 in_=pT2[0:65, 0:128], func=AF.Copy))(),
                      reads=["pT2"], writes=[("kraugT", tb)])
                if own:
                    P.add("dve", (lambda pa=pa, r=r: lambda e: e.tensor_copy(out=cq_raw[r][:, 0:384], in_=pa[:, 64:448]))(),
                          reads=[pak], writes=[("cqraw0", r)])
                    P.add("act", (lambda pa=pa, r=r: lambda e: e.activation(out=junk[:, 0:384], in_=pa[:, 64:448], func=AF.Square,
                                                                         accum_out=ss[r][:, 1:2]))(),
                          reads=[pak], writes=[("ss1", r), "junk"])
                    pa = pA[pai % 3]; pak = ("pA", pai % 3); pai += 1
                    mm_group(P, pa[:, 0:384], [(xTb[:, c, k * 128:(k + 1) * 128], Wg10[:, c, :]) for c in range(16)],
                             reads=xk + [("w10", c) for c in range(16)], writes=[pak])
                    P.add("dve", (lambda pa=pa, r=r: lambda e: e.tensor_copy(out=cq_raw[r][:, 384:768], in_=pa[:, 0:384]))(),
                          reads=[pak], writes=[("cqraw1", r)])
                    P.add("act", (lambda pa=pa, r=r: lambda e: e.activation(out=junk[:, 0:384], in_=pa[:, 0:384], func=AF.Square,
                                                                         accum_out=ss[r][:, 2:3]))(),
                          reads=[pak], writes=[("ss2", r), "junk"])
                    P.add("dve", (lambda r=r: lambda e: e.tensor_tensor(out=ss[r][:, 5:6], in0=ss[r][:, 1:2], in1=ss[r][:, 2:3],
                                                                      op=ALU.add))(),
                          reads=[("ss1", r), ("ss2", r)], writes=[("ss12", r)])
                    rstd_ops(P, ss[r][:, 7:8], ss[r][:, 5:6], 1.0 / 768, EPS, ss[r][:, 6:7], reads=[("ss12", r)], writes=[("rs1", r)], key=("a2b", r))
                    P.add("dve", (lambda r=r: lambda e: e.tensor_scalar(out=cqn_tm[r][:], in0=cq_raw[r][:], scalar1=ss[r][:, 7:8],
                                                                      scalar2=None, op0=ALU.mult))(),
                          reads=[("cqraw0", r), ("cqraw1", r), ("rs1", r)], writes=[("cqn", r)])
                    transposes(P, [(pT1[:, c * 128:(c + 1) * 128], cqn_tm[r][:, c * 128:(c + 1) * 128]) for c in range(6)],
                               ident[:], reads=[("cqn", r), "ident"], writes=["pT1"])
                    P.add("act", (lambda j=j: lambda e: e.activation(
                        out=cqnT[:, 0:6, j * 128:(j + 1) * 128],
                        in_=pT1[:, 0:768].rearrange("p (c t) -> p c t", c=6), func=AF.Copy))(),
                        reads=["pT1"], writes=[("cqnT", j)])
        P.emit()


def phase_B(C, nc, sbt, pst, ident, w_uqv, w_ukv, w_uvv, qg, kvg, c_dk, c_dq, cosM, sinM, cqnT, ckvnT, kraugT, mixM):
    with contextlib.ExitStack() as es:
        P = Prog(nc)
        qg_sb = sbt(es, "b_qg", [128, 6], F32)
        kvg_sb = sbt(es, "b_kvg", [128, 4], F32)
        dk = sbt(es, "b_dk", [1, 128], BF16)
        dq = sbt(es, "b_dq", [1, 512], BF16)
        wq_f = sbt(es, "b_wqf", [128, 6, 192], F32)
        wk_f = sbt(es, "b_wkf", [128, 4, 128], F32)
        wv_f = sbt(es, "b_wvf", [128, 4, 128], F32)
        NH = 2
        wq_b = [sbt(es, "b_wqb%d" % i, [128, 6, 192], BF16) for i in range(NH)]
        wk_b = [sbt(es, "b_wkb%d" % i, [128, 4, 128], BF16) for i in range(NH)]
        wv_b = [sbt(es, "b_wvb%d" % i, [128, 4, 128], BF16) for i in range(NH)]
        knT = [sbt(es, "b_knT%d" % i, [128, SEQ], BF16) for i in range(NH)]
        Vaug = [sbt(es, "b_V%d" % i, [128, 32, 129], BF16) for i in range(NH)]
        qnT = [sbt(es, "b_qnT%d" % i, [128, NOWN], BF16) for i in range(NH)]
        qrT = [sbt(es, "b_qrT%d" % i, [65, NOWN], BF16) for i in range(NH)]
        qtm = [sbt(es, "b_qtm%d" % i, [128, 193], BF16) for i in range(2)]
        rtmp = [sbt(es, "b_rtmp%d" % i, [128, 128], F32) for i in range(2)]
        PT = [sbt(es, "b_PT%d" % i, [128, 512], BF16) for i in range(3)]
        otm = [sbt(es, "b_otm%d" % i, [128, 128], BF16) for i in range(2)]
        rinv = [sbt(es, "b_rinv%d" % i, [128, 1], F32) for i in range(2)]
        pP = pst(es, "b_pP", [128, 512], F32)
        pS = [pst(es, "b_pS%d" % i, [128, 512], F32) for i in range(2)]
        pO = [pst(es, "b_pO%d" % i, [128, 512], F32) for i in range(4)]
        pT = pst(es, "b_pT", [128, 1024], BF16)

        dma(P, "sp", qg_sb[:], qg[:, :], writes=["qg"])
        dma(P, "sp", kvg_sb[:], kvg[:, :], writes=["kvg"])
        dma(P, "pool", dk[:], c_dk[:, :], writes=["dk"])
        dma(P, "pool", dq[:], c_dq[:, :], writes=["dq"])
        for i in range(NH):
            P.add("pool", (lambda i=i: lambda e: e.memset(Vaug[i][:, :, 128:129], 1.0))(), writes=[("Vone", i)])
        for i in range(2):
            P.add("pool", (lambda i=i: lambda e: e.memset(qtm[i][:, 192:193], -1.0))(), writes=[("qtm1", i)])

        import os
        HL = int(os.environ.get("B_H", "8"))
        QL = int(os.environ.get("B_Q", "4"))
        cnt = {"si": 0, "pti": 0, "oi": 0, "qi": 0}

        def proj_steps(h):
            hb = h % NH
            steps = []

            def w_step():
                dma(P, "sp", wq_f[:], w_uqv[:, :, h * 192:(h + 1) * 192], writes=["wqf"])
                dma(P, "sp", wk_f[:], w_ukv[:, :, h * 128:(h + 1) * 128], writes=["wkf"])
                dma(P, "sp", wv_f[:], w_uvv[:, :, h * 128:(h + 1) * 128], writes=["wvf"])

                def f_wq(e):
                    ins = None
                    for c in range(6):
                        ins = e.tensor_scalar(out=wq_b[hb][:, c, :], in0=wq_f[:, c, :], scalar1=qg_sb[:, c:c + 1], scalar2=None,
                                              op0=ALU.mult)
                    return ins
                P.add("pool", f_wq, reads=["wqf", "qg"], writes=[("wqb", hb)])

                def f_wkv(e):
                    ins = None
                    for c in range(4):
                        e.tensor_scalar(out=wk_b[hb][:, c, :], in0=wk_f[:, c, :], scalar1=kvg_sb[:, c:c + 1], scalar2=None, op0=ALU.mult)
                        ins = e.tensor_scalar(out=wv_b[hb][:, c, :], in0=wv_f[:, c, :], scalar1=kvg_sb[:, c:c + 1], scalar2=None,
                                              op0=ALU.mult)
                    return ins
                P.add("pool", f_wkv, reads=["wkf", "wvf", "kvg"], writes=[("wkvb", hb)])
            steps.append(w_step)

            def q_step(j):
                def f():
                    r = cnt["qi"] % 2
                    cnt["qi"] += 1
                    mm_group(P, pP[:, 0:192], [(cqnT[:, c, j * 128:(j + 1) * 128], wq_b[hb][:, c, :]) for c in range(6)],
                             reads=[("cqnT", j), ("wqb", hb)], writes=["pP"])
                    P.add("act", lambda e: e.activation(out=qtm[r][:, 0:128], in_=pP[:, 0:128], func=AF.Copy),
                          reads=["pP"], writes=[("qtmA", r)])
                    tbq = tb_own(j)
                    rope_ops(P, "dve", qtm[r][:, 128:192], pP[:, 128:192], cosM[:, tbq, :], sinM[:, tbq, :], 32, rtmp[r], ("rtmp", r),
                             reads=["pP", "cosM", "sinM"], writes=[("qtmB", r)])
                    transposes(P, [(pT[:, 0:128], qtm[r][:, 0:128]), (pT[0:65, 128:256], qtm[r][:, 128:193])], ident[:],
                               reads=[("qtmA", r), ("qtmB", r), ("qtm1", r), "ident"], writes=["pT"])
                    P.add("dve", lambda e: e.tensor_copy(out=qnT[hb][:, j * 128:(j + 1) * 128], in_=pT[:, 0:128]),
                          reads=["pT"], writes=[("qnT", hb, j)])
                    P.add("act", lambda e: e.activation(out=qrT[hb][0:65, j * 128:(j + 1) * 128], in_=pT[0:65, 128:256], func=AF.Copy),
                          reads=["pT"], writes=[("qrT", hb, j)])
                return f

            def k_step(kg):
                def f():
                    mm_group(P, pP[:, 0:512], [(wk_b[hb][:, c, :], ckvnT[:, c, kg * 512:(kg + 1) * 512]) for c in range(4)],
                             reads=[("ckvnT", kg * 4 + i) for i in range(4)] + [("wkvb", hb)], writes=["pP"])
                    if kg % 2 == 0:
                        P.add("dve", lambda e: e.tensor_copy(out=knT[hb][:, kg * 512:(kg + 1) * 512], in_=pP[:, 0:512]),
                              reads=["pP"], writes=[("knT", hb, kg * 4 + i) for i in range(4)])
                    else:
                        P.add("act", lambda e: e.activation(out=knT[hb][:, kg * 512:(kg + 1) * 512], in_=pP[:, 0:512], func=AF.Copy),
                              reads=["pP"], writes=[("knT", hb, kg * 4 + i) for i in range(4)])
                return f

            def v_step(vg):
                def f():
                    def f_v(e):
                        ins = None
                        for i in range(4):
                            tb = vg * 4 + i
                            for c in range(4):
                                ins = e.matmul(pP[:, i * 128:(i + 1) * 128], lhsT=ckvnT[:, c, tb * 128:(tb + 1) * 128],
                                               rhs=wv_b[hb][:, c, :], start=(c == 0), stop=(c == 3))
                        return ins
                    P.add("pe", f_v, reads=[("ckvnT", vg * 4 + i) for i in range(4)] + [("wkvb", hb)], writes=["pP"])
                    vout = Vaug[hb][:, vg * 4:(vg + 1) * 4, 0:128]
                    vin = pP[:, 0:512].rearrange("p (a b) -> p a b", a=4)
                    if vg % 2 == 1:
                        P.add("dve", lambda e: e.tensor_copy(out=vout, in_=vin), reads=["pP"],
                              writes=[("V", hb, vg * 4 + i) for i in range(4)])
                    else:
                        P.add("act", lambda e: e.activation(out=vout, in_=vin, func=AF.Copy), reads=["pP"],
                              writes=[("V", hb, vg * 4 + i) for i in range(4)])
                return f
            for j in range(16):
                steps.append(q_step(j))
            for kg in range(8):
                steps.append(k_step(kg))
            for vg in range(8):
                steps.append(v_step(vg))
            return steps

        def attn_tiles(h):
            tiles = []
            for Q in range(QL):
                for i in range(4 * Q + 4):
                    for typ in (0, 1):
                        tb = tb_own(i) if typ == 0 else tb_oth(i)
                        if i < 4 * Q:
                            qoff, nq = 0, 4
                        else:
                            qoff, nq = i - 4 * Q, 4 * Q + 4 - i
                        tiles.append(dict(Q=Q, i=i, typ=typ, tb=tb, qoff=qoff, nq=nq, N=nq * 128, c0=(4 * Q + qoff) * 128,
                                          diag=(typ == 0 and i >= 4 * Q), lastQ=(i == 4 * Q + 3 and typ == 1)))
            return tiles

        def emit_S(h, T):
            hb = h % NH
            sb_ = cnt["si"] % 2
            cnt["si"] += 1
            pr = cnt["pti"] % 3
            cnt["pti"] += 1
            T["pr"] = pr
            tb, N, c0, diag = T["tb"], T["N"], T["c0"], T["diag"]

            def f_s(e):
                e.matmul(pS[sb_][:, 0:N], lhsT=knT[hb][:, tb * 128:(tb + 1) * 128], rhs=qnT[hb][:, c0:c0 + N],
                         start=True, stop=False)
                ins = e.matmul(pS[sb_][:, 0:N], lhsT=kraugT[0:65, tb * 128:(tb + 1) * 128],
                               rhs=qrT[hb][0:65, c0:c0 + N], start=False, stop=(not diag))
                if diag:
                    ins = e.matmul(pS[sb_][:, 0:N], lhsT=dk[0:1, :], rhs=dq[0:1, 0:N], start=False, stop=True)
                return ins
            qblocks = list(range(4 * T["Q"] + T["qoff"], 4 * T["Q"] + 4))
            P.add("pe", f_s, reads=[("knT", hb, tb), ("kraugT", tb), "dk", "dq"] +
                  [("qnT", hb, jj) for jj in qblocks] + [("qrT", hb, jj) for jj in qblocks],
                  writes=[("pS", sb_)])
            P.add("act", lambda e: e.activation(out=PT[pr][:, 0:N], in_=pS[sb_][:, 0:N], func=AF.Exp, scale=SCALE),
                  reads=[("pS", sb_)], writes=[("PT", pr)])

        def emit_PV(h, T):
            hb = h % NH
            pr, tb, nq, qoff, i, typ, Q = T["pr"], T["tb"], T["nq"], T["qoff"], T["i"], T["typ"], T["Q"]

            def f_pv(e):
                ins = None
                for a in range(nq):
                    jj = qoff + a
                    J = 4 * Q + jj
                    first = (i == 0 and typ == 0)
                    last = (i == J and typ == 1)
                    ins = e.matmul(pO[jj][:, 0:129], lhsT=PT[pr][:, a * 128:(a + 1) * 128], rhs=Vaug[hb][:, tb, :],
                                   start=first, stop=last)
                return ins
            P.add("pe", f_pv, reads=[("PT", pr), ("V", hb, tb), ("Vone", hb)], writes=[("pO", qoff + a) for a in range(nq)])

        def emit_fin(h, Q):
            for jj in range(4):
                J = 4 * Q + jj
                r = cnt["oi"] % 2
                cnt["oi"] += 1
                P.add("dve", (lambda jj=jj, r=r: lambda e: e.reciprocal(out=rinv[r][:], in_=pO[jj][:, 128:129]))(),
                      reads=[("pO", jj)], writes=[("rinv", r)])
                P.add("dve", (lambda jj=jj, r=r: lambda e: e.tensor_scalar(out=otm[r][:], in0=pO[jj][:, 0:128], scalar1=rinv[r][:],
                                                                         scalar2=None, op0=ALU.mult))(),
                      reads=[("pO", jj), ("rinv", r)], writes=[("otm", r)])
                transposes(P, [(pT[:, 256:384], otm[r][:])], ident[:], reads=[("otm", r), "ident"], writes=["pT"])
                P.add("act", (lambda J=J: lambda e: e.activation(out=mixM[:, h, J * 128:(J + 1) * 128], in_=pT[:, 256:384],
                                                               func=AF.Copy))(),
                      reads=["pT"], writes=[("mixinT", 8 + h, J)])

        if PIPE_B == 2:
            for st_ in proj_steps(0):
                st_()
        for h in range(HL):
            if PIPE_B == 1:
                for st_ in proj_steps(h):
                    st_()
                prev = None
                for T in attn_tiles(h):
                    emit_S(h, T)
                    if prev is not None:
                        emit_PV(h, prev)
                        if prev["lastQ"]:
                            emit_fin(h, prev["Q"])
                    prev = T
                emit_PV(h, prev)
                emit_fin(h, prev["Q"])
                continue
            if not PIPE_B:
                for st_ in proj_steps(h):
                    st_()
                for T in attn_tiles(h):
                    emit_S(h, T)
                    emit_PV(h, T)
                    if T["lastQ"]:
                        emit_fin(h, T["Q"])
                continue
            nxt = proj_steps(h + 1) if h + 1 < HL else []
            prev = None
            for t, T in enumerate(attn_tiles(h)):
                emit_S(h, T)
                if prev is not None:
                    emit_PV(h, prev)
                    if prev["lastQ"]:
                        emit_fin(h, prev["Q"])
                prev = T
                if nxt and t % 2 == 1:
                    nxt.pop(0)()
            emit_PV(h, prev)
            emit_fin(h, prev["Q"])
            while nxt:
                nxt.pop(0)()
        P.emit()


def phase_A1(C, nc, sbt, pst, ident, xTv, w_inv, c_inv64, posf, c_decA, c_decB, c_epsq, c_d2t, gng, mixR, rope_tables):
    with contextlib.ExitStack() as es:
        P = Prog(nc)
        cosR = sbt(es, "a1_cosR", [128, 32, 64], F32)
        sinR = sbt(es, "a1_sinR", [128, 32, 64], F32)
        with contextlib.ExitStack() as es0:
            P0 = Prog(nc)
            rope_tables(P0, es0, c_inv64, 64, cosR, sinR, "R")
            P0.emit()
        xTb = sbt(es, "a1_xT", [128, 16, 2048], BF16)
        Wg = [sbt(es, "a1_w%d" % i, [128, 16, 512], BF16) for i in range(2)]
        decA = sbt(es, "a1_decA", [128, 8], F32)
        decB = sbt(es, "a1_decB", [128, 8], F32)
        epsq = sbt(es, "a1_epsq", [128, 8], F32)
        d2t = sbt(es, "a1_d2t", [128, 8, 128], F32)
        gng_sb = sbt(es, "a1_gng", [128, 1024], F32)
        U32 = sbt(es, "a1_U", [128, 8, 128], F32)
        NR = 3
        T32 = [sbt(es, "a1_T%d" % i, [128, 128], F32) for i in range(NR)]
        Tb = [sbt(es, "a1_Tb%d" % i, [128, 128], BF16) for i in range(NR)]
        k_tm = [sbt(es, "a1_k%d" % i, [128, 128], BF16) for i in range(NR)]
        kd_tm = [sbt(es, "a1_kd%d" % i, [128, 128], BF16) for i in range(NR)]
        v_tm = [sbt(es, "a1_v%d" % i, [128, 128], BF16) for i in range(NR)]
        ko_tm = [sbt(es, "a1_ko%d" % i, [128, 128], BF16) for i in range(NR)]
        vo_tm = [sbt(es, "a1_vo%d" % i, [128, 128], BF16) for i in range(NR)]
        q_tm = [sbt(es, "a1_q%d" % i, [128, 128], BF16) for i in range(NR)]
        g_tm = [sbt(es, "a1_g%d" % i, [128, 128], F32) for i in range(NR)]
        qT = [sbt(es, "a1_qT%d" % i, [128, 128], BF16) for i in range(NR)]
        kT = [sbt(es, "a1_kT%d" % i, [128, 128], BF16) for i in range(NR)]
        Pm = [sbt(es, "a1_P%d" % i, [128, 128], BF16) for i in range(NR)]
        NT_ = 4
        rtmp = [sbt(es, "a1_rtmp%d" % i, [128, 384], F32) for i in range(NT_)]
        st = [sbt(es, "a1_st%d" % i, [128, 8], F32) for i in range(NR)]
        yn = [sbt(es, "a1_yn%d" % i, [128, 128], F32) for i in range(NR)]
        ret_tm = [sbt(es, "a1_ret%d" % i, [128, 128], BF16) for i in range(NR)]
        junk = sbt(es, "a1_junk", [128, 128], BF16)
        pA = [pst(es, "a1_pA%d" % i, [128, 512], F32) for i in range(3)]
        pS = pst(es, "a1_pS", [128, 512], F32)
        pO = pst(es, "a1_pO", [128, 512], F32)
        pKV = pst(es, "a1_pKV", [128, 512], F32)
        pT = pst(es, "a1_pT", [128, 1024], BF16)
        pT2 = pst(es, "a1_pT2", [128, 1024], BF16)

        dma(P, "sp", decA[:], c_decA[:, :], writes=["decA"])
        dma(P, "sp", decB[:], c_decB[:, :], writes=["decB"])
        dma(P, "sp", epsq[:], c_epsq[:, :], writes=["epsq"])
        dma(P, "sp", d2t[:].rearrange("p a b -> p (a b)"), c_d2t[:, :], writes=["d2t"])
        dma(P, "sp", gng_sb[:], gng[0:1, :].broadcast_to([128, 1024]), writes=["gng"])
        P.add("pool", lambda e: e.memset(U32[:].rearrange("p a b -> p (a b)"), 0.0), writes=[("U", h) for h in range(8)])
        g256 = [float(np.exp(256.0 * np.log1p(-2.0 ** (-5.0 - h)))) for h in range(8)]
        xk = [("xT", c) for c in range(16)]
        cnt = {"pai": 0, "rti": 0}

        def load_W(step):
            hh = step % 8
            for c in range(16):
                dma(P, "pool", Wg[step % 2][:, c, :], w_inv[:, c, hh * 512:(hh + 1) * 512], writes=[("W", step % 2, c)])

        iters = [(ps_, h, k) for ps_ in range(2) for h in range(8) for k in range(8)]
        ST = {}

        def S0(n):
            ps_, h, k = iters[n]
            step = ps_ * 8 + h
            wb = step % 2
            if k == 0:
                if h == 0:
                    for c in range(16):
                        dma(P, "pool", xTb[:, c, :], xTv[:, c, ps_ * 2048:(ps_ + 1) * 2048], writes=[("xT", c)])
                if step == 0:
                    load_W(0)
                if step + 1 < 16:
                    load_W(step + 1)
            wk = [("W", wb, c) for c in range(16)]
            ko, kw = 8 + k, k
            a0 = cnt["pai"] % 3; cnt["pai"] += 1
            a1 = cnt["pai"] % 3; cnt["pai"] += 1
            ST[n] = (a0, a1)
            mm_group(P, pA[a0][:, 0:256], [(xTb[:, c, ko * 128:(ko + 1) * 128], Wg[wb][:, c, 0:256]) for c in range(16)],
                     reads=xk + wk, writes=[("pA", a0)])
            mm_group(P, pA[a1][:, 0:512], [(xTb[:, c, kw * 128:(kw + 1) * 128], Wg[wb][:, c, :]) for c in range(16)],
                     reads=xk + wk, writes=[("pA", a1)])

        def S1(n):
            ps_, h, k = iters[n]
            j = ps_ * 8 + k
            r = n % NR
            a0, a1 = ST[n]
            tbo, tbw = tb_oth(j), tb_own(j)
            pa = pA[a0]; pak = ("pA", a0)
            ri = cnt["rti"] % NT_; cnt["rti"] += 1
            rt = rtmp[ri]; rtk = ("rtmp", ri)
            rope_ops(P, "dve", rt[:, 0:128], pa[:, 0:128], cosR[:, tbo, :], sinR[:, tbo, :], 64, rt[:, 128:384], (rtk, "t"),
                     reads=[pak, "cosR", "sinR"], writes=[rtk])
            P.add("pool", lambda e: e.tensor_scalar(out=ko_tm[r][:], in0=rt[:, 0:128], scalar1=decB[:, h:h + 1], scalar2=None,
                                                   op0=ALU.mult), reads=[rtk, "decB"], writes=[("ko", r)])
            P.add("act", lambda e: e.activation(out=vo_tm[r][:], in_=pa[:, 128:256], func=AF.Copy), reads=[pak], writes=[("vo", r)])
            pb = pA[a1]; pbk = ("pA", a1)
            ri = cnt["rti"] % NT_; cnt["rti"] += 1
            rt2 = rtmp[ri]; rtk2 = ("rtmp", ri)
            rope_ops(P, "dve", k_tm[r][:], pb[:, 0:128], cosR[:, tbw, :], sinR[:, tbw, :], 64, rt2[:], rtk2,
                     reads=[pbk, "cosR", "sinR"], writes=[("k", r)])
            ri = cnt["rti"] % NT_; cnt["rti"] += 1
            rt3 = rtmp[ri]; rtk3 = ("rtmp", ri)
            rope_ops(P, "dve", q_tm[r][:], pb[:, 256:384], cosR[:, tbw, :], sinR[:, tbw, :], 64, rt3[:], rtk3,
                     reads=[pbk, "cosR", "sinR"], writes=[("q", r)])
            P.add("act", lambda e: e.activation(out=v_tm[r][:], in_=pb[:, 128:256], func=AF.Copy), reads=[pbk], writes=[("v", r)])
            P.add("act", lambda e: e.activation(out=g_tm[r][:], in_=pb[:, 384:512], func=AF.Silu), reads=[pbk], writes=[("g", r)])
            P.add("pool", lambda e: e.tensor_scalar(out=kd_tm[r][:], in0=k_tm[r][:], scalar1=decA[:, h:h + 1], scalar2=None,
                                                   op0=ALU.mult), reads=[("k", r), "decA"], writes=[("kd", r)])

        def S2(n):
            ps_, h, k = iters[n]
            r = n % NR
            mm_group(P, pKV[:, 0:128], [(ko_tm[r][:], vo_tm[r][:])], reads=[("ko", r), ("vo", r)], writes=["pKV"])
            P.add("dve", lambda e: e.tensor_tensor(out=T32[r][:], in0=U32[:, h, :], in1=pKV[:, 0:128], op=ALU.add),
                  reads=["pKV", ("U", h)], writes=[("T32", r)])
            P.add("pool", lambda e: e.tensor_copy(out=Tb[r][:], in_=T32[r][:]), reads=[("T32", r)], writes=[("Tb", r)])
            mm_group(P, pKV[:, 128:256], [(kd_tm[r][:], v_tm[r][:])], reads=[("kd", r), ("v", r)], writes=["pKV"])
            P.add("dve", lambda e: e.scalar_tensor_tensor(out=U32[:, h, :], in0=T32[r][:], scalar=g256[h], in1=pKV[:, 128:256],
                                                         op0=ALU.mult, op1=ALU.add),
                  reads=["pKV", ("T32", r)], writes=[("U", h)])
            transposes(P, [(pT[:, 0:128], q_tm[r][:]), (pT[:, 128:256], k_tm[r][:])], ident[:],
                       reads=[("q", r), ("k", r), "ident"], writes=["pTqk"])
            P.add("act", lambda e: e.activation(out=qT[r][:], in_=pT[:, 0:128], func=AF.Copy), reads=["pTqk"], writes=[("qT", r)])
            P.add("dve", lambda e: e.tensor_copy(out=kT[r][:], in_=pT[:, 128:256]), reads=["pTqk"], writes=[("kT", r)])
            mm_group(P, pS[:, 0:128], [(kT[r][:], qT[r][:])], reads=[("kT", r), ("qT", r)], writes=["pS"])
            P.add("dve", lambda e: e.tensor_tensor(out=Pm[r][:], in0=pS[:, 0:128], in1=d2t[:, h, :], op=ALU.mult),
                  reads=["pS", "d2t"], writes=[("Pm", r)])

        def S3(n):
            ps_, h, k = iters[n]
            j = ps_ * 8 + k
            r = n % NR
            mm_group(P, pO[:, 0:128], [(Pm[r][:], v_tm[r][:]), (qT[r][:], Tb[r][:])],
                     reads=[("Pm", r), ("v", r), ("qT", r), ("Tb", r)], writes=["pO"])
            s = st[r]
            P.add("act", lambda e: e.activation(out=junk[:], in_=pO[:, 0:128], func=AF.Copy, accum_out=s[:, 0:1]),
                  reads=["pO"], writes=[("st0", r), "junk"])
            P.add("act", lambda e: e.activation(out=junk[:], in_=pO[:, 0:128], func=AF.Square, accum_out=s[:, 1:2]),
                  reads=["pO"], writes=[("st1", r), "junk"])
            P.add("dve", lambda e: e.tensor_scalar(out=s[:, 2:3], in0=s[:, 0:1], scalar1=1.0 / 128, scalar2=None, op0=ALU.mult),
                  reads=[("st0", r)], writes=[("stm", r)])
            P.add("dve", lambda e: e.tensor_tensor(out=s[:, 3:4], in0=s[:, 2:3], in1=s[:, 2:3], op=ALU.mult),
                  reads=[("stm", r)], writes=[("stm2", r)])
            P.add("dve", lambda e: e.scalar_tensor_tensor(out=s[:, 4:5], in0=s[:, 1:2], scalar=1.0 / 128, in1=s[:, 3:4],
                                                         op0=ALU.mult, op1=ALU.subtract),
                  reads=[("st1", r), ("stm2", r)], writes=[("stv", r)])
            P.add("dve", lambda e: e.tensor_tensor(out=s[:, 7:8], in0=s[:, 4:5], in1=epsq[:, h:h + 1], op=ALU.add),
                  reads=[("stv", r), "epsq"], writes=[("st2", r)])
            P.add("act", lambda e: e.activation(out=s[:, 5:6], in_=s[:, 7:8], func=AF.Sqrt), reads=[("st2", r)], writes=[("st3", r)])
            P.add("dve", lambda e: e.reciprocal(out=s[:, 6:7], in_=s[:, 5:6]), reads=[("st3", r)], writes=[("st4", r)])
            P.add("dve", lambda e: e.tensor_scalar(out=yn[r][:], in0=pO[:, 0:128], scalar1=s[:, 2:3], scalar2=s[:, 6:7],
                                                  op0=ALU.subtract, op1=ALU.mult),
                  reads=["pO", ("stm", r), ("st4", r)], writes=[("yn", r)])
            P.add("pool", lambda e: e.tensor_tensor(out=yn[r][:], in0=yn[r][:], in1=gng_sb[:, h * 128:(h + 1) * 128], op=ALU.mult),
                  reads=[("yn", r), "gng"], writes=[("yn", r)])
            P.add("pool", lambda e: e.tensor_tensor(out=ret_tm[r][:], in0=yn[r][:], in1=g_tm[r][:], op=ALU.mult),
                  reads=[("yn", r), ("g", r)], writes=[("ret", r)])
            transposes(P, [(pT2[:, 0:128], ret_tm[r][:])], ident[:], reads=[("ret", r), "ident"], writes=["pTr"])
            P.add("act", lambda e: e.activation(out=mixR[:, h, j * 128:(j + 1) * 128], in_=pT2[:, 0:128], func=AF.Copy),
                  reads=["pTr"], writes=[("mixinT", h, j)])

        stages = [S0, S1, S2, S3]
        n_it = len(iters)
        for t in range(n_it + len(stages) - 1):
            for si_ in reversed(range(len(stages))):
                n = t - si_
                if 0 <= n < n_it:
                    stages[si_](n)
        P.emit()


def layer_norm_ops(P, y, ykey, st, stkey, junk, g_sb, b_sb, gkeys, outs):
    P.add("act", lambda e: e.activation(out=junk[:], in_=y, func=AF.Copy, accum_out=st[:, 0:1]),
          reads=[ykey], writes=[(stkey, 0), "junk"])
    P.add("dve", lambda e: e.tensor_scalar(out=st[:, 1:2], in0=st[:, 0:1], scalar1=-1.0 / D, scalar2=None, op0=ALU.mult),
          reads=[(stkey, 0)], writes=[(stkey, 1)])
    P.add("dve", lambda e: e.tensor_scalar(out=y, in0=y, scalar1=st[:, 1:2], scalar2=None, op0=ALU.add),
          reads=[ykey, (stkey, 1)], writes=[ykey])
    P.add("act", lambda e: e.activation(out=junk[:], in_=y, func=AF.Square, accum_out=st[:, 2:3]),
          reads=[ykey], writes=[(stkey, 2), "junk"])
    rstd_ops(P, st[:, 4:5], st[:, 2:3], 1.0 / D, EPS, st[:, 3:4], reads=[(stkey, 2)], writes=[(stkey, 4)], key=stkey)
    P.add("dve", lambda e: e.scalar_tensor_tensor(out=y, in0=y, scalar=st[:, 4:5], in1=g_sb[:], op0=ALU.mult, op1=ALU.mult),
          reads=[ykey, (stkey, 4)] + gkeys, writes=[ykey])
    P.add("pool", lambda e: e.tensor_tensor(out=y, in0=y, in1=b_sb[:], op=ALU.add), reads=[ykey] + gkeys, writes=[ykey])


def phase_C(C, nc, sbt, pst, ident, w_outv, x_own, ln1g, ln1b, mixR, mixM, y1s, x1Tsv):
    with contextlib.ExitStack() as es:
        P = Prog(nc)
        Wo = sbt(es, "c_wo", [128, 16, D], BF16)
        g_sb = sbt(es, "c_g", [128, D], F32)
        b_sb = sbt(es, "c_b", [128, D], F32)
        NR = 2
        xo = [sbt(es, "c_xo%d" % i, [128, D], F32) for i in range(NR)]
        y = [sbt(es, "c_y%d" % i, [128, D], F32) for i in range(NR)]
        x1b = [sbt(es, "c_x1b%d" % i, [128, D], BF16) for i in range(NR)]
        x1T = [sbt(es, "c_x1T%d" % i, [128, 16, 128], BF16) for i in range(NR)]
        st = [sbt(es, "c_st%d" % i, [128, 8], F32) for i in range(NR)]
        junk = sbt(es, "c_junk", [128, D], BF16)
        pC = [pst(es, "c_pC%d" % i, [128, 512], F32) for i in range(4)]
        pT = [pst(es, "c_pT%d" % i, [128, 1024], BF16) for i in range(2)]
        for c in range(16):
            dma(P, "pool", Wo[:, c, :], w_outv[:, c, :], writes=[("Wo", c)])
        dma(P, "sp", g_sb[:], ln1g[0:1, :].broadcast_to([128, D]), writes=["lng"])
        dma(P, "sp", b_sb[:], ln1b[0:1, :].broadcast_to([128, D]), writes=["lnb"])
        wk = [("Wo", c) for c in range(16)]

        def S0(j):
            r = j % NR
            dma(P, "sp", xo[r][:], x_own[j * 128:(j + 1) * 128, :], writes=[("xo", r)])
            for cg in range(4):
                mm_group(P, pC[cg][:, 0:512],
                         [((mixR if c < 8 else mixM)[:, c % 8, j * 128:(j + 1) * 128], Wo[:, c, cg * 512:(cg + 1) * 512])
                          for c in range(16)],
                         reads=[("mixinT", c, j) for c in range(16)] + wk, writes=[("pC", cg)])
                P.add("dve", (lambda cg=cg: lambda e: e.scalar_tensor_tensor(
                    out=y[r][:, cg * 512:(cg + 1) * 512], in0=xo[r][:, cg * 512:(cg + 1) * 512], scalar=ALPHA,
                    in1=pC[cg][:, 0:512], op0=ALU.mult, op1=ALU.add))(),
                    reads=[("xo", r), ("pC", cg)], writes=[("y", r)])

        def S1(j):
            r = j % NR
            layer_norm_ops(P, y[r][:], ("y", r), st[r], ("c_st", r), junk, g_sb, b_sb, ["lng", "lnb"], None)
            P.add("act", lambda e: e.activation(out=x1b[r][:], in_=y[r][:], func=AF.Copy), reads=[("y", r)], writes=[("x1b", r)])
            P.add("pool", lambda e: e.tensor_scalar(out=y[r][:], in0=y[r][:], scalar1=ALPHA, scalar2=None, op0=ALU.mult),
                  reads=[("y", r), ("x1b", r)], writes=[("y", r)])
            dma(P, "sp", y1s[j * 128:(j + 1) * 128, :], y[r][:], reads=[("y", r)])

        def S2(j):
            r = j % NR
            for half in range(2):
                transposes(P, [(pT[half][:, c * 128:(c + 1) * 128], x1b[r][:, (half * 8 + c) * 128:(half * 8 + c + 1) * 128])
                               for c in range(8)], ident[:], reads=[("x1b", r), "ident"], writes=[("pT", half)])
                xout = x1T[r][:, half * 8:(half + 1) * 8, :]
                xin = pT[half][:, 0:1024].rearrange("p (c t) -> p c t", c=8)
                if half == 0:
                    P.add("act", (lambda xout=xout, xin=xin: lambda e: e.activation(out=xout, in_=xin, func=AF.Copy))(),
                          reads=[("pT", half)], writes=[("x1T", r, half)])
                else:
                    P.add("dve", (lambda xout=xout, xin=xin: lambda e: e.tensor_copy(out=xout, in_=xin))(),
                          reads=[("pT", half)], writes=[("x1T", r, half)])
            dma(P, "sp", x1Tsv[:, :, j * 128:(j + 1) * 128], x1T[r][:], reads=[("x1T", r, 0), ("x1T", r, 1)])

        stages = [S0, S1, S2]
        if not PIPE_C:
            for j in range(16):
                for st_ in stages:
                    st_(j)
        else:
            for t in range(16 + len(stages) - 1):
                for si_ in reversed(range(len(stages))):
                    n = t - si_
                    if 0 <= n < 16:
                        stages[si_](n)
        P.emit()


def phase_D(C, nc, sbt, pst, w_upv, w_downv, ln2g, ln2b, y1s, x1Tsv, out):
    with contextlib.ExitStack() as es:
        P = Prog(nc)
        x1Tg = sbt(es, "d_x1T", [128, 16, 1024], BF16)
        acc = sbt(es, "d_acc", [128, 8, D], F32)
        Wup = [sbt(es, "d_wup%d" % i, [128, 16, 512], BF16) for i in range(2)]
        Wdn = [sbt(es, "d_wdn%d" % i, [128, 4, D], BF16) for i in range(2)]
        hT = [sbt(es, "d_hT%d" % i, [128, 4, 1024], BF16) for i in range(2)]
        rt = [sbt(es, "d_rt%d" % i, [128, 512], F32) for i in range(2)]
        g_sb = sbt(es, "d_g", [128, D], F32)
        b_sb = sbt(es, "d_b", [128, D], F32)
        st = [sbt(es, "d_st%d" % i, [128, 8], F32) for i in range(2)]
        junk = sbt(es, "d_junk", [128, D], BF16)
        pU = [pst(es, "d_pU%d" % i, [128, 512], F32) for i in range(3)]
        pD = [pst(es, "d_pD%d" % i, [128, 512], F32) for i in range(4)]
        dma(P, "sp", g_sb[:], ln2g[0:1, :].broadcast_to([128, D]), writes=["lng"])
        dma(P, "sp", b_sb[:], ln2b[0:1, :].broadcast_to([128, D]), writes=["lnb"])
        wi = 0
        ui = 0
        di = 0

        def load_FW(step):
            f = step % 16
            b = step % 2
            for c in range(16):
                dma(P, "pool", Wup[b][:, c, :], w_upv[:, c, f * 512:(f + 1) * 512], writes=[("Wup", b, c)])
            for c in range(4):
                dma(P, "pool", Wdn[b][:, c, :], w_downv[:, f * 4 + c, :], writes=[("Wdn", b, c)])
        for grp in range(2):
            for c in range(16):
                dma(P, "sp", x1Tg[:, c, :], x1Tsv[:, c, grp * 1024:(grp + 1) * 1024], writes=[("x1T", c)])
            for tb in range(8):
                row0 = (grp * 8 + tb) * 128
                dma(P, "sp", acc[:, tb, :], y1s[row0:row0 + 128, :], writes=[("acc", tb)])
            xk = [("x1T", c) for c in range(16)]
            for fc in range(16):
                wb = wi % 2
                if wi == 0:
                    load_FW(0)
                if wi + 1 < 32:
                    load_FW(wi + 1)
                wi += 1
                wuk = [("Wup", wb, c) for c in range(16)]
                for fs in range(4):
                    for tg in range(2):
                        pu = pU[ui % 3]; puk = ("pU", ui % 3)
                        rr = ui % 2
                        ui += 1
                        mm_group(P, pu[:, 0:512],
                                 [(Wup[wb][:, c, fs * 128:(fs + 1) * 128], x1Tg[:, c, tg * 512:(tg + 1) * 512]) for c in range(16)],
                                 reads=xk + wuk, writes=[puk])
                        P.add("act", (lambda pu=pu, rr=rr: lambda e: e.activation(out=rt[rr][:], in_=pu[:, 0:512], func=AF.Relu))(),
                              reads=[puk], writes=[("rt", rr)])
                        P.add("act", (lambda rr=rr, wb=wb, fs=fs, tg=tg: lambda e: e.activation(
                            out=hT[wb][:, fs, tg * 512:(tg + 1) * 512], in_=rt[rr][:], func=AF.Square))(),
                            reads=[("rt", rr)], writes=[("hT", wb, fs, tg)])
                for tb in range(8):
                    tg = tb // 4
                    for cg in range(4):
                        pd = pD[di % 4]; pdk = ("pD", di % 4)
                        di += 1
                        mm_group(P, pd[:, 0:512],
                                 [(hT[wb][:, fs, tb * 128:(tb + 1) * 128], Wdn[wb][:, fs, cg * 512:(cg + 1) * 512]) for fs in range(4)],
                                 reads=[("hT", wb, fs, tg) for fs in range(4)] + [("Wdn", wb, fs) for fs in range(4)], writes=[pdk])
                        P.add("dve", (lambda pd=pd, tb=tb, cg=cg: lambda e: e.tensor_tensor(
                            out=acc[:, tb, cg * 512:(cg + 1) * 512], in0=acc[:, tb, cg * 512:(cg + 1) * 512], in1=pd[:, 0:512],
                            op=ALU.add))(),
                            reads=[pdk, ("acc", tb)], writes=[("acc", tb)])
            for tb in range(8):
                row0 = (grp * 8 + tb) * 128
                layer_norm_ops(P, acc[:, tb, :], ("acc", tb), st[tb % 2], ("d_st", tb % 2), junk, g_sb, b_sb, ["lng", "lnb"], None)
                dma(P, "sp", out[row0:row0 + 128, :], acc[:, tb, :], reads=[("acc", tb)])
        P.emit()


def _consts(p):
    c = {}
    c["c_ident"] = np.eye(128, dtype=np.float32)
    inv64 = (np.float32(10000.0) ** (-np.arange(0, 128, 2, dtype=np.float32) / np.float32(128))).astype(np.float32)
    inv32 = (np.float32(10000.0) ** (-np.arange(0, 64, 2, dtype=np.float32) / np.float32(64))).astype(np.float32)
    c["c_inv64"] = np.ascontiguousarray(np.broadcast_to(inv64[None, :], (128, 64)))
    c["c_inv32"] = np.ascontiguousarray(np.broadcast_to(inv32[None, :], (128, 32)))
    t = np.arange(128, dtype=np.float64)
    decA = np.zeros((128, 8)); decB = np.zeros((128, 8)); epsq = np.zeros((128, 8)); d2t = np.zeros((128, 8, 128))
    sc = 128.0 ** -0.5
    ii = t[None, :]
    jj = t[:, None]
    ci = np.floor(ii / 64); cj = np.floor(jj / 64)
    for h in range(8):
        lg = np.log1p(-2.0 ** (-5.0 - h))
        decA[:, h] = np.exp(lg * (255 - t)) * sc
        decB[:, h] = np.exp(lg * (127 - t)) * sc
        qdec = np.exp(lg * (t + 1))
        epsq[:, h] = EPS / qdec ** 2
        dd = np.where(cj == ci, np.exp(lg * np.abs(ii - jj)), np.where(cj < ci, np.exp(lg * (ii - jj)), 0.0))
        d2t[:, h, :] = dd * sc / qdec[None, :]
    c["c_decA"] = decA.astype(np.float32)
    c["c_decB"] = decB.astype(np.float32)
    c["c_epsq"] = epsq.astype(np.float32)
    c["c_d2t"] = d2t.reshape(128, 1024).astype(np.float32)
    dk = np.zeros((1, 128), np.float32); dk[0, 64:] = BIG
    dq = np.zeros((1, 512), np.float32); dq[0, :64] = -1.0
    c["c_dk"] = dk
    c["c_dq"] = dq
    return c


def _col_perm():
    cols = []
    for h in range(8):
        cols += list(range(1024 + h * 128, 1024 + (h + 1) * 128))
        cols += list(range(2048 + h * 128, 2048 + (h + 1) * 128))
        cols += list(range(0 + h * 128, (h + 1) * 128))
        cols += list(range(3072 + h * 128, 3072 + (h + 1) * 128))
    cols += list(range(4096 + 768, 4096 + 768 + 512))
    cols += list(range(4096 + 768 + 512, 5440))
    cols += list(range(4096, 4096 + 768))
    return np.array(cols)


def prep_inputs(inputs):
    x = np.asarray(inputs["x"], np.float32)
    pos = np.asarray(inputs["positions"], np.int32)
    w_in = np.ascontiguousarray(np.asarray(inputs["w_in"], np.float32)[0][:, _col_perm()])
    shared = {
        "w_in": w_in,
        "w_uq": np.ascontiguousarray(inputs["w_uq"][0], np.float32),
        "w_uk": np.ascontiguousarray(inputs["w_uk"][0], np.float32),
        "w_uv": np.ascontiguousarray(inputs["w_uv"][0], np.float32),
        "w_out": np.ascontiguousarray(inputs["w_out"][0], np.float32),
        "w_up": np.ascontiguousarray(inputs["w_up"][0], np.float32),
        "w_down": np.ascontiguousarray(inputs["w_down"][0], np.float32),
        "qg": np.ascontiguousarray(np.asarray(inputs["q_norm_g"][0], np.float32).reshape(6, 128).T),
        "kvg": np.ascontiguousarray(np.asarray(inputs["kv_norm_g"][0], np.float32).reshape(4, 128).T),
        "gng": np.asarray(inputs["ret_gn_g"][0], np.float32).reshape(1, 1024),
        "ln1g": np.asarray(inputs["ln1_g"][0], np.float32).reshape(1, D),
        "ln1b": np.asarray(inputs["ln1_b"][0], np.float32).reshape(1, D),
        "ln2g": np.asarray(inputs["ln2_g"][0], np.float32).reshape(1, D),
        "ln2b": np.asarray(inputs["ln2_b"][0], np.float32).reshape(1, D),
    }
    in_maps = []
    metas = []
    for core in range(8):
        b, p = core // 2, core % 2
        own_g = [2 * j + p for j in range(16)]
        oth_g = [(2 * j - 1) if p == 0 else (2 * j) for j in range(16)]
        blocks = [None] * 32
        for j in range(16):
            blocks[tb_own(j)] = own_g[j]
            blocks[tb_oth(j)] = oth_g[j]
        xb = x[b]
        xTc = np.zeros((D, SEQ), np.float32)
        pos_tm = np.zeros((128, 32), np.int32)
        km = np.zeros((128, 32), np.float32)
        for tb, g in enumerate(blocks):
            if g < 0:
                km[:, tb] = BIG
                continue
            xTc[:, tb * 128:(tb + 1) * 128] = xb[g * 128:(g + 1) * 128, :].T
            pos_tm[:, tb] = pos[b, g * 128:(g + 1) * 128]
        x_own = np.concatenate([xb[g * 128:(g + 1) * 128, :] for g in own_g], axis=0)
        m = dict(shared)
        m.update(_consts(p))
        m["xT"] = xTc
        m["x_own"] = np.ascontiguousarray(x_own)
        m["pos_tm"] = pos_tm
        m["kmask"] = km
        in_maps.append(m)
        metas.append((b, own_g))
    return in_maps, metas


def kernel(**inputs):
    in_maps, metas = prep_inputs(inputs)
    nc = build()
    res = run_bass_kernel_spmd(nc, in_maps, core_ids=list(range(8)))
    outp = np.zeros((4, SEQ, D), np.float32)
    for core in range(8):
        b, own_g = metas[core]
        o = np.asarray(res.results[core]["out"], np.float32)
        for j, g in enumerate(own_g):
            outp[b, g * 128:(g + 1) * 128, :] = o[j * 128:(j + 1) * 128, :]
    return outp
```

```python
import contextlib
import numpy as np
import ml_dtypes
import concourse.bass as bass
import concourse.mybir as mybir
from concourse.bass_utils import run_bass_kernel_spmd

F32 = mybir.dt.float32
BF16 = mybir.dt.bfloat16
I32 = mybir.dt.int32
AF = mybir.ActivationFunctionType
ALU = mybir.AluOpType

D = 2048
SEQ = 4096
NOWN = 2048
H = 8
DFF = 8192
EPS = 1e-5
ALPHA = 2.0 ** 0.25
IN_W = 5440
BIG = 30000.0
MAGIC = 12582912.0
TWO_PI = 6.283185307179586
C1 = 6.28125
C2 = TWO_PI - C1
SCALE = 192.0 ** -0.5
ENGS = ("pe", "act", "dve", "pool", "sp")
PIPE_B = 2
PIPE_C = False


def tb_own(j):
    return (j // 8) * 16 + j % 8


def tb_oth(j):
    return (j // 8) * 16 + 8 + j % 8


def _is_psum_key(k):
    n = k if isinstance(k, str) else k[0]
    return isinstance(n, str) and len(n) > 1 and n[0] == "p" and n[1].isupper()


class Op:
    __slots__ = ("eng", "fn", "reads", "writes", "dma", "deps", "idx", "sig", "dsem", "dval")


class Prog:
    sems = None

    def __init__(self, nc, same_engine_sync=True, dma_ring=8):
        self.nc = nc
        self.ops = []
        self.same_engine_sync = same_engine_sync
        self.dma_ring = dma_ring
        self.last_writer = {}
        self.readers = {}

    def add(self, eng, fn, reads=(), writes=(), dma=False):
        op = Op()
        ps_r = [k for k in reads if _is_psum_key(k)]
        if ps_r:
            reads = [k for k in reads if not _is_psum_key(k)]
            writes = list(writes) + [k for k in ps_r if k not in writes]
        op.eng, op.fn, op.reads, op.writes, op.dma = eng, fn, tuple(reads), tuple(writes), dma
        op.sig = None
        op.dsem = None
        op.dval = 0
        op.idx = len(self.ops)
        deps = set()
        for k in op.reads:
            w = self.last_writer.get(k)
            if w is not None:
                deps.add(w)
        for k in op.writes:
            w = self.last_writer.get(k)
            if w is not None:
                deps.add(w)
            for r in self.readers.get(k, {}).values():
                deps.add(r)
        deps.discard(op.idx)
        op.deps = sorted(deps)
        for k in op.reads:
            d = self.readers.setdefault(k, {})
            d[(eng, op.idx) if dma else eng] = op.idx
        for k in op.writes:
            self.last_writer[k] = op.idx
            self.readers[k] = {}
        self.ops.append(op)
        return op

    def emit(self):
        nc = self.nc
        ops = self.ops
        ses = self.same_engine_sync
        need = [False] * len(ops)
        for op in ops:
            for d in op.deps:
                dop = ops[d]
                if dop.dma:
                    continue
                if dop.eng == op.eng and not op.dma and (dop.eng == "pe" or not ses):
                    continue
                need[d] = True
        G = self.sems
        cnt = G["cnt"]
        dcount = G["dcount"]
        base_cnt = dict(cnt)
        base_d = dict(dcount)
        per_eng = {e: [] for e in ENGS}
        for op in ops:
            if op.dma:
                n = dcount[op.eng]
                dcount[op.eng] += 1
                op.dsem = n % self.dma_ring
                op.dval = 16 * (n // self.dma_ring + 1)
            elif need[op.idx]:
                cnt[op.eng] += 1
                op.sig = cnt[op.eng]
            per_eng[op.eng].append(op)
        with contextlib.ExitStack() as es:
            csem = G["csem"]
            dsem = G["dsem"]
            block = es.enter_context(nc.Block())

            def run_engine(e, handle):
                waited = {}
                for e2 in ENGS:
                    waited[("c", e2)] = base_cnt[e2]
                    n0 = base_d[e2]
                    for r in range(self.dma_ring):
                        uses = (n0 - 1 - r) // self.dma_ring + 1 if n0 > r else 0
                        waited[("d", e2, r)] = 16 * uses

                def wait(key, sem, val):
                    if waited.get(key, 0) >= val:
                        return
                    handle.wait_ge(sem, val)
                    waited[key] = val

                for op in per_eng[e]:
                    for d in op.deps:
                        dop = ops[d]
                        if dop.dma:
                            wait(("d", dop.eng, dop.dsem), dsem[dop.eng][dop.dsem], dop.dval)
                        else:
                            if dop.eng == e and not op.dma and (e == "pe" or not ses):
                                continue
                            wait(("c", dop.eng), csem[dop.eng], dop.sig)
                    if op.dma:
                        if op.dval > 16:
                            wait(("d", e, op.dsem), dsem[e][op.dsem], op.dval - 16)
                        ins = op.fn(handle)
                        ins.then_inc(dsem[e][op.dsem], 16)
                    else:
                        ins = op.fn(handle)
                        if op.sig is not None:
                            ins.then_inc(csem[e], 1)
                n = dcount[e]
                for r in range(min(n, self.dma_ring)):
                    uses = (n - 1 - r) // self.dma_ring + 1
                    wait(("d", e, r), dsem[e][r], 16 * uses)

            if per_eng["sp"]:
                @block.sync
                def _(h):
                    run_engine("sp", h)
            if per_eng["pool"]:
                @block.gpsimd
                def _(h):
                    run_engine("pool", h)
            if per_eng["act"]:
                @block.scalar
                def _(h):
                    run_engine("act", h)
            if per_eng["dve"]:
                @block.vector
                def _(h):
                    run_engine("dve", h)
            if per_eng["pe"]:
                @block.tensor
                def _(h):
                    run_engine("pe", h)


def dma(P, q, out, in_, reads=(), writes=()):
    P.add(q, lambda e: e.dma_start(out=out, in_=in_), reads=reads, writes=writes, dma=True)


def mm_group(P, out, pairs, reads, writes):
    def fn(e):
        n = len(pairs)
        ins = None
        for i, (l, r) in enumerate(pairs):
            ins = e.matmul(out, lhsT=l, rhs=r, start=(i == 0), stop=(i == n - 1))
        return ins
    P.add("pe", fn, reads=reads, writes=writes)


def transposes(P, items, ident, reads, writes):
    def fn(e):
        ins = None
        for o, i in items:
            ins = e.transpose(out=o, in_=i, identity=ident)
        return ins
    P.add("pe", fn, reads=reads, writes=writes)


def rope_ops(P, eng, dst, src, cos, sin, half, tmp, tmpkey, reads, writes, eng2=None):
    h = half

    def f1(e):
        e.tensor_tensor(out=tmp[:, 0:h], in0=src[:, 0:h], in1=cos, op=ALU.mult)
        e.tensor_tensor(out=tmp[:, h:2 * h], in0=src[:, h:2 * h], in1=sin, op=ALU.mult)
        e.tensor_tensor(out=tmp[:, 2 * h:3 * h], in0=src[:, 0:h], in1=sin, op=ALU.mult)
        return e.tensor_tensor(out=tmp[:, 3 * h:4 * h], in0=src[:, h:2 * h], in1=cos, op=ALU.mult)

    def f2(e):
        e.tensor_tensor(out=dst[:, 0:h], in0=tmp[:, 0:h], in1=tmp[:, h:2 * h], op=ALU.subtract)
        return e.tensor_tensor(out=dst[:, h:2 * h], in0=tmp[:, 2 * h:3 * h], in1=tmp[:, 3 * h:4 * h], op=ALU.add)
    P.add(eng, f1, reads=reads, writes=[tmpkey])
    P.add(eng2 or eng, f2, reads=[tmpkey], writes=writes)


def rstd_ops(P, out, ssum, scale, eps_ap_or_float, tmp, reads, writes, key):
    def f1(e):
        if isinstance(eps_ap_or_float, float):
            return e.tensor_scalar(out=tmp, in0=ssum, scalar1=scale, scalar2=eps_ap_or_float, op0=ALU.mult, op1=ALU.add)
        return e.tensor_scalar(out=tmp, in0=ssum, scalar1=scale, scalar2=eps_ap_or_float, op0=ALU.mult, op1=ALU.add)
    k = ("rstd_tmp", key)
    P.add("dve", f1, reads=reads, writes=[k])
    P.add("act", lambda e: e.activation(out=tmp, in_=tmp, func=AF.Sqrt), reads=[k], writes=[k])
    P.add("dve", lambda e: e.reciprocal(out=out, in_=tmp), reads=[k], writes=writes)


class Ctx:
    pass


def build(debug=False, phases="0 A2 B A1 C D"):
    phases = phases.split()
    nc = bass.Bass("TRN2", target_bir_lowering=False)
    C = Ctx()
    C.nc = nc
    C.debug = debug

    def din(name, shape, dt=F32):
        return nc.dram_tensor(name, list(shape), dt, kind="ExternalInput").ap()

    xT = din("xT", [D, SEQ])
    x_own = din("x_own", [NOWN, D])
    pos_tm = din("pos_tm", [128, 32], I32)
    kmask = din("kmask", [128, 32])
    w_in = din("w_in", [D, IN_W])
    w_uq = din("w_uq", [768, 1536])
    w_uk = din("w_uk", [512, 1024])
    w_uv = din("w_uv", [512, 1024])
    w_out = din("w_out", [D, D])
    w_up = din("w_up", [D, DFF])
    w_down = din("w_down", [DFF, D])
    qg = din("qg", [128, 6])
    kvg = din("kvg", [128, 4])
    gng = din("gng", [1, 1024])
    ln1g = din("ln1g", [1, D])
    ln1b = din("ln1b", [1, D])
    ln2g = din("ln2g", [1, D])
    ln2b = din("ln2b", [1, D])
    c_ident = din("c_ident", [128, 128])
    c_inv64 = din("c_inv64", [128, 64])
    c_inv32 = din("c_inv32", [128, 32])
    c_decA = din("c_decA", [128, 8])
    c_decB = din("c_decB", [128, 8])
    c_epsq = din("c_epsq", [128, 8])
    c_d2t = din("c_d2t", [128, 8 * 128])
    c_dk = din("c_dk", [1, 128])
    c_dq = din("c_dq", [1, 512])
    out = nc.dram_tensor("out", [NOWN, D], F32, kind="ExternalOutput").ap()
    y1s = nc.dram_tensor("y1s", [NOWN, D], F32).ap()
    x1Ts = nc.dram_tensor("x1Ts", [D, NOWN], BF16).ap()
    dbg = {}
    if debug:
        dbg["mixR"] = nc.dram_tensor("dbg_mixR", [128, 8 * NOWN], F32, kind="ExternalOutput").ap()
        dbg["mixM"] = nc.dram_tensor("dbg_mixM", [128, 8 * NOWN], F32, kind="ExternalOutput").ap()
        dbg["cqnT"] = nc.dram_tensor("dbg_cqnT", [128, 6 * NOWN], F32, kind="ExternalOutput").ap()
        dbg["ckvnT"] = nc.dram_tensor("dbg_ckvnT", [128, 4 * SEQ], F32, kind="ExternalOutput").ap()
        dbg["krT"] = nc.dram_tensor("dbg_krT", [65, SEQ], F32, kind="ExternalOutput").ap()
        dbg["y1"] = y1s

    xTv = xT.rearrange("(c p) t -> p c t", p=128)
    w_inv = w_in.rearrange("(c p) n -> p c n", p=128)
    w_uqv = w_uq.rearrange("(c p) n -> p c n", p=128)
    w_ukv = w_uk.rearrange("(c p) n -> p c n", p=128)
    w_uvv = w_uv.rearrange("(c p) n -> p c n", p=128)
    w_outv = w_out.rearrange("(c p) n -> p c n", p=128)
    w_upv = w_up.rearrange("(c p) n -> p c n", p=128)
    w_downv = w_down.rearrange("(c p) n -> p c n", p=128)
    x1Tsv = x1Ts.rearrange("(c p) t -> p c t", p=128)

    def sbt(es, name, shape, dt):
        return es.enter_context(nc.sbuf_tensor(name, list(shape), dt))

    def pst(es, name, shape, dt):
        return es.enter_context(nc.psum_tensor(name, list(shape), dt))

    def dump(P, es, name, src_tile, ncols, src_key, dt=BF16, parts=128):
        stg = sbt(es, "dstg_" + name, [parts, 2048], F32)
        flat = src_tile
        for c0 in range(0, ncols, 2048):
            n = min(2048, ncols - c0)
            P.add("dve", (lambda c0=c0, n=n: lambda e: e.tensor_copy(out=stg[:, 0:n], in_=flat[:, c0:c0 + n]))(),
                  reads=[src_key], writes=[("dstg", name)])
            dma(P, "sp", dbg[name][:, c0:c0 + n], stg[:, 0:n], reads=[("dstg", name)])

    with contextlib.ExitStack() as g_es:
        Prog.sems = {
            "cnt": {e: 0 for e in ENGS}, "dcount": {e: 0 for e in ENGS},
            "csem": {e: g_es.enter_context(nc.semaphore("c_" + e)) for e in ENGS},
            "dsem": {e: [g_es.enter_context(nc.semaphore("d_%s_%d" % (e, i))) for i in range(8)] for e in ("sp", "pool")},
        }
        ident = sbt(g_es, "ident", [128, 128], BF16)
        with contextlib.ExitStack() as s1:
            mixM = sbt(s1, "mixM", [128, 8, NOWN], BF16)
            posf = sbt(s1, "posf", [128, 32], F32)
            kmask_sb = sbt(s1, "kmask_sb", [128, 32], F32)
            s2 = contextlib.ExitStack()
            cosM = sbt(s2, "cosM", [128, 32, 32], F32)
            sinM = sbt(s2, "sinM", [128, 32, 32], F32)

            def rope_tables(P, es, inv_dram, nf, cos_t, sin_t, tag):
                n = 32 * nf
                inv = sbt(es, "inv" + tag, [128, nf], F32)
                ang = sbt(es, "ang" + tag, [128, 32, nf], F32)
                kk = sbt(es, "kk" + tag, [128, 32, nf], F32)
                dma(P, "sp", inv[:], inv_dram[:, :], writes=["inv" + tag])
                angf = ang[:].rearrange("p a b -> p (a b)")
                kkf = kk[:].rearrange("p a b -> p (a b)")
                cosf = cos_t[:].rearrange("p a b -> p (a b)")
                sinf = sin_t[:].rearrange("p a b -> p (a b)")

                def f_ang(e):
                    ins = None
                    for tb in range(32):
                        ins = e.tensor_scalar(out=ang[:, tb, :], in0=inv[:], scalar1=posf[:, tb:tb + 1], scalar2=None,
                                              op0=ALU.mult)
                    return ins
                P.add("dve", f_ang, reads=["inv" + tag, "posf"], writes=["ang" + tag])
                ka = "ang" + tag
                kkk = "kk" + tag
                P.add("dve", lambda e: e.tensor_scalar(out=kkf, in0=angf, scalar1=1.0 / TWO_PI, scalar2=None, op0=ALU.mult),
                      reads=[ka], writes=[kkk])
                P.add("dve", lambda e: e.tensor_scalar(out=kkf, in0=kkf, scalar1=MAGIC, scalar2=None, op0=ALU.add),
                      reads=[kkk], writes=[kkk])
                P.add("dve", lambda e: e.tensor_scalar(out=kkf, in0=kkf, scalar1=-MAGIC, scalar2=None, op0=ALU.add),
                      reads=[kkk], writes=[kkk])
                P.add("dve", lambda e: e.scalar_tensor_tensor(out=angf, in0=kkf, scalar=-C1, in1=angf, op0=ALU.mult, op1=ALU.add),
                      reads=[kkk, ka], writes=[ka])
                P.add("dve", lambda e: e.scalar_tensor_tensor(out=angf, in0=kkf, scalar=-C2, in1=angf, op0=ALU.mult, op1=ALU.add),
                      reads=[kkk, ka], writes=[ka])
                P.add("dve", lambda e: e.tensor_scalar(out=angf, in0=angf, scalar1=3.1415925, scalar2=-3.1415925,
                                                      op0=ALU.min, op1=ALU.max), reads=[ka], writes=[ka])
                P.add("act", lambda e: e.activation(out=sinf, in_=angf, func=AF.Sin), reads=[ka], writes=["sin" + tag])
                P.add("act", lambda e: e.activation(out=kkf, in_=angf, func=AF.Sin, scale=0.5), reads=[ka], writes=[kkk])
                P.add("dve", lambda e: e.tensor_tensor(out=kkf, in0=kkf, in1=kkf, op=ALU.mult), reads=[kkk], writes=[kkk])
                P.add("dve", lambda e: e.tensor_scalar(out=cosf, in0=kkf, scalar1=-2.0, scalar2=1.0, op0=ALU.mult, op1=ALU.add),
                      reads=[kkk], writes=["cos" + tag])

            if "0" in phases:
                with contextlib.ExitStack() as es:
                    P = Prog(nc)
                    posi = sbt(es, "posi", [128, 32], I32)
                    dma(P, "pool", ident[:], c_ident[:, :], writes=["ident"])
                    dma(P, "sp", posi[:], pos_tm[:, :], writes=["posi"])
                    dma(P, "sp", kmask_sb[:], kmask[:, :], writes=["kmask"])
                    P.add("dve", lambda e: e.tensor_copy(out=posf[:], in_=posi[:]), reads=["posi"], writes=["posf"])
                    rope_tables(P, es, c_inv32, 32, cosM, sinM, "M")
                    P.emit()

            with s2:
                cqnT = sbt(s2, "cqnT", [128, 6, NOWN], BF16)
                ckvnT = sbt(s2, "ckvnT", [128, 4, SEQ], BF16)
                kraugT = sbt(s2, "kraugT", [65, SEQ], BF16)
                if "A2" in phases:
                    with nc.named_scope("A2"):
                        phase_A2(C, nc, sbt, pst, ident, xTv, w_inv, cosM, sinM, kmask_sb, cqnT, ckvnT, kraugT)
                    if debug:
                        with contextlib.ExitStack() as es:
                            P = Prog(nc)
                            dump(P, es, "cqnT", cqnT[:].rearrange("p a b -> p (a b)"), 6 * NOWN, "x")
                            dump(P, es, "ckvnT", ckvnT[:].rearrange("p a b -> p (a b)"), 4 * SEQ, "x")
                            stg = sbt(es, "dstg_kr", [65, SEQ], F32)
                            P.add("dve", lambda e: e.tensor_copy(out=stg[:], in_=kraugT[:]), writes=["stgkr"])
                            dma(P, "sp", dbg["krT"][:, :], stg[:], reads=["stgkr"])
                            P.emit()
                if "B" in phases:
                    with nc.named_scope("B"):
                        phase_B(C, nc, sbt, pst, ident, w_uqv, w_ukv, w_uvv, qg, kvg, c_dk, c_dq, cosM, sinM,
                                cqnT, ckvnT, kraugT, mixM)
            mixR = sbt(s1, "mixR", [128, 8, NOWN], BF16)
            if "A1" in phases:
                with nc.named_scope("A1"):
                    phase_A1(C, nc, sbt, pst, ident, xTv, w_inv, c_inv64, posf, c_decA, c_decB, c_epsq, c_d2t, gng,
                             mixR, rope_tables)
            if debug and ("A1" in phases or "B" in phases):
                with contextlib.ExitStack() as es:
                    P = Prog(nc)
                    if "A1" in phases:
                        dump(P, es, "mixR", mixR[:].rearrange("p a b -> p (a b)"), 8 * NOWN, "x")
                    if "B" in phases:
                        dump(P, es, "mixM", mixM[:].rearrange("p a b -> p (a b)"), 8 * NOWN, "x")
                    P.emit()
            if "C" in phases:
                with nc.named_scope("C"):
                    phase_C(C, nc, sbt, pst, ident, w_outv, x_own, ln1g, ln1b, mixR, mixM, y1s, x1Tsv)
        if "D" in phases:
            with nc.named_scope("D"):
                phase_D(C, nc, sbt, pst, w_upv, w_downv, ln2g, ln2b, y1s, x1Tsv, out)
    return nc


def phase_A2(C, nc, sbt, pst, ident, xTv, w_inv, cosM, sinM, kmask_sb, cqnT, ckvnT, kraugT):
    with contextlib.ExitStack() as es:
        P = Prog(nc)
        xTb = sbt(es, "a2_xT", [128, 16, 1024], BF16)
        Wg8 = sbt(es, "a2_w8", [128, 16, 512], BF16)
        Wg9 = sbt(es, "a2_w9", [128, 16, 448], BF16)
        Wg10 = sbt(es, "a2_w10", [128, 16, 384], BF16)
        junk = sbt(es, "a2_junk", [128, 512], BF16)
        NR = 2
        ss = [sbt(es, "a2_ss%d" % i, [128, 8], F32) for i in range(NR)]
        ckvn_tm = [sbt(es, "a2_ckvn%d" % i, [128, 512], BF16) for i in range(NR)]
        cq_raw = [sbt(es, "a2_cqraw%d" % i, [128, 768], F32) for i in range(NR)]
        cqn_tm = [sbt(es, "a2_cqn%d" % i, [128, 768], BF16) for i in range(NR)]
        kaug = [sbt(es, "a2_kaug%d" % i, [128, 65], BF16) for i in range(NR)]
        rtmp = [sbt(es, "a2_rtmp%d" % i, [128, 128], F32) for i in range(NR)]
        pA = [pst(es, "a2_pA%d" % i, [128, 512], F32) for i in range(3)]
        pT0 = pst(es, "a2_pT0", [128, 1024], BF16)
        pT1 = pst(es, "a2_pT1", [128, 1024], BF16)
        pT2 = pst(es, "a2_pT2", [128, 1024], BF16)
        for c in range(16):
            dma(P, "pool", Wg8[:, c, :], w_inv[:, c, 4096:4608], writes=[("w8", c)])
        for c in range(16):
            dma(P, "pool", Wg9[:, c, :], w_inv[:, c, 4608:5056], writes=[("w9", c)])
        for c in range(16):
            dma(P, "pool", Wg10[:, c, :], w_inv[:, c, 5056:5440], writes=[("w10", c)])
        it = 0
        pai = 0
        import os
        QL = int(os.environ.get("A2_Q", "4"))
        KL = int(os.environ.get("A2_K", "8"))
        for q in range(QL):
            for c in range(16):
                dma(P, "pool", xTb[:, c, :], xTv[:, c, q * 1024:(q + 1) * 1024], writes=[("xT", c)])
            own = (q % 2 == 0)
            for k in range(KL):
                tb = q * 8 + k
                j = (q // 2) * 8 + k
                r = it % NR
                it += 1
                xk = [("xT", c) for c in range(16)]
                pa = pA[pai % 3]; pak = ("pA", pai % 3); pai += 1
                mm_group(P, pa[:, 0:512], [(xTb[:, c, k * 128:(k + 1) * 128], Wg8[:, c, :]) for c in range(16)],
                         reads=xk + [("w8", c) for c in range(16)], writes=[pak])
                P.add("act", (lambda pa=pa, r=r: lambda e: e.activation(out=junk[:], in_=pa[:, 0:512], func=AF.Square,
                                                                     accum_out=ss[r][:, 0:1]))(),
                      reads=[pak], writes=[("ss0", r), "junk"])
                rstd_ops(P, ss[r][:, 4:5], ss[r][:, 0:1], 1.0 / 512, EPS, ss[r][:, 3:4], reads=[("ss0", r)], writes=[("rs0", r)], key=("a2a", r))
                P.add("dve", (lambda pa=pa, r=r: lambda e: e.tensor_scalar(out=ckvn_tm[r][:], in0=pa[:, 0:512],
                                                                         scalar1=ss[r][:, 4:5], scalar2=None, op0=ALU.mult))(),
                      reads=[pak, ("rs0", r)], writes=[("ckvn", r)])
                transposes(P, [(pT0[:, c * 128:(c + 1) * 128], ckvn_tm[r][:, c * 128:(c + 1) * 128]) for c in range(4)],
                           ident[:], reads=[("ckvn", r), "ident"], writes=["pT0a"])
                P.add("act", (lambda tb=tb: lambda e: e.activation(
                    out=ckvnT[:, 0:4, tb * 128:(tb + 1) * 128],
                    in_=pT0[:, 0:512].rearrange("p (c t) -> p c t", c=4), func=AF.Copy))(),
                    reads=["pT0a"], writes=[("ckvnT", tb)])
                ncol = 448 if own else 64
                pa = pA[pai % 3]; pak = ("pA", pai % 3); pai += 1
                mm_group(P, pa[:, 0:ncol], [(xTb[:, c, k * 128:(k + 1) * 128], Wg9[:, c, 0:ncol]) for c in range(16)],
                         reads=xk + [("w9", c) for c in range(16)], writes=[pak])
                rope_ops(P, "dve", kaug[r], pa, cosM[:, tb, :], sinM[:, tb, :], 32, rtmp[r], ("rtmp", r),
                         reads=[pak, "cosM", "sinM"], writes=[("kaug", r)])
                P.add("dve", (lambda r=r, tb=tb: lambda e: e.tensor_copy(out=kaug[r][:, 64:65], in_=kmask_sb[:, tb:tb + 1]))(),
                      reads=["kmask"], writes=[("kaug", r)])
                transposes(P, [(pT2[0:65, 0:128], kaug[r][:, 0:65])], ident[:], reads=[("kaug", r), "ident"], writes=["pT2"])
                P.add("act", (lambda tb=tb: lambda e: e.activation(out=kraugT[0:65, tb * 128:(tb + 1) * 128],
                                                                 in_=pT2[0:65, 0:128], func=AF.Copy))(),
                      reads=["pT2"], writes=[("kraugT", tb)])
                if own:
                    P.add("dve", (lambda pa=pa, r=r: lambda e: e.tensor_copy(out=cq_raw[r][:, 0:384], in_=pa[:, 64:448]))(),
                          reads=[pak], writes=[("cqraw0", r)])
                    P.add("act", (lambda pa=pa, r=r: lambda e: e.activation(out=junk[:, 0:384], in_=pa[:, 64:448], func=AF.Square,
                                                                         accum_out=ss[r][:, 1:2]))(),
                          reads=[pak], writes=[("ss1", r), "junk"])
                    pa = pA[pai % 3]; pak = ("pA", pai % 3); pai += 1
                    mm_group(P, pa[:, 0:384], [(xTb[:, c, k * 128:(k + 1) * 128], Wg10[:, c, :]) for c in range(16)],
                             reads=xk + [("w10", c) for c in range(16)], writes=[pak])
                    P.add("dve", (lambda pa=pa, r=r: lambda e: e.tensor_copy(out=cq_raw[r][:, 384:768], in_=pa[:, 0:384]))(),
                          reads=[pak], writes=[("cqraw1", r)])
                    P.add("act", (lambda pa=pa, r=r: lambda e: e.activation(out=junk[:, 0:384], in_=pa[:, 0:384], func=AF.Square,
                                                                         accum_out=ss[r][:, 2:3]))(),
                          reads=[pak], writes=[("ss2", r), "junk"])
                    P.add("dve", (lambda r=r: lambda e: e.tensor_tensor(out=ss[r][:, 5:6], in0=ss[r][:, 1:2], in1=ss[r][:, 2:3],
                                                                      op=ALU.add))(),
                          reads=[("ss1", r), ("ss2", r)], writes=[("ss12", r)])
                    rstd_ops(P, ss[r][:, 7:8], ss[r][:, 5:6], 1.0 / 768, EPS, ss[r][:, 6:7], reads=[("ss12", r)], writes=[("rs1", r)], key=("a2b", r))
                    P.add("dve", (lambda r=r: lambda e: e.tensor_scalar(out=cqn_tm[r][:], in0=cq_raw[r][:], scalar1=ss[r][:, 7:8],
                                                                      scalar2=None, op0=ALU.mult))(),
                          reads=[("cqraw0", r), ("cqraw1", r), ("rs1", r)], writes=[("cqn", r)])
                    transposes(P, [(pT1[:, c * 128:(c + 1) * 128], cqn_tm[r][:, c * 128:(c + 1) * 128]) for c in range(6)],
                               ident[:], reads=[("cqn", r), "ident"], writes=["pT1"])
                    P.add("act", (lambda j=j: lambda e: e.activation(
                        out=cqnT[:, 0:6, j * 128:(j + 1) * 128],
                        in_=pT1[:, 0:768].rearrange("p (c t) -> p c t", c=6), func=AF.Copy))(),
                        reads=["pT1"], writes=[("cqnT", j)])
        P.emit()


def phase_B(C, nc, sbt, pst, ident, w_uqv, w_ukv, w_uvv, qg, kvg, c_dk, c_dq, cosM, sinM, cqnT, ckvnT, kraugT, mixM):
    with contextlib.ExitStack() as es:
        P = Prog(nc)
        qg_sb = sbt(es, "b_qg", [128, 6], F32)
        kvg_sb = sbt(es, "b_kvg", [128, 4], F32)
        dk = sbt(es, "b_dk", [1, 128], BF16)
        dq = sbt(es, "b_dq", [1, 512], BF16)
        wq_f = sbt(es, "b_wqf", [128, 6, 192], F32)
        wk_f = sbt(es, "b_wkf", [128, 4, 128], F32)
        wv_f = sbt(es, "b_wvf", [128, 4, 128], F32)
        NH = 2
        wq_b = [sbt(es, "b_wqb%d" % i, [128, 6, 192], BF16) for i in range(NH)]
        wk_b = [sbt(es, "b_wkb%d" % i, [128, 4, 128], BF16) for i in range(NH)]
        wv_b = [sbt(es, "b_wvb%d" % i, [128, 4, 128], BF16) for i in range(NH)]
        knT = [sbt(es, "b_knT%d" % i, [128, SEQ], BF16) for i in range(NH)]
        Vaug = [sbt(es, "b_V%d" % i, [128, 32, 129], BF16) for i in range(NH)]
        qnT = [sbt(es, "b_qnT%d" % i, [128, NOWN], BF16) for i in range(NH)]
        qrT = [sbt(es, "b_qrT%d" % i, [65, NOWN], BF16) for i in range(NH)]
        qtm = [sbt(es, "b_qtm%d" % i, [128, 193], BF16) for i in range(2)]
        rtmp = [sbt(es, "b_rtmp%d" % i, [128, 128], F32) for i in range(2)]
        PT = [sbt(es, "b_PT%d" % i, [128, 512], BF16) for i in range(3)]
        otm = [sbt(es, "b_otm%d" % i, [128, 128], BF16) for i in range(2)]
        rinv = [sbt(es, "b_rinv%d" % i, [128, 1], F32) for i in range(2)]
        pP = pst(es, "b_pP", [128, 512], F32)
        pS = [pst(es, "b_pS%d" % i, [128, 512], F32) for i in range(2)]
        pO = [pst(es, "b_pO%d" % i, [128, 512], F32) for i in range(4)]
        pT = pst(es, "b_pT", [128, 1024], BF16)

        dma(P, "sp", qg_sb[:], qg[:, :], writes=["qg"])
        dma(P, "sp", kvg_sb[:], kvg[:, :], writes=["kvg"])
        dma(P, "pool", dk[:], c_dk[:, :], writes=["dk"])
        dma(P, "pool", dq[:], c_dq[:, :], writes=["dq"])
        for i in range(NH):
            P.add("pool", (lambda i=i: lambda e: e.memset(Vaug[i][:, :, 128:129], 1.0))(), writes=[("Vone", i)])
        for i in range(2):
            P.add("pool", (lambda i=i: lambda e: e.memset(qtm[i][:, 192:193], -1.0))(), writes=[("qtm1", i)])

        import os
        HL = int(os.environ.get("B_H", "8"))
        QL = int(os.environ.get("B_Q", "4"))
        cnt = {"si": 0, "pti": 0, "oi": 0, "qi": 0}

        def proj_steps(h):
            hb = h % NH
            steps = []

            def w_step():
                dma(P, "sp", wq_f[:], w_uqv[:, :, h * 192:(h + 1) * 192], writes=["wqf"])
                dma(P, "sp", wk_f[:], w_ukv[:, :, h * 128:(h + 1) * 128], writes=["wkf"])
                dma(P, "sp", wv_f[:], w_uvv[:, :, h * 128:(h + 1) * 128], writes=["wvf"])

                def f_wq(e):
                    ins = None
                    for c in range(6):
                        ins = e.tensor_scalar(out=wq_b[hb][:, c, :], in0=wq_f[:, c, :], scalar1=qg_sb[:, c:c + 1], scalar2=None,
                                              op0=ALU.mult)
                    return ins
                P.add("pool", f_wq, reads=["wqf", "qg"], writes=[("wqb", hb)])

                def f_wkv(e):
                    ins = None
                    for c in range(4):
                        e.tensor_scalar(out=wk_b[hb][:, c, :], in0=wk_f[:, c, :], scalar1=kvg_sb[:, c:c + 1], scalar2=None, op0=ALU.mult)
                        ins = e.tensor_scalar(out=wv_b[hb][:, c, :], in0=wv_f[:, c, :], scalar1=kvg_sb[:, c:c + 1], scalar2=None,
                                              op0=ALU.mult)
                    return ins
                P.add("pool", f_wkv, reads=["wkf", "wvf", "kvg"], writes=[("wkvb", hb)])
            steps.append(w_step)

            def q_step(j):
                def f():
                    r = cnt["qi"] % 2
                    cnt["qi"] += 1
                    mm_group(P, pP[:, 0:192], [(cqnT[:, c, j * 128:(j + 1) * 128], wq_b[hb][:, c, :]) for c in range(6)],
                             reads=[("cqnT", j), ("wqb", hb)], writes=["pP"])
                    P.add("act", lambda e: e.activation(out=qtm[r][:, 0:128], in_=pP[:, 0:128], func=AF.Copy),
                          reads=["pP"], writes=[("qtmA", r)])
                    tbq = tb_own(j)
                    rope_ops(P, "dve", qtm[r][:, 128:192], pP[:, 128:192], cosM[:, tbq, :], sinM[:, tbq, :], 32, rtmp[r], ("rtmp", r),
                             reads=["pP", "cosM", "sinM"], writes=[("qtmB", r)])
                    transposes(P, [(pT[:, 0:128], qtm[r][:, 0:128]), (pT[0:65, 128:256], qtm[r][:, 128:193])], ident[:],
                               reads=[("qtmA", r), ("qtmB", r), ("qtm1", r), "ident"], writes=["pT"])
                    P.add("dve", lambda e: e.tensor_copy(out=qnT[hb][:, j * 128:(j + 1) * 128], in_=pT[:, 0:128]),
                          reads=["pT"], writes=[("qnT", hb, j)])
                    P.add("act", lambda e: e.activation(out=qrT[hb][0:65, j * 128:(j + 1) * 128], in_=pT[0:65, 128:256], func=AF.Copy),
                          reads=["pT"], writes=[("qrT", hb, j)])
                return f

            def k_step(kg):
                def f():
                    mm_group(P, pP[:, 0:512], [(wk_b[hb][:, c, :], ckvnT[:, c, kg * 512:(kg + 1) * 512]) for c in range(4)],
                             reads=[("ckvnT", kg * 4 + i) for i in range(4)] + [("wkvb", hb)], writes=["pP"])
                    if kg % 2 == 0:
                        P.add("dve", lambda e: e.tensor_copy(out=knT[hb][:, kg * 512:(kg + 1) * 512], in_=pP[:, 0:512]),
                              reads=["pP"], writes=[("knT", hb, kg * 4 + i) for i in range(4)])
                    else:
                        P.add("act", lambda e: e.activation(out=knT[hb][:, kg * 512:(kg + 1) * 512], in_=pP[:, 0:512], func=AF.Copy),
                              reads=["pP"], writes=[("knT", hb, kg * 4 + i) for i in range(4)])
                return f

            def v_step(vg):
                def f():
                    def f_v(e):
                        ins = None
                        for i in range(4):
                            tb = vg * 4 + i
                            for c in range(4):
                                ins = e.matmul(pP[:, i * 128:(i + 1) * 128], lhsT=ckvnT[:, c, tb * 128:(tb + 1) * 128],
                                               rhs=wv_b[hb][:, c, :], start=(c == 0), stop=(c == 3))
                        return ins
                    P.add("pe", f_v, reads=[("ckvnT", vg * 4 + i) for i in range(4)] + [("wkvb", hb)], writes=["pP"])
                    vout = Vaug[hb][:, vg * 4:(vg + 1) * 4, 0:128]
                    vin = pP[:, 0:512].rearrange("p (a b) -> p a b", a=4)
                    if vg % 2 == 1:
                        P.add("dve", lambda e: e.tensor_copy(out=vout, in_=vin), reads=["pP"],
                              writes=[("V", hb, vg * 4 + i) for i in range(4)])
                    else:
                        P.add("act", lambda e: e.activation(out=vout, in_=vin, func=AF.Copy), reads=["pP"],
                              writes=[("V", hb, vg * 4 + i) for i in range(4)])
                return f
            for j in range(16):
                steps.append(q_step(j))
            for kg in range(8):
                steps.append(k_step(kg))
            for vg in range(8):
                steps.append(v_step(vg))
            return steps

        def attn_tiles(h):
            tiles = []
            for Q in range(QL):
                for i in range(4 * Q + 4):
                    for typ in (0, 1):
                        tb = tb_own(i) if typ == 0 else tb_oth(i)
                        if i < 4 * Q:
                            qoff, nq = 0, 4
                        else:
                            qoff, nq = i - 4 * Q, 4 * Q + 4 - i
                        tiles.append(dict(Q=Q, i=i, typ=typ, tb=tb, qoff=qoff, nq=nq, N=nq * 128, c0=(4 * Q + qoff) * 128,
                                          diag=(typ == 0 and i >= 4 * Q), lastQ=(i == 4 * Q + 3 and typ == 1)))
            return tiles

        def emit_S(h, T):
            hb = h % NH
            sb_ = cnt["si"] % 2
            cnt["si"] += 1
            pr = cnt["pti"] % 3
            cnt["pti"] += 1
            T["pr"] = pr
            tb, N, c0, diag = T["tb"], T["N"], T["c0"], T["diag"]

            def f_s(e):
                e.matmul(pS[sb_][:, 0:N], lhsT=knT[hb][:, tb * 128:(tb + 1) * 128], rhs=qnT[hb][:, c0:c0 + N],
                         start=True, stop=False)
                ins = e.matmul(pS[sb_][:, 0:N], lhsT=kraugT[0:65, tb * 128:(tb + 1) * 128],
                               rhs=qrT[hb][0:65, c0:c0 + N], start=False, stop=(not diag))
                if diag:
                    ins = e.matmul(pS[sb_][:, 0:N], lhsT=dk[0:1, :], rhs=dq[0:1, 0:N], start=False, stop=True)
                return ins
            qblocks = list(range(4 * T["Q"] + T["qoff"], 4 * T["Q"] + 4))
            P.add("pe", f_s, reads=[("knT", hb, tb), ("kraugT", tb), "dk", "dq"] +
                  [("qnT", hb, jj) for jj in qblocks] + [("qrT", hb, jj) for jj in qblocks],
                  writes=[("pS", sb_)])
            P.add("act", lambda e: e.activation(out=PT[pr][:, 0:N], in_=pS[sb_][:, 0:N], func=AF.Exp, scale=SCALE),
                  reads=[("pS", sb_)], writes=[("PT", pr)])

        def emit_PV(h, T):
            hb = h % NH
            pr, tb, nq, qoff, i, typ, Q = T["pr"], T["tb"], T["nq"], T["qoff"], T["i"], T["typ"], T["Q"]

            def f_pv(e):
                ins = None
                for a in range(nq):
                    jj = qoff + a
                    J = 4 * Q + jj
                    first = (i == 0 and typ == 0)
                    last = (i == J and typ == 1)
                    ins = e.matmul(pO[jj][:, 0:129], lhsT=PT[pr][:, a * 128:(a + 1) * 128], rhs=Vaug[hb][:, tb, :],
                                   start=first, stop=last)
                return ins
            P.add("pe", f_pv, reads=[("PT", pr), ("V", hb, tb), ("Vone", hb)], writes=[("pO", qoff + a) for a in range(nq)])

        def emit_fin(h, Q):
            for jj in range(4):
                J = 4 * Q + jj
                r = cnt["oi"] % 2
                cnt["oi"] += 1
                P.add("dve", (lambda jj=jj, r=r: lambda e: e.reciprocal(out=rinv[r][:], in_=pO[jj][:, 128:129]))(),
                      reads=[("pO", jj)], writes=[("rinv", r)])
                P.add("dve", (lambda jj=jj, r=r: lambda e: e.tensor_scalar(out=otm[r][:], in0=pO[jj][:, 0:128], scalar1=rinv[r][:],
                                                                         scalar2=None, op0=ALU.mult))(),
                      reads=[("pO", jj), ("rinv", r)], writes=[("otm", r)])
                transposes(P, [(pT[:, 256:384], otm[r][:])], ident[:], reads=[("otm", r), "ident"], writes=["pT"])
                P.add("act", (lambda J=J: lambda e: e.activation(out=mixM[:, h, J * 128:(J + 1) * 128], in_=pT[:, 256:384],
                                                               func=AF.Copy))(),
                      reads=["pT"], writes=[("mixinT", 8 + h, J)])

        if PIPE_B == 2:
            for st_ in proj_steps(0):
                st_()
        for h in range(HL):
            if PIPE_B == 1:
                for st_ in proj_steps(h):
                    st_()
                prev = None
                for T in attn_tiles(h):
                    emit_S(h, T)
                    if prev is not None:
                        emit_PV(h, prev)
                        if prev["lastQ"]:
                            emit_fin(h, prev["Q"])
                    prev = T
                emit_PV(h, prev)
                emit_fin(h, prev["Q"])
                continue
            if not PIPE_B:
                for st_ in proj_steps(h):
                    st_()
                for T in attn_tiles(h):
                    emit_S(h, T)
                    emit_PV(h, T)
                    if T["lastQ"]:
                        emit_fin(h, T["Q"])
                continue
            nxt = proj_steps(h + 1) if h + 1 < HL else []
            prev = None
            for t, T in enumerate(attn_tiles(h)):
                emit_S(h, T)
                if prev is not None:
                    emit_PV(h, prev)
                    if prev["lastQ"]:
                        emit_fin(h, prev["Q"])
                prev = T
                if nxt and t % 2 == 1:
                    nxt.pop(0)()
            emit_PV(h, prev)
            emit_fin(h, prev["Q"])
            while nxt:
                nxt.pop(0)()
        P.emit()


def phase_A1(C, nc, sbt, pst, ident, xTv, w_inv, c_inv64, posf, c_decA, c_decB, c_epsq, c_d2t, gng, mixR, rope_tables):
    with contextlib.ExitStack() as es:
        P = Prog(nc)
        cosR = sbt(es, "a1_cosR", [128, 32, 64], F32)
        sinR = sbt(es, "a1_sinR", [128, 32, 64], F32)
        with contextlib.ExitStack() as es0:
            P0 = Prog(nc)
            rope_tables(P0, es0, c_inv64, 64, cosR, sinR, "R")
            P0.emit()
        xTb = sbt(es, "a1_xT", [128, 16, 2048], BF16)
        Wg = [sbt(es, "a1_w%d" % i, [128, 16, 512], BF16) for i in range(2)]
        decA = sbt(es, "a1_decA", [128, 8], F32)
        decB = sbt(es, "a1_decB", [128, 8], F32)
        epsq = sbt(es, "a1_epsq", [128, 8], F32)
        d2t = sbt(es, "a1_d2t", [128, 8, 128], F32)
        gng_sb = sbt(es, "a1_gng", [128, 1024], F32)
        U32 = sbt(es, "a1_U", [128, 8, 128], F32)
        NR = 3
        T32 = [sbt(es, "a1_T%d" % i, [128, 128], F32) for i in range(NR)]
        Tb = [sbt(es, "a1_Tb%d" % i, [128, 128], BF16) for i in range(NR)]
        k_tm = [sbt(es, "a1_k%d" % i, [128, 128], BF16) for i in range(NR)]
        kd_tm = [sbt(es, "a1_kd%d" % i, [128, 128], BF16) for i in range(NR)]
        v_tm = [sbt(es, "a1_v%d" % i, [128, 128], BF16) for i in range(NR)]
        ko_tm = [sbt(es, "a1_ko%d" % i, [128, 128], BF16) for i in range(NR)]
        vo_tm = [sbt(es, "a1_vo%d" % i, [128, 128], BF16) for i in range(NR)]
        q_tm = [sbt(es, "a1_q%d" % i, [128, 128], BF16) for i in range(NR)]
        g_tm = [sbt(es, "a1_g%d" % i, [128, 128], F32) for i in range(NR)]
        qT = [sbt(es, "a1_qT%d" % i, [128, 128], BF16) for i in range(NR)]
        kT = [sbt(es, "a1_kT%d" % i, [128, 128], BF16) for i in range(NR)]
        Pm = [sbt(es, "a1_P%d" % i, [128, 128], BF16) for i in range(NR)]
        NT_ = 4
        rtmp = [sbt(es, "a1_rtmp%d" % i, [128, 384], F32) for i in range(NT_)]
        st = [sbt(es, "a1_st%d" % i, [128, 8], F32) for i in range(NR)]
        yn = [sbt(es, "a1_yn%d" % i, [128, 128], F32) for i in range(NR)]
        ret_tm = [sbt(es, "a1_ret%d" % i, [128, 128], BF16) for i in range(NR)]
        junk = sbt(es, "a1_junk", [128, 128], BF16)
        pA = [pst(es, "a1_pA%d" % i, [128, 512], F32) for i in range(3)]
        pS = pst(es, "a1_pS", [128, 512], F32)
        pO = pst(es, "a1_pO", [128, 512], F32)
        pKV = pst(es, "a1_pKV", [128, 512], F32)
        pT = pst(es, "a1_pT", [128, 1024], BF16)
        pT2 = pst(es, "a1_pT2", [128, 1024], BF16)

        dma(P, "sp", decA[:], c_decA[:, :], writes=["decA"])
        dma(P, "sp", decB[:], c_decB[:, :], writes=["decB"])
        dma(P, "sp", epsq[:], c_epsq[:, :], writes=["epsq"])
        dma(P, "sp", d2t[:].rearrange("p a b -> p (a b)"), c_d2t[:, :], writes=["d2t"])
        dma(P, "sp", gng_sb[:], gng[0:1, :].broadcast_to([128, 1024]), writes=["gng"])
        P.add("pool", lambda e: e.memset(U32[:].rearrange("p a b -> p (a b)"), 0.0), writes=[("U", h) for h in range(8)])
        g256 = [float(np.exp(256.0 * np.log1p(-2.0 ** (-5.0 - h)))) for h in range(8)]
        xk = [("xT", c) for c in range(16)]
        cnt = {"pai": 0, "rti": 0}

        def load_W(step):
            hh = step % 8
            for c in range(16):
                dma(P, "pool", Wg[step % 2][:, c, :], w_inv[:, c, hh * 512:(hh + 1) * 512], writes=[("W", step % 2, c)])

        iters = [(ps_, h, k) for ps_ in range(2) for h in range(8) for k in range(8)]
        ST = {}

        def S0(n):
            ps_, h, k = iters[n]
            step = ps_ * 8 + h
            wb = step % 2
            if k == 0:
                if h == 0:
                    for c in range(16):
                        dma(P, "pool", xTb[:, c, :], xTv[:, c, ps_ * 2048:(ps_ + 1) * 2048], writes=[("xT", c)])
                if step == 0:
                    load_W(0)
                if step + 1 < 16:
                    load_W(step + 1)
            wk = [("W", wb, c) for c in range(16)]
            ko, kw = 8 + k, k
            a0 = cnt["pai"] % 3; cnt["pai"] += 1
            a1 = cnt["pai"] % 3; cnt["pai"] += 1
            ST[n] = (a0, a1)
            mm_group(P, pA[a0][:, 0:256], [(xTb[:, c, ko * 128:(ko + 1) * 128], Wg[wb][:, c, 0:256]) for c in range(16)],
                     reads=xk + wk, writes=[("pA", a0)])
            mm_group(P, pA[a1][:, 0:512], [(xTb[:, c, kw * 128:(kw + 1) * 128], Wg[wb][:, c, :]) for c in range(16)],
                     reads=xk + wk, writes=[("pA", a1)])

        def S1(n):
            ps_, h, k = iters[n]
            j = ps_ * 8 + k
            r = n % NR
            a0, a1 = ST[n]
            tbo, tbw = tb_oth(j), tb_own(j)
            pa = pA[a0]; pak = ("pA", a0)
            ri = cnt["rti"] % NT_; cnt["rti"] += 1
            rt = rtmp[ri]; rtk = ("rtmp", ri)
            rope_ops(P, "dve", rt[:, 0:128], pa[:, 0:128], cosR[:, tbo, :], sinR[:, tbo, :], 64, rt[:, 128:384], (rtk, "t"),
                     reads=[pak, "cosR", "sinR"], writes=[rtk], eng2="pool")
            P.add("pool", lambda e: e.tensor_scalar(out=ko_tm[r][:], in0=rt[:, 0:128], scalar1=decB[:, h:h + 1], scalar2=None,
                                                   op0=ALU.mult), reads=[rtk, "decB"], writes=[("ko", r)])
            P.add("act", lambda e: e.activation(out=vo_tm[r][:], in_=pa[:, 128:256], func=AF.Copy), reads=[pak], writes=[("vo", r)])
            pb = pA[a1]; pbk = ("pA", a1)
            ri = cnt["rti"] % NT_; cnt["rti"] += 1
            rt2 = rtmp[ri]; rtk2 = ("rtmp", ri)
            rope_ops(P, "dve", k_tm[r][:], pb[:, 0:128], cosR[:, tbw, :], sinR[:, tbw, :], 64, rt2[:], rtk2,
                     reads=[pbk, "cosR", "sinR"], writes=[("k", r)], eng2="pool")
            ri = cnt["rti"] % NT_; cnt["rti"] += 1
            rt3 = rtmp[ri]; rtk3 = ("rtmp", ri)
            rope_ops(P, "dve", q_tm[r][:], pb[:, 256:384], cosR[:, tbw, :], sinR[:, tbw, :], 64, rt3[:], rtk3,
                     reads=[pbk, "cosR", "sinR"], writes=[("q", r)], eng2="pool")
            P.add("act", lambda e: e.activation(out=v_tm[r][:], in_=pb[:, 128:256], func=AF.Copy), reads=[pbk], writes=[("v", r)])
            P.add("act", lambda e: e.activation(out=g_tm[r][:], in_=pb[:, 384:512], func=AF.Silu), reads=[pbk], writes=[("g", r)])
            P.add("pool", lambda e: e.tensor_scalar(out=kd_tm[r][:], in0=k_tm[r][:], scalar1=decA[:, h:h + 1], scalar2=None,
                                                   op0=ALU.mult), reads=[("k", r), "decA"], writes=[("kd", r)])

        def S2(n):
            ps_, h, k = iters[n]
            r = n % NR
            mm_group(P, pKV[:, 0:128], [(ko_tm[r][:], vo_tm[r][:])], reads=[("ko", r), ("vo", r)], writes=["pKV"])
            P.add("dve", lambda e: e.tensor_tensor(out=T32[r][:], in0=U32[:, h, :], in1=pKV[:, 0:128], op=ALU.add),
                  reads=["pKV", ("U", h)], writes=[("T32", r)])
            P.add("pool", lambda e: e.tensor_copy(out=Tb[r][:], in_=T32[r][:]), reads=[("T32", r)], writes=[("Tb", r)])
            mm_group(P, pKV[:, 128:256], [(kd_tm[r][:], v_tm[r][:])], reads=[("kd", r), ("v", r)], writes=["pKV"])
            P.add("dve", lambda e: e.scalar_tensor_tensor(out=U32[:, h, :], in0=T32[r][:], scalar=g256[h], in1=pKV[:, 128:256],
                                                         op0=ALU.mult, op1=ALU.add),
                  reads=["pKV", ("T32", r)], writes=[("U", h)])
            transposes(P, [(pT[:, 0:128], q_tm[r][:]), (pT[:, 128:256], k_tm[r][:])], ident[:],
                       reads=[("q", r), ("k", r), "ident"], writes=["pTqk"])
            P.add("act", lambda e: e.activation(out=qT[r][:], in_=pT[:, 0:128], func=AF.Copy), reads=["pTqk"], writes=[("qT", r)])
            P.add("dve", lambda e: e.tensor_copy(out=kT[r][:], in_=pT[:, 128:256]), reads=["pTqk"], writes=[("kT", r)])
            mm_group(P, pS[:, 0:128], [(kT[r][:], qT[r][:])], reads=[("kT", r), ("qT", r)], writes=["pS"])
            P.add("dve", lambda e: e.tensor_tensor(out=Pm[r][:], in0=pS[:, 0:128], in1=d2t[:, h, :], op=ALU.mult),
                  reads=["pS", "d2t"], writes=[("Pm", r)])

        def S3(n):
            ps_, h, k = iters[n]
            j = ps_ * 8 + k
            r = n % NR
            mm_group(P, pO[:, 0:128], [(Pm[r][:], v_tm[r][:]), (qT[r][:], Tb[r][:])],
                     reads=[("Pm", r), ("v", r), ("qT", r), ("Tb", r)], writes=["pO"])
            s = st[r]
            P.add("act", lambda e: e.activation(out=junk[:], in_=pO[:, 0:128], func=AF.Copy, accum_out=s[:, 0:1]),
                  reads=["pO"], writes=[("st0", r), "junk"])
            P.add("act", lambda e: e.activation(out=junk[:], in_=pO[:, 0:128], func=AF.Square, accum_out=s[:, 1:2]),
                  reads=["pO"], writes=[("st1", r), "junk"])
            P.add("pool", lambda e: e.tensor_scalar(out=s[:, 2:3], in0=s[:, 0:1], scalar1=1.0 / 128, scalar2=None, op0=ALU.mult),
                  reads=[("st0", r)], writes=[("stm", r)])
            P.add("pool", lambda e: e.tensor_tensor(out=s[:, 3:4], in0=s[:, 2:3], in1=s[:, 2:3], op=ALU.mult),
                  reads=[("stm", r)], writes=[("stm2", r)])
            P.add("dve", lambda e: e.scalar_tensor_tensor(out=s[:, 4:5], in0=s[:, 1:2], scalar=1.0 / 128, in1=s[:, 3:4],
                                                         op0=ALU.mult, op1=ALU.subtract),
                  reads=[("st1", r), ("stm2", r)], writes=[("stv", r)])
            P.add("pool", lambda e: e.tensor_tensor(out=s[:, 7:8], in0=s[:, 4:5], in1=epsq[:, h:h + 1], op=ALU.add),
                  reads=[("stv", r), "epsq"], writes=[("st2", r)])
            P.add("act", lambda e: e.activation(out=s[:, 5:6], in_=s[:, 7:8], func=AF.Sqrt), reads=[("st2", r)], writes=[("st3", r)])
            P.add("dve", lambda e: e.reciprocal(out=s[:, 6:7], in_=s[:, 5:6]), reads=[("st3", r)], writes=[("st4", r)])
            P.add("dve", lambda e: e.tensor_scalar(out=yn[r][:], in0=pO[:, 0:128], scalar1=s[:, 2:3], scalar2=s[:, 6:7],
                                                  op0=ALU.subtract, op1=ALU.mult),
                  reads=["pO", ("stm", r), ("st4", r)], writes=[("yn", r)])
            P.add("pool", lambda e: e.tensor_tensor(out=yn[r][:], in0=yn[r][:], in1=gng_sb[:, h * 128:(h + 1) * 128], op=ALU.mult),
                  reads=[("yn", r), "gng"], writes=[("yn", r)])
            P.add("pool", lambda e: e.tensor_tensor(out=ret_tm[r][:], in0=yn[r][:], in1=g_tm[r][:], op=ALU.mult),
                  reads=[("yn", r), ("g", r)], writes=[("ret", r)])
            transposes(P, [(pT2[:, 0:128], ret_tm[r][:])], ident[:], reads=[("ret", r), "ident"], writes=["pTr"])
            P.add("act", lambda e: e.activation(out=mixR[:, h, j * 128:(j + 1) * 128], in_=pT2[:, 0:128], func=AF.Copy),
                  reads=["pTr"], writes=[("mixinT", h, j)])

        stages = [S0, S1, S2, S3]
        n_it = len(iters)
        for t in range(n_it + len(stages) - 1):
            for si_ in reversed(range(len(stages))):
                n = t - si_
                if 0 <= n < n_it:
                    stages[si_](n)
        P.emit()


def layer_norm_ops(P, y, ykey, st, stkey, junk, g_sb, b_sb, gkeys, outs):
    P.add("act", lambda e: e.activation(out=junk[:], in_=y, func=AF.Copy, accum_out=st[:, 0:1]),
          reads=[ykey], writes=[(stkey, 0), "junk"])
    P.add("dve", lambda e: e.tensor_scalar(out=st[:, 1:2], in0=st[:, 0:1], scalar1=-1.0 / D, scalar2=None, op0=ALU.mult),
          reads=[(stkey, 0)], writes=[(stkey, 1)])
    P.add("dve", lambda e: e.tensor_scalar(out=y, in0=y, scalar1=st[:, 1:2], scalar2=None, op0=ALU.add),
          reads=[ykey, (stkey, 1)], writes=[ykey])
    P.add("act", lambda e: e.activation(out=junk[:], in_=y, func=AF.Square, accum_out=st[:, 2:3]),
          reads=[ykey], writes=[(stkey, 2), "junk"])
    rstd_ops(P, st[:, 4:5], st[:, 2:3], 1.0 / D, EPS, st[:, 3:4], reads=[(stkey, 2)], writes=[(stkey, 4)], key=stkey)
    P.add("dve", lambda e: e.scalar_tensor_tensor(out=y, in0=y, scalar=st[:, 4:5], in1=g_sb[:], op0=ALU.mult, op1=ALU.mult),
          reads=[ykey, (stkey, 4)] + gkeys, writes=[ykey])
    P.add("dve", lambda e: e.tensor_tensor(out=y, in0=y, in1=b_sb[:], op=ALU.add), reads=[ykey] + gkeys, writes=[ykey])


def phase_C(C, nc, sbt, pst, ident, w_outv, x_own, ln1g, ln1b, mixR, mixM, y1s, x1Tsv):
    with contextlib.ExitStack() as es:
        P = Prog(nc)
        Wo = sbt(es, "c_wo", [128, 16, D], BF16)
        g_sb = sbt(es, "c_g", [128, D], F32)
        b_sb = sbt(es, "c_b", [128, D], F32)
        NR = 2
        xo = [sbt(es, "c_xo%d" % i, [128, D], F32) for i in range(NR)]
        y = [sbt(es, "c_y%d" % i, [128, D], F32) for i in range(NR)]
        x1b = [sbt(es, "c_x1b%d" % i, [128, D], BF16) for i in range(NR)]
        x1T = [sbt(es, "c_x1T%d" % i, [128, 16, 128], BF16) for i in range(NR)]
        st = [sbt(es, "c_st%d" % i, [128, 8], F32) for i in range(NR)]
        junk = sbt(es, "c_junk", [128, D], BF16)
        pC = [pst(es, "c_pC%d" % i, [128, 512], F32) for i in range(4)]
        pT = [pst(es, "c_pT%d" % i, [128, 1024], BF16) for i in range(2)]
        for c in range(16):
            dma(P, "pool", Wo[:, c, :], w_outv[:, c, :], writes=[("Wo", c)])
        dma(P, "sp", g_sb[:], ln1g[0:1, :].broadcast_to([128, D]), writes=["lng"])
        dma(P, "sp", b_sb[:], ln1b[0:1, :].broadcast_to([128, D]), writes=["lnb"])
        wk = [("Wo", c) for c in range(16)]

        def S0(j):
            r = j % NR
            dma(P, "sp", xo[r][:], x_own[j * 128:(j + 1) * 128, :], writes=[("xo", r)])
            for cg in range(4):
                mm_group(P, pC[cg][:, 0:512],
                         [((mixR if c < 8 else mixM)[:, c % 8, j * 128:(j + 1) * 128], Wo[:, c, cg * 512:(cg + 1) * 512])
                          for c in range(16)],
                         reads=[("mixinT", c, j) for c in range(16)] + wk, writes=[("pC", cg)])
                P.add("dve", (lambda cg=cg: lambda e: e.scalar_tensor_tensor(
                    out=y[r][:, cg * 512:(cg + 1) * 512], in0=xo[r][:, cg * 512:(cg + 1) * 512], scalar=ALPHA,
                    in1=pC[cg][:, 0:512], op0=ALU.mult, op1=ALU.add))(),
                    reads=[("xo", r), ("pC", cg)], writes=[("y", r)])

        def S1(j):
            r = j % NR
            layer_norm_ops(P, y[r][:], ("y", r), st[r], ("c_st", r), junk, g_sb, b_sb, ["lng", "lnb"], None)
            P.add("act", lambda e: e.activation(out=x1b[r][:], in_=y[r][:], func=AF.Copy), reads=[("y", r)], writes=[("x1b", r)])
            P.add("act", lambda e: e.activation(out=y[r][:], in_=y[r][:], func=AF.Copy, scale=ALPHA),
                  reads=[("y", r), ("x1b", r)], writes=[("y", r)])
            dma(P, "sp", y1s[j * 128:(j + 1) * 128, :], y[r][:], reads=[("y", r)])

        def S2(j):
            r = j % NR
            for half in range(2):
                transposes(P, [(pT[half][:, c * 128:(c + 1) * 128], x1b[r][:, (half * 8 + c) * 128:(half * 8 + c + 1) * 128])
                               for c in range(8)], ident[:], reads=[("x1b", r), "ident"], writes=[("pT", half)])
                xout = x1T[r][:, half * 8:(half + 1) * 8, :]
                xin = pT[half][:, 0:1024].rearrange("p (c t) -> p c t", c=8)
                if half == 0:
                    P.add("act", (lambda xout=xout, xin=xin: lambda e: e.activation(out=xout, in_=xin, func=AF.Copy))(),
                          reads=[("pT", half)], writes=[("x1T", r, half)])
                else:
                    P.add("dve", (lambda xout=xout, xin=xin: lambda e: e.tensor_copy(out=xout, in_=xin))(),
                          reads=[("pT", half)], writes=[("x1T", r, half)])
            dma(P, "sp", x1Tsv[:, :, j * 128:(j + 1) * 128], x1T[r][:], reads=[("x1T", r, 0), ("x1T", r, 1)])

        stages = [S0, S1, S2]
        if not PIPE_C:
            for j in range(16):
                for st_ in stages:
                    st_(j)
        else:
            for t in range(16 + len(stages) - 1):
                for si_ in reversed(range(len(stages))):
                    n = t - si_
                    if 0 <= n < 16:
                        stages[si_](n)
        P.emit()


def phase_D(C, nc, sbt, pst, w_upv, w_downv, ln2g, ln2b, y1s, x1Tsv, out):
    with contextlib.ExitStack() as es:
        P = Prog(nc)
        x1Tg = sbt(es, "d_x1T", [128, 16, 1024], BF16)
        acc = sbt(es, "d_acc", [128, 8, D], F32)
        Wup = [sbt(es, "d_wup%d" % i, [128, 16, 512], BF16) for i in range(2)]
        Wdn = [sbt(es, "d_wdn%d" % i, [128, 4, D], BF16) for i in range(2)]
        hT = [sbt(es, "d_hT%d" % i, [128, 4, 1024], BF16) for i in range(2)]
        rt = [sbt(es, "d_rt%d" % i, [128, 512], F32) for i in range(2)]
        g_sb = sbt(es, "d_g", [128, D], F32)
        b_sb = sbt(es, "d_b", [128, D], F32)
        st = [sbt(es, "d_st%d" % i, [128, 8], F32) for i in range(2)]
        junk = sbt(es, "d_junk", [128, D], BF16)
        pU = [pst(es, "d_pU%d" % i, [128, 512], F32) for i in range(3)]
        pD = [pst(es, "d_pD%d" % i, [128, 512], F32) for i in range(4)]
        dma(P, "sp", g_sb[:], ln2g[0:1, :].broadcast_to([128, D]), writes=["lng"])
        dma(P, "sp", b_sb[:], ln2b[0:1, :].broadcast_to([128, D]), writes=["lnb"])
        wi = 0
        ui = 0
        di = 0

        def load_FW(step):
            f = step % 16
            b = step % 2
            for c in range(16):
                dma(P, "pool", Wup[b][:, c, :], w_upv[:, c, f * 512:(f + 1) * 512], writes=[("Wup", b, c)])
            for c in range(4):
                dma(P, "pool", Wdn[b][:, c, :], w_downv[:, f * 4 + c, :], writes=[("Wdn", b, c)])
        for grp in range(2):
            for c in range(16):
                dma(P, "sp", x1Tg[:, c, :], x1Tsv[:, c, grp * 1024:(grp + 1) * 1024], writes=[("x1T", c)])
            for tb in range(8):
                row0 = (grp * 8 + tb) * 128
                dma(P, "sp", acc[:, tb, :], y1s[row0:row0 + 128, :], writes=[("acc", tb)])
            xk = [("x1T", c) for c in range(16)]
            for fc in range(16):
                wb = wi % 2
                if wi == 0:
                    load_FW(0)
                if wi + 1 < 32:
                    load_FW(wi + 1)
                wi += 1
                wuk = [("Wup", wb, c) for c in range(16)]
                for fs in range(4):
                    for tg in range(2):
                        pu = pU[ui % 3]; puk = ("pU", ui % 3)
                        rr = ui % 2
                        ui += 1
                        mm_group(P, pu[:, 0:512],
                                 [(Wup[wb][:, c, fs * 128:(fs + 1) * 128], x1Tg[:, c, tg * 512:(tg + 1) * 512]) for c in range(16)],
                                 reads=xk + wuk, writes=[puk])
                        P.add("act", (lambda pu=pu, rr=rr: lambda e: e.activation(out=rt[rr][:], in_=pu[:, 0:512], func=AF.Relu))(),
                              reads=[puk], writes=[("rt", rr)])
                        P.add("act", (lambda rr=rr, wb=wb, fs=fs, tg=tg: lambda e: e.activation(
                            out=hT[wb][:, fs, tg * 512:(tg + 1) * 512], in_=rt[rr][:], func=AF.Square))(),
                            reads=[("rt", rr)], writes=[("hT", wb, fs, tg)])
                for tb in range(8):
                    tg = tb // 4
                    for cg in range(4):
                        pd = pD[di % 4]; pdk = ("pD", di % 4)
                        di += 1
                        mm_group(P, pd[:, 0:512],
                                 [(hT[wb][:, fs, tb * 128:(tb + 1) * 128], Wdn[wb][:, fs, cg * 512:(cg + 1) * 512]) for fs in range(4)],
                                 reads=[("hT", wb, fs, tg) for fs in range(4)] + [("Wdn", wb, fs) for fs in range(4)], writes=[pdk])
                        P.add("dve", (lambda pd=pd, tb=tb, cg=cg: lambda e: e.tensor_tensor(
                            out=acc[:, tb, cg * 512:(cg + 1) * 512], in0=acc[:, tb, cg * 512:(cg + 1) * 512], in1=pd[:, 0:512],
                            op=ALU.add))(),
                            reads=[pdk, ("acc", tb)], writes=[("acc", tb)])
            for tb in range(8):
                row0 = (grp * 8 + tb) * 128
                layer_norm_ops(P, acc[:, tb, :], ("acc", tb), st[tb % 2], ("d_st", tb % 2), junk, g_sb, b_sb, ["lng", "lnb"], None)
                dma(P, "sp", out[row0:row0 + 128, :], acc[:, tb, :], reads=[("acc", tb)])
        P.emit()


def _consts(p):
    c = {}
    c["c_ident"] = np.eye(128, dtype=np.float32)
    inv64 = (np.float32(10000.0) ** (-np.arange(0, 128, 2, dtype=np.float32) / np.float32(128))).astype(np.float32)
    inv32 = (np.float32(10000.0) ** (-np.arange(0, 64, 2, dtype=np.float32) / np.float32(64))).astype(np.float32)
    c["c_inv64"] = np.ascontiguousarray(np.broadcast_to(inv64[None, :], (128, 64)))
    c["c_inv32"] = np.ascontiguousarray(np.broadcast_to(inv32[None, :], (128, 32)))
    t = np.arange(128, dtype=np.float64)
    decA = np.zeros((128, 8)); decB = np.zeros((128, 8)); epsq = np.zeros((128, 8)); d2t = np.zeros((128, 8, 128))
    sc = 128.0 ** -0.5
    ii = t[None, :]
    jj = t[:, None]
    ci = np.floor(ii / 64); cj = np.floor(jj / 64)
    for h in range(8):
        lg = np.log1p(-2.0 ** (-5.0 - h))
        decA[:, h] = np.exp(lg * (255 - t)) * sc
        decB[:, h] = np.exp(lg * (127 - t)) * sc
        qdec = np.exp(lg * (t + 1))
        epsq[:, h] = EPS / qdec ** 2
        dd = np.where(cj == ci, np.exp(lg * np.abs(ii - jj)), np.where(cj < ci, np.exp(lg * (ii - jj)), 0.0))
        d2t[:, h, :] = dd * sc / qdec[None, :]
    c["c_decA"] = decA.astype(np.float32)
    c["c_decB"] = decB.astype(np.float32)
    c["c_epsq"] = epsq.astype(np.float32)
    c["c_d2t"] = d2t.reshape(128, 1024).astype(np.float32)
    dk = np.zeros((1, 128), np.float32); dk[0, 64:] = BIG
    dq = np.zeros((1, 512), np.float32); dq[0, :64] = -1.0
    c["c_dk"] = dk
    c["c_dq"] = dq
    return c


def _col_perm():
    cols = []
    for h in range(8):
        cols += list(range(1024 + h * 128, 1024 + (h + 1) * 128))
        cols += list(range(2048 + h * 128, 2048 + (h + 1) * 128))
        cols += list(range(0 + h * 128, (h + 1) * 128))
        cols += list(range(3072 + h * 128, 3072 + (h + 1) * 128))
    cols += list(range(4096 + 768, 4096 + 768 + 512))
    cols += list(range(4096 + 768 + 512, 5440))
    cols += list(range(4096, 4096 + 768))
    return np.array(cols)


def prep_inputs(inputs):
    x = np.asarray(inputs["x"], np.float32)
    pos = np.asarray(inputs["positions"], np.int32)
    w_in = np.ascontiguousarray(np.asarray(inputs["w_in"], np.float32)[0][:, _col_perm()])
    shared = {
        "w_in": w_in,
        "w_uq": np.ascontiguousarray(inputs["w_uq"][0], np.float32),
        "w_uk": np.ascontiguousarray(inputs["w_uk"][0], np.float32),
        "w_uv": np.ascontiguousarray(inputs["w_uv"][0], np.float32),
        "w_out": np.ascontiguousarray(inputs["w_out"][0], np.float32),
        "w_up": np.ascontiguousarray(inputs["w_up"][0], np.float32),
        "w_down": np.ascontiguousarray(inputs["w_down"][0], np.float32),
        "qg": np.ascontiguousarray(np.asarray(inputs["q_norm_g"][0], np.float32).reshape(6, 128).T),
        "kvg": np.ascontiguousarray(np.asarray(inputs["kv_norm_g"][0], np.float32).reshape(4, 128).T),
        "gng": np.asarray(inputs["ret_gn_g"][0], np.float32).reshape(1, 1024),
        "ln1g": np.asarray(inputs["ln1_g"][0], np.float32).reshape(1, D),
        "ln1b": np.asarray(inputs["ln1_b"][0], np.float32).reshape(1, D),
        "ln2g": np.asarray(inputs["ln2_g"][0], np.float32).reshape(1, D),
        "ln2b": np.asarray(inputs["ln2_b"][0], np.float32).reshape(1, D),
    }
    in_maps = []
    metas = []
    for core in range(8):
        b, p = core // 2, core % 2
        own_g = [2 * j + p for j in range(16)]
        oth_g = [(2 * j - 1) if p == 0 else (2 * j) for j in range(16)]
        blocks = [None] * 32
        for j in range(16):
            blocks[tb_own(j)] = own_g[j]
            blocks[tb_oth(j)] = oth_g[j]
        xb = x[b]
        xTc = np.zeros((D, SEQ), np.float32)
        pos_tm = np.zeros((128, 32), np.int32)
        km = np.zeros((128, 32), np.float32)
        for tb, g in enumerate(blocks):
            if g < 0:
                km[:, tb] = BIG
                continue
            xTc[:, tb * 128:(tb + 1) * 128] = xb[g * 128:(g + 1) * 128, :].T
            pos_tm[:, tb] = pos[b, g * 128:(g + 1) * 128]
        x_own = np.concatenate([xb[g * 128:(g + 1) * 128, :] for g in own_g], axis=0)
        m = dict(shared)
        m.update(_consts(p))
        m["xT"] = xTc
        m["x_own"] = np.ascontiguousarray(x_own)
        m["pos_tm"] = pos_tm
        m["kmask"] = km
        in_maps.append(m)
        metas.append((b, own_g))
    return in_maps, metas


def kernel(**inputs):
    in_maps, metas = prep_inputs(inputs)
    nc = build()
    res = run_bass_kernel_spmd(nc, in_maps, core_ids=list(range(8)))
    outp = np.zeros((4, SEQ, D), np.float32)
    for core in range(8):
        b, own_g = metas[core]
        o = np.asarray(res.results[core]["out"], np.float32)
        for j, g in enumerate(own_g):
            outp[b, g * 128:(g + 1) * 128, :] = o[j * 128:(j + 1) * 128, :]
    return outp
```
